# Optimizing a Trainium2 kernel written in Bass

```python
import jax, jax.numpy as jnp
from jax import lax
import numpy as np

D_MODEL = 1024
BATCH = 8
SEQ = 2048
DEPTH = 2
DEC_BATCH = 128
DEC_SEQ = 8
PAST_LEN = 16384
PAGE_SIZE = 128

POOL_WINDOWS = (2, 4, 8, 16)
N_POOL_GROUPS = len(POOL_WINDOWS)
D_POOL = D_MODEL // 2
POOL_GROUP = D_POOL // N_POOL_GROUPS
POOL_BUF = max(POOL_WINDOWS) - 1
D_RNN = D_MODEL
N_RNN_BLOCKS = 8
RNN_BLOCK = D_RNN // N_RNN_BLOCKS
RNN_CONV = 4
RG_C = 8.0
N_HEADS = 4
HEAD_DIM = 128
D_Q = N_HEADS * HEAD_DIM
N_MEM = 256
D_MIX = D_POOL + D_RNN + D_Q
N_BRANCH = 3
D_FF = 3 * D_MODEL
FFN_CONV = 3
EPS = 1e-6

kernel_name = "hybrid_pool_rglru_memxattn_decoder_step"


def rmsnorm(x, g):
    xf = x.astype(jnp.float32)
    y = xf * lax.rsqrt(jnp.mean(xf * xf, axis=-1, keepdims=True) + EPS)
    return (y * g.astype(jnp.float32)).astype(x.dtype)


def causal_dwconv(u, buf, w, b):
    k = w.shape[0]
    t = u.shape[1]
    ext = jnp.concatenate([buf.astype(u.dtype), u], axis=1)
    y = b
    for j in range(k):
        y = y + ext[:, j:j + t] * w[j]
    return y, ext[:, -(k - 1):]


def pool_mixer(u, buf, pos0, w_grp, scale):
    B, T, _ = u.shape
    ext = jnp.concatenate([buf.astype(u.dtype), u], axis=1)
    cs = jnp.cumsum(ext.astype(jnp.float32), axis=1)
    cs = jnp.pad(cs, ((0, 0), (1, 0), (0, 0)))
    pos = pos0 + jnp.arange(T, dtype=jnp.int32)
    outs = []
    for g, w in enumerate(POOL_WINDOWS):
        sl = slice(g * POOL_GROUP, (g + 1) * POOL_GROUP)
        win = cs[:, POOL_BUF + 1:POOL_BUF + 1 + T, sl] - cs[:, POOL_BUF + 1 - w:POOL_BUF + 1 - w + T, sl]
        cnt = jnp.minimum(pos + 1, w).astype(jnp.float32)[None, :, None]
        outs.append(win / cnt)
    d = (jnp.concatenate(outs, axis=-1) - u.astype(jnp.float32)).reshape(B, T, N_POOL_GROUPS, POOL_GROUP)
    y = jnp.einsum('btgc,gcd->btgd', d, w_grp.astype(jnp.float32)).reshape(B, T, D_POOL)
    y = y * scale.astype(jnp.float32)
    return y.astype(u.dtype), ext[:, -POOL_BUF:]


def _lin_combine(c1, c2):
    a1, b1 = c1
    a2, b2 = c2
    return a1 * a2, a2 * b1 + b2


def rglru(u, conv_buf, h0, conv_w, conv_b, w_a, b_a, w_x, b_x, lam):
    xc, new_conv = causal_dwconv(u, conv_buf, conv_w, conv_b)
    B, T, _ = xc.shape
    xf = xc.astype(jnp.float32)
    xb = xf.reshape(B, T, N_RNN_BLOCKS, RNN_BLOCK)
    r = jax.nn.sigmoid(jnp.einsum('btnc,ncd->btnd', xb, w_a.astype(jnp.float32)).reshape(B, T, D_RNN) + b_a.astype(jnp.float32))
    i = jax.nn.sigmoid(jnp.einsum('btnc,ncd->btnd', xb, w_x.astype(jnp.float32)).reshape(B, T, D_RNN) + b_x.astype(jnp.float32))
    log_a = -RG_C * r * jax.nn.softplus(-lam.astype(jnp.float32))
    a = jnp.exp(log_a)
    beta = jnp.sqrt(-jnp.expm1(2.0 * log_a))
    bterm = beta * (i * xf)
    bterm = bterm.at[:, 0].add(a[:, 0] * h0.astype(jnp.float32))
    _, h = lax.associative_scan(_lin_combine, (a, bterm), axis=1)
    return h.astype(u.dtype), new_conv, h[:, -1].astype(h0.dtype)


def mem_kv(mem, g_mem, w_k, w_v):
    B = mem.shape[0]
    mn = rmsnorm(mem, g_mem)
    k = (mn @ w_k).reshape(B, N_MEM, N_HEADS, HEAD_DIM)
    v = (mn @ w_v).reshape(B, N_MEM, N_HEADS, HEAD_DIM)
    return k, v


def cross_attn(q, k, v):
    s = jnp.einsum('bthd,bmhd->bhtm', q.astype(jnp.float32), k.astype(jnp.float32)) * (HEAD_DIM ** -0.5)
    p = jax.nn.softmax(s, axis=-1)
    o = jnp.einsum('bhtm,bmhd->bthd', p, v.astype(jnp.float32))
    return o.astype(q.dtype)


def layer(h, m_k, m_v, pool_buf, rnn_buf, rnn_h0, ffn_buf, pos0, lw):
    B, T, _ = h.shape
    xn = rmsnorm(h, lw['g_mix'])
    z = xn @ lw['w_in']
    u_pool = z[..., :D_POOL]
    u_rnn = z[..., D_POOL:D_POOL + D_RNN]
    q = z[..., D_POOL + D_RNN:].reshape(B, T, N_HEADS, HEAD_DIM)
    gates = jax.nn.sigmoid((xn @ lw['w_gate'] + lw['b_gate']).astype(jnp.float32)).reshape(B, T, N_BRANCH, D_MODEL)
    y_pool, new_pool = pool_mixer(u_pool, pool_buf, pos0, lw['pool_w'], lw['pool_scale'])
    y_rnn, new_rnn_buf, new_h = rglru(u_rnn, rnn_buf, rnn_h0, lw['rnn_conv_w'], lw['rnn_conv_b'],
                                      lw['rnn_wa'], lw['rnn_ba'], lw['rnn_wx'], lw['rnn_bx'], lw['rnn_lambda'])
    y_attn = cross_attn(q, m_k, m_v).reshape(B, T, D_Q)
    merged = (gates[:, :, 0] * (y_pool @ lw['w_br_pool']).astype(jnp.float32)
              + gates[:, :, 1] * (y_rnn @ lw['w_br_rnn']).astype(jnp.float32)
              + gates[:, :, 2] * (y_attn @ lw['w_br_attn']).astype(jnp.float32)).astype(h.dtype)
    h = h + merged @ lw['w_out']
    xn2 = rmsnorm(h, lw['g_ffn'])
    gv = xn2 @ lw['w_up']
    g_pre, val = gv[..., :D_FF], gv[..., D_FF:]
    g_conv, new_ffn = causal_dwconv(g_pre, ffn_buf, lw['ffn_conv_w'], lw['ffn_conv_b'])
    h = h + (jax.nn.gelu(g_conv) * val) @ lw['w_down']
    return h, new_pool, new_rnn_buf, new_h, new_ffn


def setup_inputs(seed: int = 0) -> dict:
    key = jax.random.key(seed)
    ks = iter(jax.random.split(key, 40))

    def nrm(shape, scale):
        return jax.random.normal(next(ks), shape, jnp.float32) * scale

    u = jax.random.uniform(next(ks), (DEPTH, D_RNN), jnp.float32, minval=0.9, maxval=0.999)
    a0 = u ** (1.0 / RG_C)
    rnn_lambda = jnp.log(a0) - jnp.log1p(-a0)
    return {
        'x_prompt': nrm((BATCH, SEQ, D_MODEL), 1.0),
        'x_sample': nrm((DEC_BATCH, DEC_SEQ, D_MODEL), 1.0),
        'mem_prompt': nrm((BATCH, N_MEM, D_MODEL), 1.0),
        'cache_mem_k': nrm((DEPTH, DEC_BATCH, N_MEM, N_HEADS, HEAD_DIM), 1.0),
        'cache_mem_v': nrm((DEPTH, DEC_BATCH, N_MEM, N_HEADS, HEAD_DIM), 1.0),
        'state_pool': nrm((DEPTH, DEC_BATCH, POOL_BUF, D_POOL), 1.0),
        'state_rnn_conv': nrm((DEPTH, DEC_BATCH, RNN_CONV - 1, D_RNN), 1.0),
        'state_rnn_h': nrm((DEPTH, DEC_BATCH, D_RNN), 0.5),
        'state_ffn_conv': nrm((DEPTH, DEC_BATCH, FFN_CONV - 1, D_FF), 1.0),
        'g_mix': 1.0 + nrm((DEPTH, D_MODEL), 0.05),
        'w_in': nrm((DEPTH, D_MODEL, D_MIX), D_MODEL ** -0.5),
        'w_gate': nrm((DEPTH, D_MODEL, N_BRANCH * D_MODEL), D_MODEL ** -0.5),
        'b_gate': nrm((DEPTH, N_BRANCH * D_MODEL), 0.01),
        'pool_w': nrm((DEPTH, N_POOL_GROUPS, POOL_GROUP, POOL_GROUP), POOL_GROUP ** -0.5),
        'pool_scale': 1.0 + nrm((DEPTH, D_POOL), 0.1),
        'rnn_conv_w': nrm((DEPTH, RNN_CONV, D_RNN), RNN_CONV ** -0.5),
        'rnn_conv_b': nrm((DEPTH, D_RNN), 0.01),
        'rnn_wa': nrm((DEPTH, N_RNN_BLOCKS, RNN_BLOCK, RNN_BLOCK), RNN_BLOCK ** -0.5),
        'rnn_ba': nrm((DEPTH, D_RNN), 0.01),
        'rnn_wx': nrm((DEPTH, N_RNN_BLOCKS, RNN_BLOCK, RNN_BLOCK), RNN_BLOCK ** -0.5),
        'rnn_bx': nrm((DEPTH, D_RNN), 0.01),
        'rnn_lambda': rnn_lambda,
        'g_mem': 1.0 + nrm((DEPTH, D_MODEL), 0.05),
        'w_k': nrm((DEPTH, D_MODEL, D_Q), D_MODEL ** -0.5),
        'w_v': nrm((DEPTH, D_MODEL, D_Q), D_MODEL ** -0.5),
        'w_br_pool': nrm((DEPTH, D_POOL, D_MODEL), D_POOL ** -0.5),
        'w_br_rnn': nrm((DEPTH, D_RNN, D_MODEL), D_RNN ** -0.5),
        'w_br_attn': nrm((DEPTH, D_Q, D_MODEL), D_Q ** -0.5),
        'w_out': nrm((DEPTH, D_MODEL, D_MODEL), D_MODEL ** -0.5),
        'g_ffn': 1.0 + nrm((DEPTH, D_MODEL), 0.05),
        'w_up': nrm((DEPTH, D_MODEL, 2 * D_FF), D_MODEL ** -0.5),
        'ffn_conv_w': nrm((DEPTH, FFN_CONV, D_FF), FFN_CONV ** -0.5),
        'ffn_conv_b': nrm((DEPTH, D_FF), 0.01),
        'w_down': nrm((DEPTH, D_FF, D_MODEL), D_FF ** -0.5),
        'g_final': 1.0 + nrm((D_MODEL,), 0.05),
    }


def reference(x_prompt, x_sample, mem_prompt, cache_mem_k, cache_mem_v, state_pool, state_rnn_conv,
              state_rnn_h, state_ffn_conv, g_mix, w_in, w_gate, b_gate, pool_w, pool_scale,
              rnn_conv_w, rnn_conv_b, rnn_wa, rnn_ba, rnn_wx, rnn_bx, rnn_lambda, g_mem, w_k, w_v,
              w_br_pool, w_br_rnn, w_br_attn, w_out, g_ffn, w_up, ffn_conv_w, ffn_conv_b, w_down, g_final):
    Bp = x_prompt.shape[0]
    hp = x_prompt
    hs = x_sample
    p_pool, p_rconv, p_rh, p_fconv, p_mk, p_mv = [], [], [], [], [], []
    s_pool, s_rconv, s_rh, s_fconv = [], [], [], []
    for l in range(DEPTH):
        lw = {
            'g_mix': g_mix[l], 'w_in': w_in[l], 'w_gate': w_gate[l], 'b_gate': b_gate[l],
            'pool_w': pool_w[l], 'pool_scale': pool_scale[l],
            'rnn_conv_w': rnn_conv_w[l], 'rnn_conv_b': rnn_conv_b[l], 'rnn_wa': rnn_wa[l], 'rnn_ba': rnn_ba[l],
            'rnn_wx': rnn_wx[l], 'rnn_bx': rnn_bx[l], 'rnn_lambda': rnn_lambda[l],
            'w_br_pool': w_br_pool[l], 'w_br_rnn': w_br_rnn[l], 'w_br_attn': w_br_attn[l], 'w_out': w_out[l],
            'g_ffn': g_ffn[l], 'w_up': w_up[l], 'ffn_conv_w': ffn_conv_w[l], 'ffn_conv_b': ffn_conv_b[l],
            'w_down': w_down[l],
        }
        mk, mv = mem_kv(mem_prompt, g_mem[l], w_k[l], w_v[l])
        hp, npool, nrc, nrh, nfc = layer(
            hp, mk, mv,
            jnp.zeros((Bp, POOL_BUF, D_POOL), x_prompt.dtype),
            jnp.zeros((Bp, RNN_CONV - 1, D_RNN), x_prompt.dtype),
            jnp.zeros((Bp, D_RNN), state_rnn_h.dtype),
            jnp.zeros((Bp, FFN_CONV - 1, D_FF), x_prompt.dtype),
            0, lw)
        p_pool.append(npool); p_rconv.append(nrc); p_rh.append(nrh); p_fconv.append(nfc)
        p_mk.append(mk); p_mv.append(mv)
        hs, spool, src, srh, sfc = layer(
            hs, cache_mem_k[l], cache_mem_v[l], state_pool[l], state_rnn_conv[l], state_rnn_h[l],
            state_ffn_conv[l], PAST_LEN, lw)
        s_pool.append(spool); s_rconv.append(src); s_rh.append(srh); s_fconv.append(sfc)
    y_prompt = rmsnorm(hp, g_final)
    y_sample = rmsnorm(hs, g_final)
    return (y_prompt, y_sample,
            jnp.stack(p_pool), jnp.stack(p_rconv), jnp.stack(p_rh), jnp.stack(p_fconv),
            jnp.stack(p_mk), jnp.stack(p_mv),
            jnp.stack(s_pool), jnp.stack(s_rconv), jnp.stack(s_rh), jnp.stack(s_fconv))
```

```python
import contextlib
import numpy as np
import concourse.bass as bass
import concourse.mybir as mybir
from concourse.bass_utils import run_bass_kernel_spmd

F32 = mybir.dt.float32
BF16 = mybir.dt.bfloat16
AF = mybir.ActivationFunctionType
ALU = mybir.AluOpType

NCORES = 8
D = 1024
SEQ = 2048
NT = 512
NPT = SEQ // NT
SB_ = 16
ST_ = 8
NMEM = 256
DFF = 3072
EPS = 1e-6
NSLOT = 3
WSC_KIND = "Internal"
SLOTC = 4096
NDS = 24
NDS_POOL = 6
SAME_ENGINE_SYNC = True
INTERLEAVE = True
INTERLEAVE_SAMPLE = False
import os as _os0
SERIALIZE = _os0.environ.get('SERIALIZE', '0') == '1'


class Prog:
    ENG = ["pe", "act", "dve", "pool", "sp"]

    def __init__(self, nc, es):
        self.nc = nc
        self.sem = {e: es.enter_context(nc.semaphore("sem_" + e)) for e in self.ENG}
        self.cnt = {e: 0 for e in self.ENG}
        self.waited = {e: {} for e in self.ENG}
        self.code = {e: [] for e in self.ENG}
        self.lastw = {}
        self.readers = {}
        self.dsem = [es.enter_context(nc.semaphore("dsem%d" % i)) for i in range(NDS + NDS_POOL)]
        self.dcnt = [0] * (NDS + NDS_POOL)
        self.dnext = 0
        self.dnext_pool = 0
        self.nops = 0

    def _deps(self, engine, reads, writes):
        deps = {}

        def add(tok):
            if tok is None:
                return
            key, sem, val, eng = tok
            if eng == engine and (engine == "pe" or not SAME_ENGINE_SYNC) and not key.startswith("d"):
                return
            if self.waited[engine].get(key, 0) >= val:
                return
            if key not in deps or deps[key][1] < val:
                deps[key] = (sem, val)

        for r in reads:
            add(self.lastw.get(r))
            if isinstance(r, tuple) and r[0] == "bank":
                for tok in self.readers.get(r, {}).values():
                    if tok[3] != engine:
                        add(tok)
        for w in writes:
            add(self.lastw.get(w))
            for tok in self.readers.get(w, {}).values():
                add(tok)
        for key, (sem, val) in deps.items():
            self.waited[engine][key] = val
        return list(deps.values())

    def _commit(self, tok, reads, writes):
        key = tok[0]
        for w in writes:
            self.lastw[w] = tok
            self.readers[w] = {}
        for r in reads:
            if r in writes:
                continue
            d = self.readers.setdefault(r, {})
            if key not in d or d[key][2] < tok[2]:
                d[key] = tok

    def op(self, engine, fn, reads=(), writes=()):
        reads = tuple(reads)
        writes = tuple(writes)
        deps = self._deps(engine, reads, writes)
        if SERIALIZE and getattr(self, "lastop", None) is not None:
            lk, ls, lv, le = self.lastop
            if le != engine and self.waited[engine].get(lk, 0) < lv:
                deps.append((ls, lv))
                self.waited[engine][lk] = lv
        self.cnt[engine] += 1
        val = self.cnt[engine]
        sem = self.sem[engine]
        tok = ("c_" + engine, sem, val, engine)
        self.lastop = tok

        def emit(e, deps=deps, fn=fn, sem=sem):
            for s, v in deps:
                e.wait_ge(s, v)
            inst = fn(e)
            inst.then_inc(sem, 1)

        self.code[engine].append(emit)
        self._commit(tok, reads, writes)
        self.nops += 1
        self._coop_yield()

    def _coop_yield(self):
        import threading
        st = getattr(self, "_coop", None)
        if st is None or getattr(self, "_noyield", 0):
            return
        me = threading.current_thread()
        if me not in st["go"]:
            return
        st["yielded"].set()
        st["go"][me].wait()
        st["go"][me].clear()

    def interleave(self, chains):
        import threading
        st = {"go": {}, "yielded": threading.Event()}
        done = {}
        errs = []
        threads = []
        for fn in chains:
            def target(fn=fn):
                me = threading.current_thread()
                st["go"][me].wait()
                st["go"][me].clear()
                try:
                    fn()
                except BaseException as ex:
                    errs.append(ex)
                done[me] = True
                st["yielded"].set()
            t = threading.Thread(target=target)
            st["go"][t] = threading.Event()
            done[t] = False
            threads.append(t)
        self._coop = st
        for t in threads:
            t.start()
        active = list(threads)
        while active:
            for t in list(active):
                st["yielded"].clear()
                st["go"][t].set()
                st["yielded"].wait()
                if done[t]:
                    active.remove(t)
                if errs:
                    break
            if errs:
                break
        self._coop = None
        if errs:
            for t in threads:
                st["go"][t].set()
            raise errs[0]
        for t in threads:
            t.join()

    def dma(self, engine, out, in_, reads=(), writes=()):
        reads = tuple(reads)
        writes = tuple(writes)
        deps = self._deps(engine, reads, writes)
        if engine == "pool":
            i = NDS + self.dnext_pool
            self.dnext_pool = (self.dnext_pool + 1) % NDS_POOL
        else:
            i = self.dnext
            self.dnext = (self.dnext + 1) % NDS
        prev = self.dcnt[i]
        self.dcnt[i] += 16
        val = self.dcnt[i]
        sem = self.dsem[i]
        key = "d%d" % i
        if prev > 0 and self.waited[engine].get(key, 0) < prev:
            deps.append((sem, prev))
            self.waited[engine][key] = prev
        tok = (key, sem, val, None)

        def emit(e, deps=deps, sem=sem, out=out, in_=in_):
            for s, v in deps:
                e.wait_ge(s, v)
            e.dma_start(out=out, in_=in_).then_inc(sem, 16)

        self.code[engine].append(emit)
        self._commit(tok, reads, writes)

    def finalize(self):
        final = [(self.dsem[i], self.dcnt[i]) for i in range(NDS + NDS_POOL) if self.dcnt[i] > 0]
        allc = [(self.sem[e], self.cnt[e]) for e in self.ENG if self.cnt[e] > 0]
        code = self.code
        with self.nc.Block() as block:
            @block.tensor
            def _(e):
                for f in code["pe"]:
                    f(e)

            @block.scalar
            def _(e):
                for f in code["act"]:
                    f(e)

            @block.vector
            def _(e):
                for f in code["dve"]:
                    f(e)

            @block.gpsimd
            def _(e):
                for f in code["pool"]:
                    f(e)
                for s, v in final:
                    e.wait_ge(s, v)

            @block.sync
            def _(e):
                for f in code["sp"]:
                    f(e)
                for s, v in final:
                    e.wait_ge(s, v)
                for s, v in allc:
                    e.wait_ge(s, v)


def build_program(stage=99, ntiles=99, conv_only=None):
    nc = bass.Bass("TRN2", target_bir_lowering=False)
    es = contextlib.ExitStack()
    with es:
        es.enter_context(nc.allow_low_precision("bf16 matmul operands, fp32 accumulation"))
        es.enter_context(nc.allow_non_contiguous_dma("small strided state rows"))
        P = Prog(nc, es)

        def dram(name, shape, dt=F32, kind="ExternalInput"):
            return nc.dram_tensor(name, list(shape), dt, kind=kind).ap()

        xp = dram("xp", [SEQ, D])
        xs = dram("xs", [SB_, ST_, D])
        memp = dram("memp", [NMEM, D])
        ck = dram("ck", [2, SB_, NMEM, 512])
        cv = dram("cv", [2, SB_, NMEM, 512])
        st_pool = dram("st_pool", [2, SB_, 15, 512])
        st_rconv = dram("st_rconv", [2, SB_, 3, D])
        st_rh = dram("st_rh", [2, SB_, D])
        st_fconv = dram("st_fconv", [2, SB_, 2, DFF])
        consts = dram("consts", [128, 160])
        g_mix = dram("g_mix", [2, D]); w_in = dram("w_in", [2, D, 2048]); w_gate = dram("w_gate", [2, D, 3072])
        b_gate = dram("b_gate", [2, 3072]); pool_w = dram("pool_w", [2, 4, 128, 128]); pool_scale = dram("pool_scale", [2, 512])
        rnn_conv_w = dram("rnn_conv_w", [2, 4, D]); rnn_conv_b = dram("rnn_conv_b", [2, D])
        rnn_wa = dram("rnn_wa", [2, 8, 128, 128]); rnn_ba = dram("rnn_ba", [2, D])
        rnn_wx = dram("rnn_wx", [2, 8, 128, 128]); rnn_bx = dram("rnn_bx", [2, D]); rnn_lambda = dram("rnn_lambda", [2, D])
        g_mem = dram("g_mem", [2, D]); w_k = dram("w_k", [2, D, 512]); w_v = dram("w_v", [2, D, 512])
        w_br_pool = dram("w_br_pool", [2, 512, D]); w_br_rnn = dram("w_br_rnn", [2, D, D]); w_br_attn = dram("w_br_attn", [2, 512, D])
        w_out = dram("w_out", [2, D, D]); g_ffn = dram("g_ffn", [2, D]); w_up = dram("w_up", [2, D, 2 * DFF])
        ffn_conv_w = dram("ffn_conv_w", [2, 3, DFF]); ffn_conv_b = dram("ffn_conv_b", [2, DFF])
        w_down = dram("w_down", [2, DFF, D]); g_final = dram("g_final", [D])

        O = "ExternalOutput"
        y_p = dram("y_p", [SEQ, D], kind=O); y_s = dram("y_s", [SB_, ST_, D], kind=O)
        p_pool = dram("p_pool", [2, 15, 512], kind=O); p_rconv = dram("p_rconv", [2, 3, D], kind=O)
        p_rh = dram("p_rh", [2, 1, D], kind=O); p_fconv = dram("p_fconv", [2, 2, DFF], kind=O)
        p_mk = dram("p_mk", [2, NMEM, 512], kind=O); p_mv = dram("p_mv", [2, NMEM, 512], kind=O)
        s_pool = dram("s_pool", [2, SB_, 15, 512], kind=O); s_rconv = dram("s_rconv", [2, SB_, 3, D], kind=O)
        s_rh = dram("s_rh", [2, SB_, D], kind=O); s_fconv = dram("s_fconv", [2, SB_, 2, DFF], kind=O)

        blocks = []
        for i in (3, 1, 2, 0):
            blocks.append(("win%d" % i, 4096))
        blocks.append(("wk", 4096)); blocks.append(("wv", 4096))
        for f in range(8):
            blocks.append(("gg%d" % f, 3072))
            blocks.append(("gr%d" % f, 2048))
        for i in range(2):
            blocks.append(("wout%d" % i, 4096))
        for b in range(12):
            blocks.append(("wup%d" % b, 4096))
        for f in range(8):
            blocks.append(("wdn%d" % f, 3072))
        boff = {}
        off = 0
        for n, c in blocks:
            boff[n] = (off, c)
            off += c
        LCOLS = off
        wsc = nc.dram_tensor("wsc", [2, 128, LCOLS], BF16, kind=WSC_KIND).ap()

        def sb(name, shape, dt=F32):
            return es.enter_context(nc.sbuf_tensor(name, list(shape), dt))

        cst = sb("cst", [128, 160])
        ident = cst[:, 0:128]
        invcnt = cst[:, 128:143]
        c_one = cst[:, 143:144]
        c_eps = cst[:, 144:145]
        onesD = sb("onesD", [128, 128], BF16)
        ones1 = sb("ones1", [128, 128], BF16)
        NPV = 440
        pv = sb("pv", [128, NPV])
        clam = sb("clam", [128, 2, 8])
        clam2 = sb("clam2", [128, 2, 8])
        ltmp = sb("ltmp", [128, 8])
        poolw = sb("poolw", [128, 2, 4, 128], BF16)
        rwa = sb("rwa", [128, 2, 8, 128], BF16)
        rwx = sb("rwx", [128, 2, 8, 128], BF16)
        kT = sb("kT", [128, 2, 4, NMEM], BF16)
        vtm = sb("vtm", [128, 2, 2, 512], BF16)
        h_pool = sb("h_pool", [128, 2, 4, 15])
        h_rnn = sb("h_rnn", [128, 2, 8, 3])
        h_h = sb("h_h", [128, 2, 8])
        h_ffn = sb("h_ffn", [128, 2, 24, 2])
        def view(reg, off, shape, dt):
            n = 1
            for s_ in shape[1:]:
                n *= s_
            nb = n * (4 if dt == F32 else 2)
            ap = reg[:, off // 2:(off + nb) // 2]
            if dt == F32:
                ap = ap.bitcast(F32)
            if len(shape) == 3:
                ap = ap.rearrange("p (a b) -> p a b", a=shape[1])
            return ap

        hs_ffn = sb("hs_ffn", [128, 24, 32])
        hT = sb("hT", [128, 8, NT])
        xn = sb("xn", [128, 8, NT], BF16)
        xsq = sb("xsq", [128, 2, NT], BF16)
        rstd = sb("rstd", [128, NT])
        lnb = sb("lnb", [128, NT])
        PEXT = max(15 + NT, 23 * 16)
        REXT = max(3 + NT, 11 * 16)
        FEXT = max(2 + NT, 10 * 16)
        NRB = 2
        NFB = 2
        al = lambda x: (x + 63) // 64 * 64
        oA = {}
        o = 0
        for nm, nb in (("upx", 4 * PEXT * 4), ("urx", 8 * REXT * 4), ("qb", 4 * NT * 2), ("sA", PEXT * 4), ("sB", PEXT * 4),
                       ("dpl", 2 * NT * 2)):
            oA[nm] = o
            o = al(o + nb)
        szA = o
        o = 0
        for nm, nb in (("actb", 24 * NT * 2), ("fext0", FEXT * 4), ("fext1", FEXT * 4), ("fgc0", NT * 4), ("fgc1", NT * 4)):
            oA[nm] = o
            o = al(o + nb)
        szA = max(szA, o)
        regA = sb("regA", [128, szA // 2], BF16)
        upx = view(regA, oA["upx"], [128, 4, PEXT], F32)
        urx = view(regA, oA["urx"], [128, 8, REXT], F32)
        qb = view(regA, oA["qb"], [128, 4, NT], BF16)
        sA = view(regA, oA["sA"], [128, PEXT], F32)
        sB = view(regA, oA["sB"], [128, PEXT], F32)
        dpl = view(regA, oA["dpl"], [128, 2, NT], BF16)
        actb = view(regA, oA["actb"], [128, 24, NT], BF16)
        f_ext = [view(regA, oA["fext%d" % i], [128, FEXT], F32) for i in range(NFB)]
        f_gc = [view(regA, oA["fgc%d" % i], [128, NT], F32) for i in range(NFB)]
        ypool = sb("ypool", [128, 4, NT], BF16)
        yrnn = sb("yrnn", [128, 8, NT], BF16)
        yattn = sb("yattn", [128, 4, NT], BF16)
        r_xc = [sb("r_xc%d" % i, [128, NT]) for i in range(NRB)]
        r_xb = [sb("r_xb%d" % i, [128, NT], BF16) for i in range(NRB)]
        r_r = [sb("r_r%d" % i, [128, NT]) for i in range(NRB)]
        r_t = [sb("r_t%d" % i, [128, NT]) for i in range(NRB)]
        r_i = [sb("r_i%d" % i, [128, NT]) for i in range(NRB)]
        r_h = [sb("r_h%d" % i, [128, NT]) for i in range(NRB)]
        f_ge = r_xc
        regP = sb("regP", [128, 8192 // 2], BF16)
        PTb = [view(regP, i * 2048, [128, 2, 512], BF16) for i in range(2)]
        rsb = [view(regP, 4096 + i * 2048, [128, 512], F32) for i in range(2)]
        hs_pool = view(regP, 0, [128, 4, 240], F32)
        hs_rnn = view(regP, 6144, [128, 8, 48], F32)
        hs_h = view(regP, 6144 + 1536, [128, 8, 16], F32)
        NGS = 3
        regM = sb("regM", [128, (8 * NT * 2 + NGS * NT * 4) // 2], BF16)
        merged = view(regM, 0, [128, 8, NT], BF16)
        gsb = [view(regM, 8 * NT * 2 + i * NT * 4, [128, NT], F32) for i in range(NGS)]
        Kst = [view(regM, i * 4096, [128, 2, 512], F32) for i in range(2)]
        Vst = [view(regM, 8192, [128, 2, 512], BF16)]
        kTb = [view(regM, 10240, [128, 4, NMEM], BF16)]
        PTs = [view(regM, 12288 + i * 128, [128, 64], BF16) for i in range(2)]
        macc = sb("macc", [128, NT])
        mtmp = sb("mtmp", [128, NT])
        xr = sb("xr", [128, 2, D])
        yrow = sb("yrow", [128, D])
        orow = [sb("orow%d" % i, [128, 512]) for i in range(2)]
        kvrow = orow
        ring = [sb("ring%d" % i, [128, SLOTC], BF16) for i in range(NSLOT)]
        banks = [es.enter_context(nc.psum_tensor("psb%d" % i, [128, 512], F32)) for i in range(8)]

        bstate = {"next": 0, "held": set()}

        def bank(hold=False):
            for _ in range(16):
                i = bstate["next"]
                bstate["next"] = (i + 1) % 8
                if i not in bstate["held"]:
                    if hold:
                        bstate["held"].add(i)
                    return i
            raise RuntimeError("no psum bank")

        def unhold(i):
            bstate["held"].discard(i)

        def BK(i):
            return ("bank", i)

        def mm_group(bi, out_ap, pairs, reads, extra_writes=()):
            n = len(pairs)

            def fn(e):
                inst = None
                for k, (l, r) in enumerate(pairs):
                    inst = e.matmul(out_ap, l, r, start=(k == 0), stop=(k == n - 1))
                return inst
            P.op("pe", fn, reads=reads, writes=(BK(bi),) + tuple(extra_writes))

        def transposes(bi, items, reads):
            def fn(e):
                inst = None
                for (o, i_, idn) in items:
                    inst = e.transpose(o, i_, idn)
                return inst
            P.op("pe", fn, reads=reads, writes=(BK(bi),))

        def act(out, in_, func, reads, writes, bias=None, scale=None):
            kw = {}
            if bias is not None:
                kw["bias"] = bias
            if scale is not None:
                kw["scale"] = scale
            P.op("act", lambda e: e.activation(out=out, in_=in_, func=func, **kw), reads=reads, writes=writes)

        def acopy(out, in_, reads, writes):
            P.op("act", lambda e: e.copy(out=out, in_=in_), reads=reads, writes=writes)

        def vcopy(out, in_, reads, writes):
            P.op("dve", lambda e: e.tensor_copy(out=out, in_=in_), reads=reads, writes=writes)

        pool_ok = {"v": False}

        def pcopy(out, in_, reads, writes, alt="dve"):
            if pool_ok["v"]:
                P.op("pool", lambda e: e.tensor_copy(out=out, in_=in_), reads=reads, writes=writes)
            elif alt == "act":
                acopy(out, in_, reads, writes)
            else:
                vcopy(out, in_, reads, writes)

        cp_toggle = {"i": 0}

        def anycopy(out, in_, reads, writes):
            cp_toggle["i"] ^= 1
            if cp_toggle["i"]:
                acopy(out, in_, reads, writes)
            else:
                vcopy(out, in_, reads, writes)

        def vtt(out, in0, in1, op, reads, writes):
            P.op("dve", lambda e: e.tensor_tensor(out=out, in0=in0, in1=in1, op=op), reads=reads, writes=writes)

        def vts(out, in0, s1, s2, op0, op1, reads, writes):
            if op1 is None:
                P.op("dve", lambda e: e.tensor_scalar(out=out, in0=in0, scalar1=s1, scalar2=None, op0=op0),
                     reads=reads, writes=writes)
            else:
                P.op("dve", lambda e: e.tensor_scalar(out=out, in0=in0, scalar1=s1, scalar2=s2, op0=op0, op1=op1),
                     reads=reads, writes=writes)

        def vstt(out, in0, scalar, in1, op0, op1, reads, writes):
            P.op("dve", lambda e: e.scalar_tensor_tensor(out=out, in0=in0, scalar=scalar, in1=in1, op0=op0, op1=op1),
                 reads=reads, writes=writes)

        P.dma("sp", cst[:, :], consts, writes=["cst"])
        P.op("dve", lambda e: e.memset(onesD[:, :], 1.0 / D), writes=["onesD"])
        P.op("dve", lambda e: e.memset(ones1[:, :], 1.0), writes=["ones1"])
        P.op("dve", lambda e: e.memset(h_pool[:, :, :, :], 0.0), writes=["h_pool0", "h_pool1"])
        P.op("dve", lambda e: e.memset(h_rnn[:, :, :, :], 0.0), writes=["h_rnn0", "h_rnn1"])
        P.op("dve", lambda e: e.memset(h_h[:, :, :], 0.0), writes=["h_h0", "h_h1"])
        P.op("dve", lambda e: e.memset(h_ffn[:, :, :, :], 0.0), writes=["h_ffn0", "h_ffn1"])

        plist = []
        for l in range(2):
            plist += [
                (("g_mix", l), g_mix[l].rearrange("(c p) -> c p", p=128), 8),
                (("g_ffn", l), g_ffn[l].rearrange("(c p) -> c p", p=128), 8),
                (("b_gate", l), b_gate[l].rearrange("(c p) -> c p", p=128), 24),
                (("pool_scale", l), pool_scale[l].rearrange("(c p) -> c p", p=128), 4),
                (("rnn_conv_w", l), rnn_conv_w[l].rearrange("j (c p) -> (j c) p", p=128), 32),
                (("rnn_conv_b", l), rnn_conv_b[l].rearrange("(c p) -> c p", p=128), 8),
                (("rnn_ba", l), rnn_ba[l].rearrange("(c p) -> c p", p=128), 8),
                (("rnn_bx", l), rnn_bx[l].rearrange("(c p) -> c p", p=128), 8),
                (("rnn_lambda", l), rnn_lambda[l].rearrange("(c p) -> c p", p=128), 8),
                (("g_mem", l), g_mem[l].rearrange("(c p) -> c p", p=128), 8),
                (("ffn_conv_w", l), ffn_conv_w[l].rearrange("j (c p) -> (j c) p", p=128), 72),
                (("ffn_conv_b", l), ffn_conv_b[l].rearrange("(c p) -> c p", p=128), 24),
            ]
        plist.append((("g_final", 0), g_final.rearrange("(c p) -> c p", p=128), 8))
        pcol = {}
        col = 0
        segs = []
        for key, ap, C in plist:
            pcol[key] = col
            done = 0
            while done < C:
                ti, r0 = divmod(col + done, 128)
                n = min(C - done, 128 - r0)
                segs.append((ti, r0, n, ap[done:done + n, :]))
                done += n
            col += C
        assert col <= NPV
        nstage = (col + 127) // 128
        for (ti, r0, n, src) in segs:
            P.dma("sp", xr[r0:r0 + n, 0, ti * 128:(ti + 1) * 128], src, writes=[("pstage", ti)])
        for ti in range(nstage):
            R = min(128, col - ti * 128)
            bi = bank()
            transposes(bi, [(banks[bi][:, 0:R], xr[0:R, 0, ti * 128:(ti + 1) * 128], ident[0:R, 0:R])],
                       reads=[("pstage", ti), "cst"])
            vcopy(pv[:, ti * 128:ti * 128 + R], banks[bi][:, 0:R], [BK(bi)], ["pv"])
        P.op("dve", lambda e: e.memset(ltmp[:, :], 0.0), reads=[("pstage", t) for t in range(nstage)] + ["pv"],
             writes=["xr", "ltmp"])

        def PVc(name, l, c, n=1):
            o = pcol[(name, l)] + c
            return pv[:, o:o + n]

        for l in range(2):
            act(ltmp[:, :], PVc("rnn_lambda", l, 0, 8), AF.Exp, ["pv", "ltmp"], ["ltmp"], scale=-1.0)
            act(ltmp[:, :], ltmp[:, :], AF.Ln, ["ltmp", "cst"], ["ltmp"], bias=c_one)
            P.op("act", lambda e, l=l: e.activation(out=clam[:, l, :], in_=ltmp[:, :], func=AF.Copy, scale=-8.0),
                 reads=["ltmp"], writes=["clam"])
            P.op("act", lambda e, l=l: e.activation(out=clam2[:, l, :], in_=ltmp[:, :], func=AF.Copy, scale=-16.0),
                 reads=["ltmp"], writes=["clam"])

        for l in range(2):
            P.dma("pool", poolw[:, l, :, :], pool_w[l].rearrange("g c d -> c g d"), writes=[("poolw", l)])
            P.dma("pool", rwa[:, l, :, :], rnn_wa[l].rearrange("g c d -> c g d"), writes=[("rwa", l)])
            P.dma("pool", rwx[:, l, :, :], rnn_wx[l].rearrange("g c d -> c g d"), writes=[("rwx", l)])

        def conv_block(l, name):
            o, c = boff[name]
            dst = wsc[l][:, o:o + c]
            res = ("wsc", l, name)
            parts = []
            if name.startswith("win"):
                i = int(name[3:])
                parts.append((dst.rearrange("p (k f) -> p k f", k=8),
                              w_in[l][:, i * 512:(i + 1) * 512].rearrange("(k p) f -> p k f", p=128)))
            elif name == "wk":
                parts.append((dst.rearrange("p (k f) -> p k f", k=8), w_k[l].rearrange("(k p) f -> p k f", p=128)))
            elif name == "wv":
                parts.append((dst.rearrange("p (k f) -> p k f", k=8), w_v[l].rearrange("(k p) f -> p k f", p=128)))
            elif name.startswith("gg"):
                f = int(name[2:])
                d3 = dst.rearrange("p (k c) -> p k c", c=128)
                for br in range(3):
                    parts.append((d3[:, br * 8:(br + 1) * 8, :],
                                  w_gate[l][:, br * 1024 + f * 128: br * 1024 + (f + 1) * 128].rearrange("(k p) c -> p k c", p=128)))
            elif name.startswith("gr"):
                f = int(name[2:])
                d3 = dst.rearrange("p (k c) -> p k c", c=128)
                parts.append((d3[:, 0:4, :], w_br_pool[l][:, f * 128:(f + 1) * 128].rearrange("(k p) c -> p k c", p=128)))
                parts.append((d3[:, 4:12, :], w_br_rnn[l][:, f * 128:(f + 1) * 128].rearrange("(k p) c -> p k c", p=128)))
                parts.append((d3[:, 12:16, :], w_br_attn[l][:, f * 128:(f + 1) * 128].rearrange("(k p) c -> p k c", p=128)))
            elif name.startswith("wout"):
                i = int(name[4:])
                parts.append((dst.rearrange("p (k f) -> p k f", k=8),
                              w_out[l][:, i * 512:(i + 1) * 512].rearrange("(k p) f -> p k f", p=128)))
            elif name.startswith("wup"):
                b = int(name[3:])
                d4 = dst.rearrange("p (g k f) -> p g k f", g=2, k=8)
                for gv in range(2):
                    parts.append((d4[:, gv, :, :],
                                  w_up[l][:, gv * DFF + b * 256: gv * DFF + (b + 1) * 256].rearrange("(k p) f -> p k f", p=128)))
            elif name.startswith("wdn"):
                f = int(name[3:])
                parts.append((dst.rearrange("p (k c) -> p k c", c=128),
                              w_down[l][:, f * 128:(f + 1) * 128].rearrange("(k p) c -> p k c", p=128)))
            for pi, (d_, s_) in enumerate(parts):
                P.dma("pool", d_, s_, writes=[res + (pi,)])
            return [res + (pi,) for pi in range(len(parts))]

        wsc_parts = {}
        if stage < 1:
            conv_order_skip = True
        else:
            conv_order_skip = False
        conv_order = [(0, "wk"), (0, "wv"), (1, "wk"), (1, "wv")]
        for l in range(2):
            for n, _ in blocks:
                if n not in ("wk", "wv"):
                    conv_order.append((l, n))
        for (l, n) in conv_order:
            wsc_parts[(l, n)] = [] if (conv_order_skip or (conv_only is not None and not n.startswith(conv_only))) else conv_block(l, n)

        tiles = [("p", i) for i in range(NPT)] + [("s", 0)]
        seq = [(0, "wk"), (0, "wv"), (1, "wk"), (1, "wv")]
        for tl in tiles:
            for l in range(2):
                for n, _ in blocks:
                    if n not in ("wk", "wv"):
                        seq.append((l, n))
        wst = {"loaded": 0, "pos": 0}

        def w_issue(k):
            l, n = seq[k]
            o, c = boff[n]
            s = k % NSLOT
            P.dma("sp", ring[s][:, 0:c], wsc[l][:, o:o + c], reads=wsc_parts[(l, n)], writes=[("slot", s)])

        def w_get(name_expected, l_expected):
            k = wst["pos"]
            assert seq[k] == (l_expected, name_expected), (seq[k], l_expected, name_expected)
            while wst["loaded"] <= min(k + NSLOT - 1, len(seq) - 1) and wst["loaded"] < k + NSLOT:
                if wst["loaded"] >= len(seq):
                    break
                w_issue(wst["loaded"])
                wst["loaded"] += 1
            s = k % NSLOT
            return ring[s], ("slot", s)

        def w_done():
            wst["pos"] += 1
            k = wst["pos"]
            nxt = k - 1 + NSLOT
            if nxt < len(seq) and wst["loaded"] == nxt:
                w_issue(nxt)
                wst["loaded"] += 1

        def rmsnorm(src, src_res, N, gname, l, out_fn, out_res):
            bi = bank()
            for c in range(8):
                q = c % 2
                act(xsq[:, q, 0:N], src(c), AF.Square, [src_res(c), ("xsq", q)], [("xsq", q)])
                P.op("pe", lambda e, c=c, q=q, bi=bi: e.matmul(banks[bi][:, 0:N], onesD[:, :], xsq[:, q, 0:N],
                                                              start=(c == 0), stop=(c == 7)),
                     reads=[("xsq", q), "onesD"], writes=[BK(bi)])
            act(lnb[:, 0:N], banks[bi][:, 0:N], AF.Ln, [BK(bi), "cst"], ["lnb"], bias=c_eps)
            act(rstd[:, 0:N], lnb[:, 0:N], AF.Exp, ["lnb"], ["rstd"], scale=-0.5)
            for c in range(8):
                vstt(out_fn(c), src(c), PVc(gname, l, c), rstd[:, 0:N], ALU.mult, ALU.mult,
                     [src_res(c), "pv", "rstd"], [out_res(c)])

        class RowOut:
            def __init__(self, R, nchunks, dst_fn):
                self.R, self.n, self.dst_fn = R, nchunks, dst_fn
                self.bi = None
                self.k = 0
                self.flip = 0

            def add(self, c, src_ap, src_reads):
                P._noyield = getattr(P, "_noyield", 0) + 1
                try:
                    self._add(c, src_ap, src_reads)
                finally:
                    P._noyield -= 1

            def _add(self, c, src_ap, src_reads):
                R = self.R
                if self.bi is None:
                    self.bi = bank(hold=True)
                bi = self.bi
                j = c % 4
                transposes(bi, [(banks[bi][0:R, j * 128:(j + 1) * 128], src_ap, ident)], reads=list(src_reads) + ["cst"])
                self.k += 1
                if j == 3 or c == self.n - 1:
                    grp = c // 4
                    w = (j + 1) * 128
                    ob = orow[self.flip]
                    ores = ("orow", self.flip)
                    self.flip ^= 1
                    anycopy(ob[0:R, 0:w], banks[bi][0:R, 0:w], [BK(bi)], [ores])
                    for (dap, r0, nr) in self.dst_fn(grp, w):
                        P.dma("sp", dap, ob[r0:r0 + nr, 0:w], reads=[ores], writes=[])
                    unhold(bi)
                    self.bi = None

        def load_fm(rows_ap, rows_res, R, nch, dst_fn, dst_res):
            c = 0
            while c < nch:
                g = min(4, nch - c)
                bi = bank()
                transposes(bi, [(banks[bi][:, j * 128:j * 128 + R], rows_ap[0:R, (c + j) * 128:(c + j + 1) * 128],
                                 ident[0:R, 0:R]) for j in range(g)], reads=[rows_res, "cst"])
                cp_toggle["i"] ^= 1
                cpf = acopy if cp_toggle["i"] else vcopy
                for j in range(g):
                    cpf(dst_fn(c + j), banks[bi][:, j * 128:j * 128 + R], [BK(bi)], [dst_res(c + j)])
                c += g

        if stage < 2:
            P.finalize()
            return nc
        P.dma("sp", xr[:, 0:2, :], memp.rearrange("(a p) d -> p a d", p=128), reads=[], writes=["xr"])
        for a in range(2):
            for half in range(2):
                bi = bank()
                transposes(bi, [(banks[bi][:, k * 128:(k + 1) * 128], xr[:, a, (half * 4 + k) * 128:(half * 4 + k + 1) * 128], ident)
                                for k in range(4)], reads=["xr", "cst"])
                anycopy(hT[:, half * 4:(half + 1) * 4, a * 128:(a + 1) * 128],
                        banks[bi][:, :].rearrange("p (k t) -> p k t", k=4), [BK(bi)],
                        [("hT", c) for c in range(half * 4, half * 4 + 4)])
        for l in range(2):
            rmsnorm(lambda c: hT[:, c, 0:NMEM], lambda c: ("hT", c), NMEM, "g_mem", l,
                    lambda c: xn[:, c, 0:NMEM], lambda c: ("xn", c))
            wk_t, wk_r = w_get("wk", l)
            wk3 = wk_t[:, 0:4096].rearrange("p (k f) -> p k f", k=8)
            xnr = [("xn", c) for c in range(8)]
            for h in range(4):
                bi = bank()
                mm_group(bi, banks[bi][:, 0:NMEM], [(wk3[:, k, h * 128:(h + 1) * 128], xn[:, k, 0:NMEM]) for k in range(8)],
                         reads=[wk_r] + xnr)
                anycopy(kT[:, l, h, :], banks[bi][:, 0:NMEM], [BK(bi)], [("kT", l)])
            for mc in range(2):
                bi = bank()
                mm_group(bi, banks[bi][:, :], [(xn[:, k, mc * 128:(mc + 1) * 128], wk3[:, k, :]) for k in range(8)],
                         reads=[wk_r] + xnr)
                acopy(kvrow[mc][:, :], banks[bi][:, :], [BK(bi)], [("orow", mc)])
                P.dma("sp", p_mk[l][mc * 128:(mc + 1) * 128, :], kvrow[mc][:, :], reads=[("orow", mc)])
            w_done()
            wv_t, wv_r = w_get("wv", l)
            wv3 = wv_t[:, 0:4096].rearrange("p (k f) -> p k f", k=8)
            for mc in range(2):
                bi = bank()
                mm_group(bi, banks[bi][:, :], [(xn[:, k, mc * 128:(mc + 1) * 128], wv3[:, k, :]) for k in range(8)],
                         reads=[wv_r] + xnr)
                acopy(kvrow[mc][:, :], banks[bi][:, :], [BK(bi)], [("orow", mc)])
                acopy(vtm[:, l, mc, :], banks[bi][:, :], [BK(bi)], [("vtm", l)])
                P.dma("sp", p_mv[l][mc * 128:(mc + 1) * 128, :], kvrow[mc][:, :], reads=[("orow", mc)])
            w_done()

        KEYS_A = ([("upx", g) for g in range(4)] + [("urx", n) for n in range(8)] + [("qb", h) for h in range(4)]
                  + ["sA", "sB", ("dpl", 0), ("dpl", 1)] + [("xc", q) for q in range(NRB)]
                  + [("actb", j) for j in range(24)] + [(nm, q) for nm in ("fext", "fgc", "fge") for q in range(NFB)])
        KEYS_M = ([("merged", f) for f in range(8)] + [("gs", i) for i in range(NGS)] + [("Kst", 0), ("Kst", 1), ("Vst", 0),
                  ("kTb", 0, 0), ("kTb", 0, 1), ("PTs", 0), ("PTs", 1)])
        KEYS_P = ([("PT", i, mc) for i in range(2) for mc in range(2)] + [("rs", 0), ("rs", 1)]
                  + [("hs_pool", c) for c in range(4)] + [("hs_rnn", c) for c in range(8)] + [("hs_h", c) for c in range(8)])

        def barrier(keys):
            P.op("dve", lambda e: e.memset(ltmp[:, :], 0.0), reads=[], writes=list(keys) + ["ltmp"])

        def tile_layer(kind, ti, l):
            samp = (kind == "s")
            N = 128 if samp else NT
            inner = 16 if samp else 1
            T = N // inner
            import os as _os2
            first = (not samp) and (ti == 0 or _os2.environ.get('FORCE_FIRST') == '1')
            last = (not samp) and ti == NPT - 1
            hTr = [("hT", c) for c in range(8)]
            xnr = [("xn", c) for c in range(8)]
            HP, HR, HF = 15 * inner, 3 * inner, 2 * inner

            barrier(KEYS_A + KEYS_M + KEYS_P)
            if samp:
                for j in range(15):
                    rr = (j % 8) * 16
                    if j % 8 == 0:
                        pass
                    P.dma("sp", xr[rr:rr + 16, 0 if j < 8 else 1, 0:512], st_pool[l][:, j, :], reads=["xr"], writes=[("xrs", j)])
                P.op("dve", lambda e: e.memset(ltmp[:, :], 0.0), reads=[("xrs", j) for j in range(15)] + ["xr"], writes=["xr", "ltmp"])
                load_fm(xr[:, 0, :], "xr", 128, 4, lambda c: hs_pool[:, c, 0:128], lambda c: ("hs_pool", c))
                load_fm(xr[:, 1, :], "xr", 112, 4, lambda c: hs_pool[:, c, 128:240], lambda c: ("hs_pool", c))
                P.dma("sp", s_pool[l][:, 0:7, :], st_pool[l][:, 8:15, :])
                P.op("dve", lambda e: e.memset(ltmp[:, :], 0.0), reads=["xr", ("hs_pool", 0), ("hs_pool", 1), ("hs_pool", 2), ("hs_pool", 3)],
                     writes=["xr", "ltmp"])
                for j in range(3):
                    P.dma("sp", xr[j * 16:(j + 1) * 16, 0, :], st_rconv[l][:, j, :], reads=["xr"], writes=[("xrs", j)])
                P.dma("sp", xr[0:16, 1, :], st_rh[l][:, :], reads=["xr"], writes=[("xrs", 3)])
                P.op("dve", lambda e: e.memset(ltmp[:, :], 0.0), reads=[("xrs", j) for j in range(4)] + ["xr"], writes=["xr", "ltmp"])
                load_fm(xr[:, 0, :], "xr", 48, 8, lambda c: hs_rnn[:, c, :], lambda c: ("hs_rnn", c))
                load_fm(xr[:, 1, :], "xr", 16, 8, lambda c: hs_h[:, c, :], lambda c: ("hs_h", c))
                P.op("dve", lambda e: e.memset(ltmp[:, :], 0.0), reads=["xr"] + [("hs_rnn", c) for c in range(8)] + [("hs_h", c) for c in range(8)],
                     writes=["xr", "ltmp"])
                for part in range(3):
                    for j in range(2):
                        P.dma("sp", xr[j * 16:(j + 1) * 16, 0, :], st_fconv[l][:, j, part * 1024:(part + 1) * 1024],
                              reads=["xr"], writes=[("xrs", j)])
                    P.op("dve", lambda e: e.memset(ltmp[:, :], 0.0), reads=[("xrs", 0), ("xrs", 1), "xr"], writes=["xr", "ltmp"])
                    load_fm(xr[:, 0, :], "xr", 32, 8, lambda c, part=part: hs_ffn[:, part * 8 + c, :],
                            lambda c, part=part: ("hs_ffn", part * 8 + c))
                    P.op("dve", lambda e: e.memset(ltmp[:, :], 0.0), reads=["xr"] + [("hs_ffn", part * 8 + c) for c in range(8)],
                         writes=["xr", "ltmp"])

            rmsnorm(lambda c: hT[:, c, 0:N], lambda c: ("hT", c), N, "g_mix", l,
                    lambda c: xn[:, c, 0:N], lambda c: ("xn", c))

            for zb in (3, 1, 2, 0):
                wt, wr = w_get("win%d" % zb, l)
                w3 = wt[:, 0:4096].rearrange("p (k f) -> p k f", k=8)
                for zi in range(4):
                    zc = zb * 4 + zi
                    bi = bank()
                    mm_group(bi, banks[bi][:, 0:N], [(w3[:, k, zi * 128:(zi + 1) * 128], xn[:, k, 0:N]) for k in range(8)],
                             reads=[wr] + xnr)
                    if zc < 4:
                        anycopy(upx[:, zc, HP:HP + N], banks[bi][:, 0:N], [BK(bi)], [("upx", zc)])
                    elif zc < 12:
                        anycopy(urx[:, zc - 4, HR:HR + N], banks[bi][:, 0:N], [BK(bi)], [("urx", zc - 4)])
                    else:
                        h = zc - 12
                        if samp:
                            o_ = qb[:, h, 0:N].rearrange("p (b t) -> p t b", t=ST_)
                            i_ = banks[bi][:, 0:N].rearrange("p (t b) -> p t b", b=SB_)
                        else:
                            o_ = qb[:, h, 0:N]
                            i_ = banks[bi][:, 0:N]
                        P.op("act", lambda e, o_=o_, i_=i_: e.activation(out=o_, in_=i_, func=AF.Copy, scale=128.0 ** -0.5),
                             reads=[BK(bi)], writes=[("qb", h)])
                w_done()

            for g in range(4):
                if samp:
                    vcopy(upx[:, g, 0:HP], hs_pool[:, g, :], [("hs_pool", g)], [("upx", g)])
                else:
                    vcopy(upx[:, g, 0:HP], h_pool[:, l, g, :], [("h_pool%d" % l)], [("upx", g)])
            for n in range(8):
                if samp:
                    vcopy(urx[:, n, 0:HR], hs_rnn[:, n, :], [("hs_rnn", n)], [("urx", n)])
                else:
                    vcopy(urx[:, n, 0:HR], h_rnn[:, l, n, :], [("h_rnn%d" % l)], [("urx", n)])

            def gen_attn():
                if not samp:
                    for h in range(4):
                        pt = PTb[h % 2]
                        ptr = ("PT", h % 2)
                        for mc in range(2):
                            bi = bank()
                            mm_group(bi, banks[bi][:, 0:N], [(kT[:, l, h, mc * 128:(mc + 1) * 128], qb[:, h, 0:N])],
                                     reads=[("kT", l), ("qb", h)])
                            act(pt[:, mc, 0:N], banks[bi][:, 0:N], AF.Exp, [BK(bi)], [ptr + (mc,)])
                        bo = bank()
                        mm_group(bo, banks[bo][:, 0:N], [(vtm[:, l, mc, h * 128:(h + 1) * 128], pt[:, mc, 0:N]) for mc in range(2)],
                                 reads=[("vtm", l), ptr + (0,), ptr + (1,)])
                        bd = bank()
                        mm_group(bd, banks[bd][:, 0:N], [(ones1[:, :], pt[:, mc, 0:N]) for mc in range(2)],
                                 reads=["ones1", ptr + (0,), ptr + (1,)])
                        rs = rsb[h % 2]
                        rsr = ("rs", h % 2)
                        act(rs[:, 0:N], banks[bd][:, 0:N], AF.Ln, [BK(bd)], [rsr])
                        act(rs[:, 0:N], rs[:, 0:N], AF.Exp, [rsr], [rsr], scale=-1.0)
                        vtt(yattn[:, h, 0:N], banks[bo][:, 0:N], rs[:, 0:N], ALU.mult, [BK(bo), rsr], [("yattn", h)])
                        yield
                else:
                    barrier(KEYS_M)
                    bO = bank(hold=True)
                    bD = bank(hold=True)
                    for b in range(SB_):
                        ks = Kst[b % 2]; vs = Vst[0]; kb = kTb[0]; ps = PTs[b % 2]
                        P.dma("sp", ks[:, :, :], ck[l][b].rearrange("(a p) f -> p a f", p=128), writes=[("Kst", b % 2)])
                        P.dma("pool", vs[:, :, :], cv[l][b].rearrange("(a p) f -> p a f", p=128), writes=[("Vst", 0)])
                        for hp in range(2):
                            bi = bank()
                            transposes(bi, [(banks[bi][:, (hh * 2 + mc) * 128:(hh * 2 + mc + 1) * 128],
                                             ks[:, mc, (hp * 2 + hh) * 128:(hp * 2 + hh + 1) * 128], ident)
                                            for hh in range(2) for mc in range(2)], reads=[("Kst", b % 2), "cst"])
                            anycopy(kb[:, hp * 2:hp * 2 + 2, :], banks[bi][:, :].rearrange("p (h m) -> p h m", h=2),
                                    [BK(bi)], [("kTb", 0, hp)])
                        bs = bank()
                        def fn_s(e, kb=kb, bs=bs, b=b):
                            inst = None
                            for mc in range(2):
                                for h in range(4):
                                    inst = e.matmul(banks[bs][:, (mc * 4 + h) * 8:(mc * 4 + h + 1) * 8],
                                                    kb[:, h, mc * 128:(mc + 1) * 128], qb[:, h, b * 8:(b + 1) * 8],
                                                    start=True, stop=True)
                            return inst
                        P.op("pe", fn_s, reads=[("kTb", 0, 0), ("kTb", 0, 1)] + [("qb", h) for h in range(4)], writes=[BK(bs)])
                        act(ps[:, 0:64], banks[bs][:, 0:64], AF.Exp, [BK(bs)], [("PTs", b % 2)])

                        def fn_o(e, vs=vs, ps=ps, b=b):
                            inst = None
                            for h in range(4):
                                for mc in range(2):
                                    inst = e.matmul(banks[bO][:, h * 128 + b * 8: h * 128 + (b + 1) * 8],
                                                    vs[:, mc, h * 128:(h + 1) * 128], ps[:, (mc * 4 + h) * 8:(mc * 4 + h + 1) * 8],
                                                    start=(mc == 0), stop=(mc == 1))
                            return inst
                        P.op("pe", fn_o, reads=[("Vst", 0), ("PTs", b % 2)], writes=[BK(bO)])

                        def fn_d(e, ps=ps, b=b):
                            inst = None
                            for mc in range(2):
                                inst = e.matmul(banks[bD][:, b * 32:(b + 1) * 32], ones1[:, :], ps[:, mc * 32:(mc + 1) * 32],
                                                start=(mc == 0), stop=(mc == 1))
                            return inst
                        P.op("pe", fn_d, reads=["ones1", ("PTs", b % 2)], writes=[BK(bD)])
                    rs = rsb[0]
                    act(rs[:, 0:512], banks[bD][:, :], AF.Ln, [BK(bD)], [("rs", 0)])
                    act(rs[:, 0:512], rs[:, 0:512], AF.Exp, [("rs", 0)], [("rs", 0)], scale=-1.0)
                    rs4 = rs[:, 0:512].rearrange("p (b h t) -> p h b t", h=4, t=ST_)
                    for h in range(4):
                        vtt(yattn[:, h, 0:N].rearrange("p (t b) -> p b t", b=SB_),
                            banks[bO][:, h * 128:(h + 1) * 128].rearrange("p (b t) -> p b t", t=ST_),
                            rs4[:, h, :, :], ALU.mult, [BK(bO), ("rs", 0)], [("yattn", h)])
                    unhold(bO); unhold(bD)
                    barrier(KEYS_M)

                yield
            def gen_pool():
                if samp:
                    ro_pool = RowOut(128, 4, lambda grp, w: [(s_pool[l][:, 7 + t, :], t * 16, 16) for t in range(ST_)])
                elif last:
                    ro_pool = RowOut(15, 4, lambda grp, w: [(p_pool[l][:, :], 0, 15)])
                else:
                    ro_pool = None
                for g in range(4):
                    wdw = 2 ** (g + 1)
                    e_ = upx[:, g, :]
                    er = ("upx", g)
                    L = HP + N
                    if g == 0:
                        vtt(sA[:, HP:L], e_[:, HP:L], e_[:, HP - inner:L - inner], ALU.add, [er, "sA"], ["sA"])
                        win = sA
                        winr = "sA"
                    else:
                        lo = HP - (wdw - 2) * inner
                        vtt(sA[:, lo:L], e_[:, lo:L], e_[:, lo - inner:L - inner], ALU.add, [er, "sA"], ["sA"])
                        cur, curr, oth, othr = sA, "sA", sB, "sB"
                        step = 2
                        while step < wdw:
                            lo = lo + step * inner
                            vtt(oth[:, lo:L], cur[:, lo:L], cur[:, lo - step * inner:L - step * inner], ALU.add, [curr, othr], [othr])
                            cur, curr, oth, othr = oth, othr, cur, curr
                            step *= 2
                        win, winr = cur, curr
                    dq = g % 2
                    vstt(dpl[:, dq, 0:N], win[:, HP:L], 1.0 / wdw, e_[:, HP:L], ALU.mult, ALU.subtract,
                         [winr, er, ("dpl", dq)], [("dpl", dq)])
                    if first:
                        k = wdw - 1
                        vtt(mtmp[:, 0:k], win[:, HP:HP + k], invcnt[:, 0:k], ALU.mult, [winr, "cst", "mtmp"], ["mtmp"])
                        vtt(dpl[:, dq, 0:k], mtmp[:, 0:k], e_[:, HP:HP + k], ALU.subtract, ["mtmp", er, ("dpl", dq)], [("dpl", dq)])
                    bi = bank()
                    mm_group(bi, banks[bi][:, 0:N], [(poolw[:, l, g, :], dpl[:, dq, 0:N])], reads=[("poolw", l), ("dpl", dq)])
                    vts(ypool[:, g, 0:N], banks[bi][:, 0:N], PVc("pool_scale", l, g), None, ALU.mult, None,
                        [BK(bi), "pv"], [("ypool", g)])
                    if ro_pool is not None:
                        if samp:
                            ro_pool.add(g, e_[:, HP:HP + 128], [er])
                        else:
                            ro_pool.add(g, e_[:, L - 15:L], [er])
                    if not samp and not last:
                        vcopy(h_pool[:, l, g, :], e_[:, L - 15:L], [er], ["h_pool%d" % l])

                    yield
                yield
            def mk_ro_r():
                if samp:
                    ro_rc = RowOut(48, 8, lambda grp, w: [(s_rconv[l][:, t, grp * 512:grp * 512 + w], t * 16, 16) for t in range(3)])
                    ro_rh = RowOut(16, 8, lambda grp, w: [(s_rh[l][:, grp * 512:grp * 512 + w], 0, 16)])
                elif last:
                    ro_rc = RowOut(3, 8, lambda grp, w: [(p_rconv[l][:, grp * 512:grp * 512 + w], 0, 3)])
                    ro_rh = RowOut(1, 8, lambda grp, w: [(p_rh[l][:, grp * 512:grp * 512 + w], 0, 1)])
                else:
                    ro_rc = ro_rh = None
                return ro_rc, ro_rh
            ro_rc, ro_rh = mk_ro_r()

            def gen_rglru(ns=range(8)):
                cw0 = pcol[("rnn_conv_w", l)]
                for n in ns:
                    q = n % NRB
                    e_ = urx[:, n, :]
                    er = ("urx", n)
                    xc, xb, rr, tt, ii, hh = r_xc[q], r_xb[q], r_r[q], r_t[q], r_i[q], r_h[q]
                    R = lambda s, q=q: (s, q)
                    if pool_ok["v"]:
                        P.op("pool", lambda e, xc=xc, e_=e_, n=n: e.tensor_scalar(
                            out=xc[:, 0:N], in0=e_[:, 0:N], scalar1=pv[:, cw0 + n:cw0 + n + 1], scalar2=PVc("rnn_conv_b", l, n),
                            op0=ALU.mult, op1=ALU.add), reads=[er, "pv", R("xc")], writes=[R("xc")])
                    else:
                        act(xc[:, 0:N], e_[:, 0:N], AF.Identity, [er, "pv", R("xc")], [R("xc")],
                            bias=PVc("rnn_conv_b", l, n), scale=pv[:, cw0 + n:cw0 + n + 1])
                    for j in range(1, 4):
                        vstt(xc[:, 0:N], e_[:, j * inner:j * inner + N], pv[:, cw0 + j * 8 + n:cw0 + j * 8 + n + 1], xc[:, 0:N],
                             ALU.mult, ALU.add, [er, "pv", R("xc")], [R("xc")])
                    pcopy(xb[:, 0:N], xc[:, 0:N], [R("xc"), R("xb")], [R("xb")], alt="act")
                    br_ = bank()
                    mm_group(br_, banks[br_][:, 0:N], [(rwa[:, l, n, :], xb[:, 0:N])], reads=[("rwa", l), R("xb")])
                    bx_ = bank()
                    mm_group(bx_, banks[bx_][:, 0:N], [(rwx[:, l, n, :], xb[:, 0:N])], reads=[("rwx", l), R("xb")])
                    act(rr[:, 0:N], banks[br_][:, 0:N], AF.Sigmoid, [BK(br_), "pv", R("r")], [R("r")], bias=PVc("rnn_ba", l, n))
                    act(ii[:, 0:N], banks[bx_][:, 0:N], AF.Sigmoid, [BK(bx_), "pv", R("i")], [R("i")], bias=PVc("rnn_bx", l, n))
                    act(tt[:, 0:N], rr[:, 0:N], AF.Exp, [R("r"), "clam", R("t")], [R("t")], scale=clam2[:, l, n:n + 1])
                    act(rr[:, 0:N], rr[:, 0:N], AF.Exp, [R("r"), "clam"], [R("r")], scale=clam[:, l, n:n + 1])
                    act(tt[:, 0:N], tt[:, 0:N], AF.Ln, [R("t"), "cst"], [R("t")], bias=c_one, scale=-1.0)
                    act(tt[:, 0:N], tt[:, 0:N], AF.Exp, [R("t")], [R("t")], scale=0.5)
                    vtt(ii[:, 0:N], ii[:, 0:N], xc[:, 0:N], ALU.mult, [R("i"), R("xc")], [R("i")])
                    vtt(ii[:, 0:N], ii[:, 0:N], tt[:, 0:N], ALU.mult, [R("i"), R("t")], [R("i")])
                    if not samp:
                        P.op("dve", lambda e, hh=hh, rr=rr, ii=ii, n=n: e.tensor_tensor_scan(
                            out=hh[:, 0:N], data0=rr[:, 0:N], data1=ii[:, 0:N], initial=h_h[:, l, n:n + 1],
                            op0=ALU.mult, op1=ALU.add), reads=[R("r"), R("i"), "h_h%d" % l, R("h")], writes=[R("h")])
                        vcopy(h_h[:, l, n:n + 1], hh[:, N - 1:N], [R("h")], ["h_h%d" % l])
                    else:
                        for t in range(ST_):
                            prev = hs_h[:, n, :] if t == 0 else hh[:, (t - 1) * 16:t * 16]
                            vtt(hh[:, t * 16:(t + 1) * 16], rr[:, t * 16:(t + 1) * 16], prev, ALU.mult,
                                [R("r"), R("h"), ("hs_h", n)], [R("h")])
                            vtt(hh[:, t * 16:(t + 1) * 16], hh[:, t * 16:(t + 1) * 16], ii[:, t * 16:(t + 1) * 16], ALU.add,
                                [R("i"), R("h")], [R("h")])
                    pcopy(yrnn[:, n, 0:N], hh[:, 0:N], [R("h")], [("yrnn", n)], alt="act")
                    if ro_rc is not None:
                        if samp:
                            ro_rc.add(n, e_[:, HR + 5 * 16:HR + 8 * 16], [er])
                            ro_rh.add(n, hh[:, 7 * 16:8 * 16], [R("h")])
                        else:
                            ro_rc.add(n, e_[:, HR + N - 3:HR + N], [er])
                            ro_rh.add(n, hh[:, N - 1:N], [R("h")])
                    if not samp and not last:
                        vcopy(h_rnn[:, l, n, :], e_[:, HR + N - 3:HR + N], [er], ["h_rnn%d" % l])

                    yield
                yield
            def drain(g_):
                for _ in g_:
                    pass
            if (samp and not INTERLEAVE_SAMPLE) or not INTERLEAVE:
                drain(gen_attn()); drain(gen_pool()); drain(gen_rglru())
            else:
                P.interleave([lambda: drain(gen_rglru(range(0, 8, 2))), lambda: drain(gen_rglru(range(1, 8, 2))),
                              lambda: drain(gen_attn()), lambda: drain(gen_pool())])

            bcol = pcol[("b_gate", l)]
            yp_r = [("ypool", g) for g in range(4)]
            yr_r = [("yrnn", n) for n in range(8)]
            ya_r = [("yattn", h) for h in range(4)]
            gi = 0
            for f in range(8):
                wtg, wrg = w_get("gg%d" % f, l)
                wg3 = wtg[:, 0:3072].rearrange("p (k c) -> p k c", c=128)
                gates_ps = {}
                for br in (0, 2, 1):
                    bg = bank()
                    mm_group(bg, banks[bg][:, 0:N], [(wg3[:, br * 8 + k, :], xn[:, k, 0:N]) for k in range(8)], reads=[wrg] + xnr)
                    gs = gsb[gi % NGS]
                    gr = ("gs", gi % NGS)
                    gi += 1
                    act(gs[:, 0:N], banks[bg][:, 0:N], AF.Sigmoid, [BK(bg), "pv", gr], [gr],
                        bias=pv[:, bcol + br * 8 + f: bcol + br * 8 + f + 1])
                    gates_ps[br] = (gs, gr)
                w_done()
                wt, wr = w_get("gr%d" % f, l)
                w3 = wt[:, 0:2048].rearrange("p (k c) -> p k c", c=128)
                first_term = True
                for (br, k0, nk, ysrc, yres) in ((0, 0, 4, ypool, yp_r), (2, 12, 4, yattn, ya_r), (1, 4, 8, yrnn, yr_r)):
                    gs, gr = gates_ps[br]
                    bp = bank()
                    mm_group(bp, banks[bp][:, 0:N], [(w3[:, k0 + k, :], ysrc[:, k, 0:N]) for k in range(nk)], reads=[wr] + yres)
                    if first_term:
                        vtt(macc[:, 0:N], gs[:, 0:N], banks[bp][:, 0:N], ALU.mult, [gr, BK(bp), "macc"], ["macc"])
                        first_term = False
                    elif br == 2:
                        vtt(mtmp[:, 0:N], gs[:, 0:N], banks[bp][:, 0:N], ALU.mult, [gr, BK(bp), "mtmp"], ["mtmp"])
                        vtt(macc[:, 0:N], macc[:, 0:N], mtmp[:, 0:N], ALU.add, ["macc", "mtmp"], ["macc"])
                    else:
                        vtt(mtmp[:, 0:N], gs[:, 0:N], banks[bp][:, 0:N], ALU.mult, [gr, BK(bp), "mtmp"], ["mtmp"])
                        vtt(merged[:, f, 0:N], macc[:, 0:N], mtmp[:, 0:N], ALU.add, ["macc", "mtmp"], [("merged", f)])
                w_done()

            mr = [("merged", f) for f in range(8)]
            for ob in range(2):
                wt, wr = w_get("wout%d" % ob, l)
                w3 = wt[:, 0:4096].rearrange("p (k f) -> p k f", k=8)
                for fi in range(4):
                    f = ob * 4 + fi
                    bi = bank()
                    mm_group(bi, banks[bi][:, 0:N], [(w3[:, k, fi * 128:(fi + 1) * 128], merged[:, k, 0:N]) for k in range(8)],
                             reads=[wr] + mr)
                    vtt(hT[:, f, 0:N], hT[:, f, 0:N], banks[bi][:, 0:N], ALU.add, [("hT", f), BK(bi)], [("hT", f)])
                w_done()

            barrier(KEYS_A)
            rmsnorm(lambda c: hT[:, c, 0:N], lambda c: ("hT", c), N, "g_ffn", l,
                    lambda c: xn[:, c, 0:N], lambda c: ("xn", c))
            if samp:
                ro_fc = RowOut(32, 24, lambda grp, w: [(s_fconv[l][:, t, grp * 512:grp * 512 + w], t * 16, 16) for t in range(2)])
            elif last:
                ro_fc = RowOut(2, 24, lambda grp, w: [(p_fconv[l][:, grp * 512:grp * 512 + w], 0, 2)])
            else:
                ro_fc = None
            fw0 = pcol[("ffn_conv_w", l)]
            for ub in range(12):
                wt, wr = w_get("wup%d" % ub, l)
                w4 = wt[:, 0:4096].rearrange("p (g k f) -> p g k f", g=2, k=8)
                for jj in range(2):
                    j = ub * 2 + jj
                    q = j % NFB
                    ex, gc, ge = f_ext[q], f_gc[q], f_ge[q]
                    R = lambda s, q=q: (s, q)
                    bg = bank()
                    mm_group(bg, banks[bg][:, 0:N], [(w4[:, 0, k, jj * 128:(jj + 1) * 128], xn[:, k, 0:N]) for k in range(8)],
                             reads=[wr] + xnr)
                    bv = bank()
                    mm_group(bv, banks[bv][:, 0:N], [(w4[:, 1, k, jj * 128:(jj + 1) * 128], xn[:, k, 0:N]) for k in range(8)],
                             reads=[wr] + xnr)
                    if samp:
                        pcopy(ex[:, 0:HF], hs_ffn[:, j, :], [("hs_ffn", j), R("fext")], [R("fext")])
                    else:
                        pcopy(ex[:, 0:HF], h_ffn[:, l, j, :], ["h_ffn%d" % l, R("fext")], [R("fext")])
                    acopy(ex[:, HF:HF + N], banks[bg][:, 0:N], [BK(bg), R("fext")], [R("fext")])
                    act(gc[:, 0:N], banks[bg][:, 0:N], AF.Identity, [BK(bg), "pv", R("fgc")], [R("fgc")],
                        bias=PVc("ffn_conv_b", l, j), scale=pv[:, fw0 + 2 * 24 + j:fw0 + 2 * 24 + j + 1])
                    for tap in range(0, 2):
                        vstt(gc[:, 0:N], ex[:, tap * inner:tap * inner + N], pv[:, fw0 + tap * 24 + j:fw0 + tap * 24 + j + 1],
                             gc[:, 0:N], ALU.mult, ALU.add, [R("fext"), "pv", R("fgc")], [R("fgc")])
                    act(ge[:, 0:N], gc[:, 0:N], AF.Gelu_apprx_tanh, [R("fgc"), R("fge")], [R("fge")])
                    vtt(actb[:, j, 0:N], ge[:, 0:N], banks[bv][:, 0:N], ALU.mult, [R("fge"), BK(bv)], [("actb", j)])
                    if ro_fc is not None:
                        ro_fc.add(j, ex[:, HF + N - 2 * inner:HF + N], [R("fext")])
                    if not samp and not last:
                        pcopy(h_ffn[:, l, j, :], ex[:, HF + N - 2:HF + N], [R("fext")], ["h_ffn%d" % l])
                w_done()
            ar = [("actb", j) for j in range(24)]
            for f in range(8):
                wt, wr = w_get("wdn%d" % f, l)
                w3 = wt[:, 0:3072].rearrange("p (k c) -> p k c", c=128)
                bi = bank()
                mm_group(bi, banks[bi][:, 0:N], [(w3[:, k, :], actb[:, k, 0:N]) for k in range(24)], reads=[wr] + ar)
                vtt(hT[:, f, 0:N], hT[:, f, 0:N], banks[bi][:, 0:N], ALU.add, [("hT", f), BK(bi)], [("hT", f)])
                w_done()

        if stage < 3:
            P.finalize()
            return nc
        import os as _os
        _sel = _os.environ.get('TILESEL')
        _tl = [tiles[int(x)] for x in _sel.split(',')] if _sel else (tiles[:ntiles] + (tiles[-1:] if ntiles < 0 else []))
        for (kind, ti) in _tl:
            samp = kind == "s"
            N = 128 if samp else NT
            if samp:
                for t in range(ST_):
                    P.dma("sp", xr[t * 16:(t + 1) * 16, 0, :], xs[:, t, :], reads=["xr"], writes=[("xrs", t)])
                P.op("dve", lambda e: e.memset(ltmp[:, :], 0.0), reads=[("xrs", t) for t in range(ST_)] + ["xr"], writes=["xr", "ltmp"])
            for a in range(N // 128):
                q = a % 2
                if samp:
                    xres = "xr"
                else:
                    xres = ("xrg", q)
                    r0 = ti * NT + a * 128
                    P.dma("sp", xr[:, q, :], xp[r0:r0 + 128, :], reads=["xr"], writes=[xres])
                for half in range(2):
                    bi = bank()
                    transposes(bi, [(banks[bi][:, k * 128:(k + 1) * 128], xr[:, q, (half * 4 + k) * 128:(half * 4 + k + 1) * 128], ident)
                                    for k in range(4)], reads=[xres, "cst"])
                    anycopy(hT[:, half * 4:(half + 1) * 4, a * 128:(a + 1) * 128],
                            banks[bi][:, :].rearrange("p (k t) -> p k t", k=4), [BK(bi)],
                            [("hT", c) for c in range(half * 4, half * 4 + 4)])
            P.op("dve", lambda e: e.memset(ltmp[:, :], 0.0), reads=[("hT", c) for c in range(8)],
                 writes=["xr", ("xrg", 0), ("xrg", 1), "ltmp"])
            pool_ok["v"] = not (kind == "p" and ti == 0)
            for l in range(2):
                tile_layer(kind, ti, l)
            rmsnorm(lambda c: hT[:, c, 0:N], lambda c: ("hT", c), N, "g_final", 0,
                    lambda c: hT[:, c, 0:N], lambda c: ("hT", c))
            for a in range(N // 128):
                for half in range(2):
                    bi = bank()
                    transposes(bi, [(banks[bi][:, k * 128:(k + 1) * 128], hT[:, half * 4 + k, a * 128:(a + 1) * 128], ident)
                                    for k in range(4)], reads=[("hT", half * 4 + k) for k in range(4)] + ["cst"])
                    anycopy(yrow[:, half * 512:(half + 1) * 512], banks[bi][:, :], [BK(bi)], [("yrow", half)])
                if samp:
                    for t in range(ST_):
                        P.dma("sp", y_s[:, t, :], yrow[t * 16:(t + 1) * 16, :], reads=[("yrow", 0), ("yrow", 1)])
                else:
                    r0 = ti * NT + a * 128
                    P.dma("sp", y_p[r0:r0 + 128, :], yrow[:, :], reads=[("yrow", 0), ("yrow", 1)])
        assert stage < 99 or ntiles < 99 or wst["pos"] == len(seq), (wst["pos"], len(seq))
        print('OPCOUNTS', P.cnt, P.dcnt, flush=True)
        P.finalize()
    return nc


_CACHE = {}


def _consts():
    c = np.zeros((128, 160), np.float32)
    c[:, 0:128] = np.eye(128, dtype=np.float32)
    c[:, 128:143] = (1.0 / np.arange(1, 16, dtype=np.float32))[None, :]
    c[:, 143] = 1.0
    c[:, 144] = EPS
    return c


def kernel(x_prompt, x_sample, mem_prompt, cache_mem_k, cache_mem_v, state_pool, state_rnn_conv,
           state_rnn_h, state_ffn_conv, g_mix, w_in, w_gate, b_gate, pool_w, pool_scale,
           rnn_conv_w, rnn_conv_b, rnn_wa, rnn_ba, rnn_wx, rnn_bx, rnn_lambda, g_mem, w_k, w_v,
           w_br_pool, w_br_rnn, w_br_attn, w_out, g_ffn, w_up, ffn_conv_w, ffn_conv_b, w_down, g_final):
    f32 = lambda a: np.ascontiguousarray(np.asarray(a, dtype=np.float32))
    if "nc" not in _CACHE:
        _CACHE["nc"] = build_program()
    nc = _CACHE["nc"]
    shared = dict(consts=_consts(), g_mix=f32(g_mix), w_in=f32(w_in), w_gate=f32(w_gate), b_gate=f32(b_gate),
                  pool_w=f32(pool_w), pool_scale=f32(pool_scale), rnn_conv_w=f32(rnn_conv_w), rnn_conv_b=f32(rnn_conv_b),
                  rnn_wa=f32(rnn_wa), rnn_ba=f32(rnn_ba), rnn_wx=f32(rnn_wx), rnn_bx=f32(rnn_bx), rnn_lambda=f32(rnn_lambda),
                  g_mem=f32(g_mem), w_k=f32(w_k), w_v=f32(w_v), w_br_pool=f32(w_br_pool), w_br_rnn=f32(w_br_rnn),
                  w_br_attn=f32(w_br_attn), w_out=f32(w_out), g_ffn=f32(g_ffn), w_up=f32(w_up), ffn_conv_w=f32(ffn_conv_w),
                  ffn_conv_b=f32(ffn_conv_b), w_down=f32(w_down), g_final=f32(g_final))
    xpr = f32(x_prompt); xsa = f32(x_sample); mp = f32(mem_prompt)
    ckk = f32(cache_mem_k).reshape(2, 128, NMEM, 512); cvv = f32(cache_mem_v).reshape(2, 128, NMEM, 512)
    sp_ = f32(state_pool); src_ = f32(state_rnn_conv); srh_ = f32(state_rnn_h); sfc_ = f32(state_ffn_conv)
    in_maps = []
    for c in range(NCORES):
        b0, b1 = c * SB_, (c + 1) * SB_
        m = dict(shared)
        m.update(xp=xpr[c], xs=np.ascontiguousarray(xsa[b0:b1]), memp=mp[c],
                 ck=np.ascontiguousarray(ckk[:, b0:b1]), cv=np.ascontiguousarray(cvv[:, b0:b1]),
                 st_pool=np.ascontiguousarray(sp_[:, b0:b1]), st_rconv=np.ascontiguousarray(src_[:, b0:b1]),
                 st_rh=np.ascontiguousarray(srh_[:, b0:b1]), st_fconv=np.ascontiguousarray(sfc_[:, b0:b1]))
        in_maps.append(m)
    res = run_bass_kernel_spmd(nc, in_maps, core_ids=list(range(NCORES)))
    R = res.results
    g = lambda k: [np.asarray(R[c][k], dtype=np.float32) for c in range(NCORES)]
    y_prompt = np.stack(g("y_p"), 0)
    y_sample = np.concatenate(g("y_s"), 0)
    p_pool = np.stack(g("p_pool"), 1)
    p_rconv = np.stack(g("p_rconv"), 1)
    p_rh = np.stack([a.reshape(2, D) for a in g("p_rh")], 1)
    p_fconv = np.stack(g("p_fconv"), 1)
    p_mk = np.stack(g("p_mk"), 1).reshape(2, NCORES, NMEM, 4, 128)
    p_mv = np.stack(g("p_mv"), 1).reshape(2, NCORES, NMEM, 4, 128)
    s_pool = np.concatenate(g("s_pool"), 1)
    s_rconv = np.concatenate(g("s_rconv"), 1)
    s_rh = np.concatenate(g("s_rh"), 1)
    s_fconv = np.concatenate(g("s_fconv"), 1)
    return (y_prompt, y_sample, p_pool, p_rconv, p_rh, p_fconv, p_mk, p_mv, s_pool, s_rconv, s_rh, s_fconv)
```

```python
import contextlib
import numpy as np
import concourse.bass as bass
import concourse.mybir as mybir
from concourse.bass_utils import run_bass_kernel_spmd

F32 = mybir.dt.float32
BF16 = mybir.dt.bfloat16
AF = mybir.ActivationFunctionType
ALU = mybir.AluOpType

NCORES = 8
D = 1024
SEQ = 2048
NT = 512
NPT = SEQ // NT
SB_ = 16
ST_ = 8
NMEM = 256
DFF = 3072
EPS = 1e-6
NSLOT = 3
WSC_KIND = "Internal"
SLOTC = 4096
NDS = 24
NDS_POOL = 6
SAME_ENGINE_SYNC = True
INTERLEAVE = True
INTERLEAVE_SAMPLE = True
import os as _os0
SERIALIZE = _os0.environ.get('SERIALIZE', '0') == '1'


class Prog:
    ENG = ["pe", "act", "dve", "pool", "sp"]

    def __init__(self, nc, es):
        self.nc = nc
        self.sem = {e: es.enter_context(nc.semaphore("sem_" + e)) for e in self.ENG}
        self.cnt = {e: 0 for e in self.ENG}
        self.waited = {e: {} for e in self.ENG}
        self.code = {e: [] for e in self.ENG}
        self.lastw = {}
        self.readers = {}
        self.dsem = [es.enter_context(nc.semaphore("dsem%d" % i)) for i in range(NDS + NDS_POOL)]
        self.dcnt = [0] * (NDS + NDS_POOL)
        self.dnext = 0
        self.dnext_pool = 0
        self.nops = 0

    def _deps(self, engine, reads, writes):
        deps = {}

        def add(tok):
            if tok is None:
                return
            key, sem, val, eng = tok
            if eng == engine and (engine == "pe" or not SAME_ENGINE_SYNC) and not key.startswith("d"):
                return
            if self.waited[engine].get(key, 0) >= val:
                return
            if key not in deps or deps[key][1] < val:
                deps[key] = (sem, val)

        for r in reads:
            add(self.lastw.get(r))
            if isinstance(r, tuple) and r[0] == "bank":
                for tok in self.readers.get(r, {}).values():
                    if tok[3] != engine:
                        add(tok)
        for w in writes:
            add(self.lastw.get(w))
            for tok in self.readers.get(w, {}).values():
                add(tok)
        for key, (sem, val) in deps.items():
            self.waited[engine][key] = val
        return list(deps.values())

    def _commit(self, tok, reads, writes):
        key = tok[0]
        for w in writes:
            self.lastw[w] = tok
            self.readers[w] = {}
        for r in reads:
            if r in writes:
                continue
            d = self.readers.setdefault(r, {})
            if key not in d or d[key][2] < tok[2]:
                d[key] = tok

    def op(self, engine, fn, reads=(), writes=()):
        reads = tuple(reads)
        writes = tuple(writes)
        deps = self._deps(engine, reads, writes)
        if SERIALIZE and getattr(self, "lastop", None) is not None:
            lk, ls, lv, le = self.lastop
            if le != engine and self.waited[engine].get(lk, 0) < lv:
                deps.append((ls, lv))
                self.waited[engine][lk] = lv
        self.cnt[engine] += 1
        val = self.cnt[engine]
        sem = self.sem[engine]
        tok = ("c_" + engine, sem, val, engine)
        self.lastop = tok

        def emit(e, deps=deps, fn=fn, sem=sem):
            for s, v in deps:
                e.wait_ge(s, v)
            inst = fn(e)
            inst.then_inc(sem, 1)

        self.code[engine].append(emit)
        self._commit(tok, reads, writes)
        self.nops += 1
        self._coop_yield()

    def _coop_yield(self):
        import threading
        st = getattr(self, "_coop", None)
        if st is None or getattr(self, "_noyield", 0):
            return
        me = threading.current_thread()
        if me not in st["go"]:
            return
        st["yielded"].set()
        st["go"][me].wait()
        st["go"][me].clear()

    def interleave(self, chains):
        import threading
        st = {"go": {}, "yielded": threading.Event()}
        done = {}
        errs = []
        threads = []
        for fn in chains:
            def target(fn=fn):
                me = threading.current_thread()
                st["go"][me].wait()
                st["go"][me].clear()
                try:
                    fn()
                except BaseException as ex:
                    errs.append(ex)
                done[me] = True
                st["yielded"].set()
            t = threading.Thread(target=target)
            st["go"][t] = threading.Event()
            done[t] = False
            threads.append(t)
        self._coop = st
        for t in threads:
            t.start()
        active = list(threads)
        while active:
            for t in list(active):
                st["yielded"].clear()
                st["go"][t].set()
                st["yielded"].wait()
                if done[t]:
                    active.remove(t)
                if errs:
                    break
            if errs:
                break
        self._coop = None
        if errs:
            for t in threads:
                st["go"][t].set()
            raise errs[0]
        for t in threads:
            t.join()

    def dma(self, engine, out, in_, reads=(), writes=()):
        reads = tuple(reads)
        writes = tuple(writes)
        deps = self._deps(engine, reads, writes)
        if engine == "pool":
            i = NDS + self.dnext_pool
            self.dnext_pool = (self.dnext_pool + 1) % NDS_POOL
        else:
            i = self.dnext
            self.dnext = (self.dnext + 1) % NDS
        prev = self.dcnt[i]
        self.dcnt[i] += 16
        val = self.dcnt[i]
        sem = self.dsem[i]
        key = "d%d" % i
        if prev > 0 and self.waited[engine].get(key, 0) < prev:
            deps.append((sem, prev))
            self.waited[engine][key] = prev
        tok = (key, sem, val, None)

        def emit(e, deps=deps, sem=sem, out=out, in_=in_):
            for s, v in deps:
                e.wait_ge(s, v)
            e.dma_start(out=out, in_=in_).then_inc(sem, 16)

        self.code[engine].append(emit)
        self._commit(tok, reads, writes)

    def finalize(self):
        final = [(self.dsem[i], self.dcnt[i]) for i in range(NDS + NDS_POOL) if self.dcnt[i] > 0]
        allc = [(self.sem[e], self.cnt[e]) for e in self.ENG if self.cnt[e] > 0]
        code = self.code
        with self.nc.Block() as block:
            @block.tensor
            def _(e):
                for f in code["pe"]:
                    f(e)

            @block.scalar
            def _(e):
                for f in code["act"]:
                    f(e)

            @block.vector
            def _(e):
                for f in code["dve"]:
                    f(e)

            @block.gpsimd
            def _(e):
                for f in code["pool"]:
                    f(e)
                for s, v in final:
                    e.wait_ge(s, v)

            @block.sync
            def _(e):
                for f in code["sp"]:
                    f(e)
                for s, v in final:
                    e.wait_ge(s, v)
                for s, v in allc:
                    e.wait_ge(s, v)


def build_program(stage=99, ntiles=99, conv_only=None):
    nc = bass.Bass("TRN2", target_bir_lowering=False)
    es = contextlib.ExitStack()
    with es:
        es.enter_context(nc.allow_low_precision("bf16 matmul operands, fp32 accumulation"))
        es.enter_context(nc.allow_non_contiguous_dma("small strided state rows"))
        P = Prog(nc, es)

        def dram(name, shape, dt=F32, kind="ExternalInput"):
            return nc.dram_tensor(name, list(shape), dt, kind=kind).ap()

        xp = dram("xp", [SEQ, D])
        xs = dram("xs", [SB_, ST_, D])
        memp = dram("memp", [NMEM, D])
        ck = dram("ck", [2, SB_, NMEM, 512])
        cv = dram("cv", [2, SB_, NMEM, 512])
        st_pool = dram("st_pool", [2, SB_, 15, 512])
        st_rconv = dram("st_rconv", [2, SB_, 3, D])
        st_rh = dram("st_rh", [2, SB_, D])
        st_fconv = dram("st_fconv", [2, SB_, 2, DFF])
        consts = dram("consts", [128, 160])
        g_mix = dram("g_mix", [2, D]); w_in = dram("w_in", [2, D, 2048]); w_gate = dram("w_gate", [2, D, 3072])
        b_gate = dram("b_gate", [2, 3072]); pool_w = dram("pool_w", [2, 4, 128, 128]); pool_scale = dram("pool_scale", [2, 512])
        rnn_conv_w = dram("rnn_conv_w", [2, 4, D]); rnn_conv_b = dram("rnn_conv_b", [2, D])
        rnn_wa = dram("rnn_wa", [2, 8, 128, 128]); rnn_ba = dram("rnn_ba", [2, D])
        rnn_wx = dram("rnn_wx", [2, 8, 128, 128]); rnn_bx = dram("rnn_bx", [2, D]); rnn_lambda = dram("rnn_lambda", [2, D])
        g_mem = dram("g_mem", [2, D]); w_k = dram("w_k", [2, D, 512]); w_v = dram("w_v", [2, D, 512])
        w_br_pool = dram("w_br_pool", [2, 512, D]); w_br_rnn = dram("w_br_rnn", [2, D, D]); w_br_attn = dram("w_br_attn", [2, 512, D])
        w_out = dram("w_out", [2, D, D]); g_ffn = dram("g_ffn", [2, D]); w_up = dram("w_up", [2, D, 2 * DFF])
        ffn_conv_w = dram("ffn_conv_w", [2, 3, DFF]); ffn_conv_b = dram("ffn_conv_b", [2, DFF])
        w_down = dram("w_down", [2, DFF, D]); g_final = dram("g_final", [D])

        O = "ExternalOutput"
        y_p = dram("y_p", [SEQ, D], kind=O); y_s = dram("y_s", [SB_, ST_, D], kind=O)
        p_pool = dram("p_pool", [2, 15, 512], kind=O); p_rconv = dram("p_rconv", [2, 3, D], kind=O)
        p_rh = dram("p_rh", [2, 1, D], kind=O); p_fconv = dram("p_fconv", [2, 2, DFF], kind=O)
        p_mk = dram("p_mk", [2, NMEM, 512], kind=O); p_mv = dram("p_mv", [2, NMEM, 512], kind=O)
        s_pool = dram("s_pool", [2, SB_, 15, 512], kind=O); s_rconv = dram("s_rconv", [2, SB_, 3, D], kind=O)
        s_rh = dram("s_rh", [2, SB_, D], kind=O); s_fconv = dram("s_fconv", [2, SB_, 2, DFF], kind=O)

        blocks = []
        for i in (3, 1, 2, 0):
            blocks.append(("win%d" % i, 4096))
        blocks.append(("wk", 4096)); blocks.append(("wv", 4096))
        for f in range(8):
            blocks.append(("gg%d" % f, 3072))
            blocks.append(("gr%d" % f, 2048))
        for i in range(2):
            blocks.append(("wout%d" % i, 4096))
        for b in range(12):
            blocks.append(("wup%d" % b, 4096))
        for f in range(8):
            blocks.append(("wdn%d" % f, 3072))
        boff = {}
        off = 0
        for n, c in blocks:
            boff[n] = (off, c)
            off += c
        LCOLS = off
        wsc = nc.dram_tensor("wsc", [2, 128, LCOLS], BF16, kind=WSC_KIND).ap()

        def sb(name, shape, dt=F32):
            return es.enter_context(nc.sbuf_tensor(name, list(shape), dt))

        cst = sb("cst", [128, 160])
        ident = cst[:, 0:128]
        invcnt = cst[:, 128:143]
        c_one = cst[:, 143:144]
        c_eps = cst[:, 144:145]
        onesD = sb("onesD", [128, 128], BF16)
        ones1 = sb("ones1", [128, 128], BF16)
        NPV = 440
        pv = sb("pv", [128, NPV])
        clam = sb("clam", [128, 2, 8])
        clam2 = sb("clam2", [128, 2, 8])
        ltmp = sb("ltmp", [128, 8])
        poolw = sb("poolw", [128, 2, 4, 128], BF16)
        rwa = sb("rwa", [128, 2, 8, 128], BF16)
        rwx = sb("rwx", [128, 2, 8, 128], BF16)
        kT = sb("kT", [128, 2, 4, NMEM], BF16)
        vtm = sb("vtm", [128, 2, 2, 512], BF16)
        h_pool = sb("h_pool", [128, 2, 4, 15])
        h_rnn = sb("h_rnn", [128, 2, 8, 3])
        h_h = sb("h_h", [128, 2, 8])
        h_ffn = sb("h_ffn", [128, 2, 24, 2])
        def view(reg, off, shape, dt):
            n = 1
            for s_ in shape[1:]:
                n *= s_
            nb = n * (4 if dt == F32 else 2)
            ap = reg[:, off // 2:(off + nb) // 2]
            if dt == F32:
                ap = ap.bitcast(F32)
            if len(shape) == 3:
                ap = ap.rearrange("p (a b) -> p a b", a=shape[1])
            return ap

        hs_ffn = sb("hs_ffn", [128, 24, 32])
        hT = sb("hT", [128, 8, NT])
        xn = sb("xn", [128, 8, NT], BF16)
        xsq = sb("xsq", [128, 2, NT], BF16)
        rstd = sb("rstd", [128, NT])
        lnb = sb("lnb", [128, NT])
        PEXT = max(15 + NT, 23 * 16)
        REXT = max(3 + NT, 11 * 16)
        FEXT = max(2 + NT, 10 * 16)
        NRB = 2
        NFB = 2
        al = lambda x: (x + 63) // 64 * 64
        oA = {}
        o = 0
        for nm, nb in (("upx", 4 * PEXT * 4), ("urx", 8 * REXT * 4), ("qb", 4 * NT * 2), ("sA", PEXT * 4), ("sB", PEXT * 4),
                       ("dpl", 2 * NT * 2)):
            oA[nm] = o
            o = al(o + nb)
        szA = o
        o = 0
        for nm, nb in (("actb", 24 * NT * 2), ("fext0", FEXT * 4), ("fext1", FEXT * 4), ("fgc0", NT * 4), ("fgc1", NT * 4)):
            oA[nm] = o
            o = al(o + nb)
        szA = max(szA, o)
        regA = sb("regA", [128, szA // 2], BF16)
        upx = view(regA, oA["upx"], [128, 4, PEXT], F32)
        urx = view(regA, oA["urx"], [128, 8, REXT], F32)
        qb = view(regA, oA["qb"], [128, 4, NT], BF16)
        sA = view(regA, oA["sA"], [128, PEXT], F32)
        sB = view(regA, oA["sB"], [128, PEXT], F32)
        dpl = view(regA, oA["dpl"], [128, 2, NT], BF16)
        actb = view(regA, oA["actb"], [128, 24, NT], BF16)
        f_ext = [view(regA, oA["fext%d" % i], [128, FEXT], F32) for i in range(NFB)]
        f_gc = [view(regA, oA["fgc%d" % i], [128, NT], F32) for i in range(NFB)]
        ypool = sb("ypool", [128, 4, NT], BF16)
        yrnn = sb("yrnn", [128, 8, NT], BF16)
        yattn = sb("yattn", [128, 4, NT], BF16)
        r_xc = [sb("r_xc%d" % i, [128, NT]) for i in range(NRB)]
        r_xb = [sb("r_xb%d" % i, [128, NT], BF16) for i in range(NRB)]
        r_r = [sb("r_r%d" % i, [128, NT]) for i in range(NRB)]
        r_t = [sb("r_t%d" % i, [128, NT]) for i in range(NRB)]
        r_i = [sb("r_i%d" % i, [128, NT]) for i in range(NRB)]
        r_h = [sb("r_h%d" % i, [128, NT]) for i in range(NRB)]
        f_ge = r_xc
        regP = sb("regP", [128, 8192 // 2], BF16)
        PTb = [view(regP, i * 2048, [128, 2, 512], BF16) for i in range(2)]
        rsb = [view(regP, 4096 + i * 2048, [128, 512], F32) for i in range(2)]
        hs_pool = view(regP, 0, [128, 4, 240], F32)
        hs_rnn = view(regP, 6144, [128, 8, 48], F32)
        hs_h = view(regP, 6144 + 1536, [128, 8, 16], F32)
        NGS = 3
        regM = sb("regM", [128, (8 * NT * 2 + NGS * NT * 4) // 2], BF16)
        merged = view(regM, 0, [128, 8, NT], BF16)
        gsb = [view(regM, 8 * NT * 2 + i * NT * 4, [128, NT], F32) for i in range(NGS)]
        Kst = [view(regM, i * 4096, [128, 2, 512], F32) for i in range(2)]
        Vst = [view(regM, 8192, [128, 2, 512], BF16)]
        kTb = [view(regM, 10240, [128, 4, NMEM], BF16)]
        PTs = [view(regM, 12288 + i * 128, [128, 64], BF16) for i in range(2)]
        macc = sb("macc", [128, NT])
        mtmp = sb("mtmp", [128, NT])
        xr = sb("xr", [128, 2, D])
        yrow = sb("yrow", [128, D])
        orow = [sb("orow%d" % i, [128, 512]) for i in range(2)]
        kvrow = orow
        ring = [sb("ring%d" % i, [128, SLOTC], BF16) for i in range(NSLOT)]
        banks = [es.enter_context(nc.psum_tensor("psb%d" % i, [128, 512], F32)) for i in range(8)]

        bstate = {"next": 0, "held": set()}

        def bank_free(i):
            if i in bstate["held"]:
                return False
            key = ("bank", i)
            if P.lastw.get(key) is None:
                return True
            return len(P.readers.get(key, {})) > 0

        def bank(hold=False):
            tries = 0
            while True:
                for _ in range(8):
                    i = bstate["next"]
                    bstate["next"] = (i + 1) % 8
                    if bank_free(i):
                        if hold:
                            bstate["held"].add(i)
                        return i
                tries += 1
                if getattr(P, "_coop", None) is None or tries > 5000:
                    raise RuntimeError("no psum bank available (tries=%d)" % tries)
                P._coop_yield()

        def unhold(i):
            bstate["held"].discard(i)

        def BK(i):
            return ("bank", i)

        def mm_group(bi, out_ap, pairs, reads, extra_writes=()):
            n = len(pairs)

            def fn(e):
                inst = None
                for k, (l, r) in enumerate(pairs):
                    inst = e.matmul(out_ap, l, r, start=(k == 0), stop=(k == n - 1))
                return inst
            P.op("pe", fn, reads=reads, writes=(BK(bi),) + tuple(extra_writes))

        def transposes(bi, items, reads):
            def fn(e):
                inst = None
                for (o, i_, idn) in items:
                    inst = e.transpose(o, i_, idn)
                return inst
            P.op("pe", fn, reads=reads, writes=(BK(bi),))

        def act(out, in_, func, reads, writes, bias=None, scale=None):
            kw = {}
            if bias is not None:
                kw["bias"] = bias
            if scale is not None:
                kw["scale"] = scale
            P.op("act", lambda e: e.activation(out=out, in_=in_, func=func, **kw), reads=reads, writes=writes)

        def acopy(out, in_, reads, writes):
            P.op("act", lambda e: e.copy(out=out, in_=in_), reads=reads, writes=writes)

        def vcopy(out, in_, reads, writes):
            P.op("dve", lambda e: e.tensor_copy(out=out, in_=in_), reads=reads, writes=writes)

        pool_ok = {"v": False}

        def pcopy(out, in_, reads, writes, alt="dve"):
            if pool_ok["v"]:
                P.op("pool", lambda e: e.tensor_copy(out=out, in_=in_), reads=reads, writes=writes)
            elif alt == "act":
                acopy(out, in_, reads, writes)
            else:
                vcopy(out, in_, reads, writes)

        cp_toggle = {"i": 0}

        def anycopy(out, in_, reads, writes):
            cp_toggle["i"] ^= 1
            if cp_toggle["i"]:
                acopy(out, in_, reads, writes)
            else:
                vcopy(out, in_, reads, writes)

        def vtt(out, in0, in1, op, reads, writes):
            P.op("dve", lambda e: e.tensor_tensor(out=out, in0=in0, in1=in1, op=op), reads=reads, writes=writes)

        def vts(out, in0, s1, s2, op0, op1, reads, writes):
            if op1 is None:
                P.op("dve", lambda e: e.tensor_scalar(out=out, in0=in0, scalar1=s1, scalar2=None, op0=op0),
                     reads=reads, writes=writes)
            else:
                P.op("dve", lambda e: e.tensor_scalar(out=out, in0=in0, scalar1=s1, scalar2=s2, op0=op0, op1=op1),
                     reads=reads, writes=writes)

        def vstt(out, in0, scalar, in1, op0, op1, reads, writes):
            P.op("dve", lambda e: e.scalar_tensor_tensor(out=out, in0=in0, scalar=scalar, in1=in1, op0=op0, op1=op1),
                 reads=reads, writes=writes)

        P.dma("sp", cst[:, :], consts, writes=["cst"])
        P.op("dve", lambda e: e.memset(onesD[:, :], 1.0 / D), writes=["onesD"])
        P.op("dve", lambda e: e.memset(ones1[:, :], 1.0), writes=["ones1"])
        P.op("dve", lambda e: e.memset(h_pool[:, :, :, :], 0.0), writes=["h_pool0", "h_pool1"])
        P.op("dve", lambda e: e.memset(h_rnn[:, :, :, :], 0.0), writes=["h_rnn0", "h_rnn1"])
        P.op("dve", lambda e: e.memset(h_h[:, :, :], 0.0), writes=["h_h0", "h_h1"])
        P.op("dve", lambda e: e.memset(h_ffn[:, :, :, :], 0.0), writes=["h_ffn0", "h_ffn1"])

        plist = []
        for l in range(2):
            plist += [
                (("g_mix", l), g_mix[l].rearrange("(c p) -> c p", p=128), 8),
                (("g_ffn", l), g_ffn[l].rearrange("(c p) -> c p", p=128), 8),
                (("b_gate", l), b_gate[l].rearrange("(c p) -> c p", p=128), 24),
                (("pool_scale", l), pool_scale[l].rearrange("(c p) -> c p", p=128), 4),
                (("rnn_conv_w", l), rnn_conv_w[l].rearrange("j (c p) -> (j c) p", p=128), 32),
                (("rnn_conv_b", l), rnn_conv_b[l].rearrange("(c p) -> c p", p=128), 8),
                (("rnn_ba", l), rnn_ba[l].rearrange("(c p) -> c p", p=128), 8),
                (("rnn_bx", l), rnn_bx[l].rearrange("(c p) -> c p", p=128), 8),
                (("rnn_lambda", l), rnn_lambda[l].rearrange("(c p) -> c p", p=128), 8),
                (("g_mem", l), g_mem[l].rearrange("(c p) -> c p", p=128), 8),
                (("ffn_conv_w", l), ffn_conv_w[l].rearrange("j (c p) -> (j c) p", p=128), 72),
                (("ffn_conv_b", l), ffn_conv_b[l].rearrange("(c p) -> c p", p=128), 24),
            ]
        plist.append((("g_final", 0), g_final.rearrange("(c p) -> c p", p=128), 8))
        pcol = {}
        col = 0
        segs = []
        for key, ap, C in plist:
            pcol[key] = col
            done = 0
            while done < C:
                ti, r0 = divmod(col + done, 128)
                n = min(C - done, 128 - r0)
                segs.append((ti, r0, n, ap[done:done + n, :]))
                done += n
            col += C
        assert col <= NPV
        nstage = (col + 127) // 128
        for (ti, r0, n, src) in segs:
            P.dma("sp", xr[r0:r0 + n, 0, ti * 128:(ti + 1) * 128], src, writes=[("pstage", ti)])
        for ti in range(nstage):
            R = min(128, col - ti * 128)
            bi = bank()
            transposes(bi, [(banks[bi][:, 0:R], xr[0:R, 0, ti * 128:(ti + 1) * 128], ident[0:R, 0:R])],
                       reads=[("pstage", ti), "cst"])
            vcopy(pv[:, ti * 128:ti * 128 + R], banks[bi][:, 0:R], [BK(bi)], ["pv"])
        P.op("dve", lambda e: e.memset(ltmp[:, :], 0.0), reads=[("pstage", t) for t in range(nstage)] + ["pv"],
             writes=["xr", "ltmp"])

        def PVc(name, l, c, n=1):
            o = pcol[(name, l)] + c
            return pv[:, o:o + n]

        for l in range(2):
            act(ltmp[:, :], PVc("rnn_lambda", l, 0, 8), AF.Exp, ["pv", "ltmp"], ["ltmp"], scale=-1.0)
            act(ltmp[:, :], ltmp[:, :], AF.Ln, ["ltmp", "cst"], ["ltmp"], bias=c_one)
            P.op("act", lambda e, l=l: e.activation(out=clam[:, l, :], in_=ltmp[:, :], func=AF.Copy, scale=-8.0),
                 reads=["ltmp"], writes=["clam"])
            P.op("act", lambda e, l=l: e.activation(out=clam2[:, l, :], in_=ltmp[:, :], func=AF.Copy, scale=-16.0),
                 reads=["ltmp"], writes=["clam"])

        for l in range(2):
            P.dma("pool", poolw[:, l, :, :], pool_w[l].rearrange("g c d -> c g d"), writes=[("poolw", l)])
            P.dma("pool", rwa[:, l, :, :], rnn_wa[l].rearrange("g c d -> c g d"), writes=[("rwa", l)])
            P.dma("pool", rwx[:, l, :, :], rnn_wx[l].rearrange("g c d -> c g d"), writes=[("rwx", l)])

        def conv_block(l, name):
            o, c = boff[name]
            dst = wsc[l][:, o:o + c]
            res = ("wsc", l, name)
            parts = []
            if name.startswith("win"):
                i = int(name[3:])
                parts.append((dst.rearrange("p (k f) -> p k f", k=8),
                              w_in[l][:, i * 512:(i + 1) * 512].rearrange("(k p) f -> p k f", p=128)))
            elif name == "wk":
                parts.append((dst.rearrange("p (k f) -> p k f", k=8), w_k[l].rearrange("(k p) f -> p k f", p=128)))
            elif name == "wv":
                parts.append((dst.rearrange("p (k f) -> p k f", k=8), w_v[l].rearrange("(k p) f -> p k f", p=128)))
            elif name.startswith("gg"):
                f = int(name[2:])
                d3 = dst.rearrange("p (k c) -> p k c", c=128)
                for br in range(3):
                    parts.append((d3[:, br * 8:(br + 1) * 8, :],
                                  w_gate[l][:, br * 1024 + f * 128: br * 1024 + (f + 1) * 128].rearrange("(k p) c -> p k c", p=128)))
            elif name.startswith("gr"):
                f = int(name[2:])
                d3 = dst.rearrange("p (k c) -> p k c", c=128)
                parts.append((d3[:, 0:4, :], w_br_pool[l][:, f * 128:(f + 1) * 128].rearrange("(k p) c -> p k c", p=128)))
                parts.append((d3[:, 4:12, :], w_br_rnn[l][:, f * 128:(f + 1) * 128].rearrange("(k p) c -> p k c", p=128)))
                parts.append((d3[:, 12:16, :], w_br_attn[l][:, f * 128:(f + 1) * 128].rearrange("(k p) c -> p k c", p=128)))
            elif name.startswith("wout"):
                i = int(name[4:])
                parts.append((dst.rearrange("p (k f) -> p k f", k=8),
                              w_out[l][:, i * 512:(i + 1) * 512].rearrange("(k p) f -> p k f", p=128)))
            elif name.startswith("wup"):
                b = int(name[3:])
                d4 = dst.rearrange("p (g k f) -> p g k f", g=2, k=8)
                for gv in range(2):
                    parts.append((d4[:, gv, :, :],
                                  w_up[l][:, gv * DFF + b * 256: gv * DFF + (b + 1) * 256].rearrange("(k p) f -> p k f", p=128)))
            elif name.startswith("wdn"):
                f = int(name[3:])
                parts.append((dst.rearrange("p (k c) -> p k c", c=128),
                              w_down[l][:, f * 128:(f + 1) * 128].rearrange("(k p) c -> p k c", p=128)))
            for pi, (d_, s_) in enumerate(parts):
                P.dma("pool", d_, s_, writes=[res + (pi,)])
            return [res + (pi,) for pi in range(len(parts))]

        wsc_parts = {}
        if stage < 1:
            conv_order_skip = True
        else:
            conv_order_skip = False
        conv_order = [(0, "wk"), (0, "wv"), (1, "wk"), (1, "wv")]
        for l in range(2):
            for n, _ in blocks:
                if n not in ("wk", "wv"):
                    conv_order.append((l, n))
        for (l, n) in conv_order:
            wsc_parts[(l, n)] = [] if (conv_order_skip or (conv_only is not None and not n.startswith(conv_only))) else conv_block(l, n)

        tiles = [("p", i) for i in range(NPT)] + [("s", 0)]
        seq = [(0, "wk"), (0, "wv"), (1, "wk"), (1, "wv")]
        for tl in tiles:
            for l in range(2):
                for n, _ in blocks:
                    if n not in ("wk", "wv"):
                        seq.append((l, n))
        wst = {"loaded": 0, "pos": 0}

        def w_issue(k):
            l, n = seq[k]
            o, c = boff[n]
            s = k % NSLOT
            P.dma("sp", ring[s][:, 0:c], wsc[l][:, o:o + c], reads=wsc_parts[(l, n)], writes=[("slot", s)])

        def w_get(name_expected, l_expected):
            k = wst["pos"]
            assert seq[k] == (l_expected, name_expected), (seq[k], l_expected, name_expected)
            while wst["loaded"] <= min(k + NSLOT - 1, len(seq) - 1) and wst["loaded"] < k + NSLOT:
                if wst["loaded"] >= len(seq):
                    break
                w_issue(wst["loaded"])
                wst["loaded"] += 1
            s = k % NSLOT
            return ring[s], ("slot", s)

        def w_done():
            wst["pos"] += 1
            k = wst["pos"]
            nxt = k - 1 + NSLOT
            if nxt < len(seq) and wst["loaded"] == nxt:
                w_issue(nxt)
                wst["loaded"] += 1

        def rmsnorm(src, src_res, N, gname, l, out_fn, out_res):
            bi = bank()
            for c in range(8):
                q = c % 2
                act(xsq[:, q, 0:N], src(c), AF.Square, [src_res(c), ("xsq", q)], [("xsq", q)])
                P.op("pe", lambda e, c=c, q=q, bi=bi: e.matmul(banks[bi][:, 0:N], onesD[:, :], xsq[:, q, 0:N],
                                                              start=(c == 0), stop=(c == 7)),
                     reads=[("xsq", q), "onesD"], writes=[BK(bi)])
            act(lnb[:, 0:N], banks[bi][:, 0:N], AF.Ln, [BK(bi), "cst"], ["lnb"], bias=c_eps)
            act(rstd[:, 0:N], lnb[:, 0:N], AF.Exp, ["lnb"], ["rstd"], scale=-0.5)
            for c in range(8):
                vstt(out_fn(c), src(c), PVc(gname, l, c), rstd[:, 0:N], ALU.mult, ALU.mult,
                     [src_res(c), "pv", "rstd"], [out_res(c)])

        class RowOut:
            def __init__(self, R, nchunks, dst_fn):
                self.R, self.n, self.dst_fn = R, nchunks, dst_fn
                self.bi = None
                self.k = 0
                self.flip = 0

            def add(self, c, src_ap, src_reads):
                P._noyield = getattr(P, "_noyield", 0) + 1
                try:
                    self._add(c, src_ap, src_reads)
                finally:
                    P._noyield -= 1

            def _add(self, c, src_ap, src_reads):
                R = self.R
                if self.bi is None:
                    self.bi = bank(hold=True)
                bi = self.bi
                j = c % 4
                transposes(bi, [(banks[bi][0:R, j * 128:(j + 1) * 128], src_ap, ident)], reads=list(src_reads) + ["cst"])
                self.k += 1
                if j == 3 or c == self.n - 1:
                    grp = c // 4
                    w = (j + 1) * 128
                    ob = orow[self.flip]
                    ores = ("orow", self.flip)
                    self.flip ^= 1
                    anycopy(ob[0:R, 0:w], banks[bi][0:R, 0:w], [BK(bi)], [ores])
                    for (dap, r0, nr) in self.dst_fn(grp, w):
                        P.dma("sp", dap, ob[r0:r0 + nr, 0:w], reads=[ores], writes=[])
                    unhold(bi)
                    self.bi = None

        def load_fm(rows_ap, rows_res, R, nch, dst_fn, dst_res):
            c = 0
            while c < nch:
                g = min(4, nch - c)
                bi = bank()
                transposes(bi, [(banks[bi][:, j * 128:j * 128 + R], rows_ap[0:R, (c + j) * 128:(c + j + 1) * 128],
                                 ident[0:R, 0:R]) for j in range(g)], reads=[rows_res, "cst"])
                cp_toggle["i"] ^= 1
                cpf = acopy if cp_toggle["i"] else vcopy
                for j in range(g):
                    cpf(dst_fn(c + j), banks[bi][:, j * 128:j * 128 + R], [BK(bi)], [dst_res(c + j)])
                c += g

        if stage < 2:
            P.finalize()
            return nc
        P.dma("sp", xr[:, 0:2, :], memp.rearrange("(a p) d -> p a d", p=128), reads=[], writes=["xr"])
        for a in range(2):
            for half in range(2):
                bi = bank()
                transposes(bi, [(banks[bi][:, k * 128:(k + 1) * 128], xr[:, a, (half * 4 + k) * 128:(half * 4 + k + 1) * 128], ident)
                                for k in range(4)], reads=["xr", "cst"])
                anycopy(hT[:, half * 4:(half + 1) * 4, a * 128:(a + 1) * 128],
                        banks[bi][:, :].rearrange("p (k t) -> p k t", k=4), [BK(bi)],
                        [("hT", c) for c in range(half * 4, half * 4 + 4)])
        for l in range(2):
            rmsnorm(lambda c: hT[:, c, 0:NMEM], lambda c: ("hT", c), NMEM, "g_mem", l,
                    lambda c: xn[:, c, 0:NMEM], lambda c: ("xn", c))
            wk_t, wk_r = w_get("wk", l)
            wk3 = wk_t[:, 0:4096].rearrange("p (k f) -> p k f", k=8)
            xnr = [("xn", c) for c in range(8)]
            for h in range(4):
                bi = bank()
                mm_group(bi, banks[bi][:, 0:NMEM], [(wk3[:, k, h * 128:(h + 1) * 128], xn[:, k, 0:NMEM]) for k in range(8)],
                         reads=[wk_r] + xnr)
                anycopy(kT[:, l, h, :], banks[bi][:, 0:NMEM], [BK(bi)], [("kT", l)])
            for mc in range(2):
                bi = bank()
                mm_group(bi, banks[bi][:, :], [(xn[:, k, mc * 128:(mc + 1) * 128], wk3[:, k, :]) for k in range(8)],
                         reads=[wk_r] + xnr)
                acopy(kvrow[mc][:, :], banks[bi][:, :], [BK(bi)], [("orow", mc)])
                P.dma("sp", p_mk[l][mc * 128:(mc + 1) * 128, :], kvrow[mc][:, :], reads=[("orow", mc)])
            w_done()
            wv_t, wv_r = w_get("wv", l)
            wv3 = wv_t[:, 0:4096].rearrange("p (k f) -> p k f", k=8)
            for mc in range(2):
                bi = bank()
                mm_group(bi, banks[bi][:, :], [(xn[:, k, mc * 128:(mc + 1) * 128], wv3[:, k, :]) for k in range(8)],
                         reads=[wv_r] + xnr)
                acopy(kvrow[mc][:, :], banks[bi][:, :], [BK(bi)], [("orow", mc)])
                acopy(vtm[:, l, mc, :], banks[bi][:, :], [BK(bi)], [("vtm", l)])
                P.dma("sp", p_mv[l][mc * 128:(mc + 1) * 128, :], kvrow[mc][:, :], reads=[("orow", mc)])
            w_done()

        KEYS_A = ([("upx", g) for g in range(4)] + [("urx", n) for n in range(8)] + [("qb", h) for h in range(4)]
                  + ["sA", "sB", ("dpl", 0), ("dpl", 1)] + [("xc", q) for q in range(NRB)]
                  + [("actb", j) for j in range(24)] + [(nm, q) for nm in ("fext", "fgc", "fge") for q in range(NFB)])
        KEYS_M = ([("merged", f) for f in range(8)] + [("gs", i) for i in range(NGS)] + [("Kst", 0), ("Kst", 1), ("Vst", 0),
                  ("kTb", 0, 0), ("kTb", 0, 1), ("PTs", 0), ("PTs", 1)])
        KEYS_P = ([("PT", i, mc) for i in range(2) for mc in range(2)] + [("rs", 0), ("rs", 1)]
                  + [("hs_pool", c) for c in range(4)] + [("hs_rnn", c) for c in range(8)] + [("hs_h", c) for c in range(8)])

        def barrier(keys):
            P.op("dve", lambda e: e.memset(ltmp[:, :], 0.0), reads=[], writes=list(keys) + ["ltmp"])

        def tile_layer(kind, ti, l):
            samp = (kind == "s")
            N = 128 if samp else NT
            inner = 16 if samp else 1
            T = N // inner
            import os as _os2
            first = (not samp) and (ti == 0 or _os2.environ.get('FORCE_FIRST') == '1')
            last = (not samp) and ti == NPT - 1
            hTr = [("hT", c) for c in range(8)]
            xnr = [("xn", c) for c in range(8)]
            HP, HR, HF = 15 * inner, 3 * inner, 2 * inner

            barrier(KEYS_A + KEYS_M + KEYS_P)
            if samp:
                for j in range(15):
                    rr = (j % 8) * 16
                    if j % 8 == 0:
                        pass
                    P.dma("sp", xr[rr:rr + 16, 0 if j < 8 else 1, 0:512], st_pool[l][:, j, :], reads=["xr"], writes=[("xrs", j)])
                P.op("dve", lambda e: e.memset(ltmp[:, :], 0.0), reads=[("xrs", j) for j in range(15)] + ["xr"], writes=["xr", "ltmp"])
                load_fm(xr[:, 0, :], "xr", 128, 4, lambda c: hs_pool[:, c, 0:128], lambda c: ("hs_pool", c))
                load_fm(xr[:, 1, :], "xr", 112, 4, lambda c: hs_pool[:, c, 128:240], lambda c: ("hs_pool", c))
                P.dma("sp", s_pool[l][:, 0:7, :], st_pool[l][:, 8:15, :])
                P.op("dve", lambda e: e.memset(ltmp[:, :], 0.0), reads=["xr", ("hs_pool", 0), ("hs_pool", 1), ("hs_pool", 2), ("hs_pool", 3)],
                     writes=["xr", "ltmp"])
                for j in range(3):
                    P.dma("sp", xr[j * 16:(j + 1) * 16, 0, :], st_rconv[l][:, j, :], reads=["xr"], writes=[("xrs", j)])
                P.dma("sp", xr[0:16, 1, :], st_rh[l][:, :], reads=["xr"], writes=[("xrs", 3)])
                P.op("dve", lambda e: e.memset(ltmp[:, :], 0.0), reads=[("xrs", j) for j in range(4)] + ["xr"], writes=["xr", "ltmp"])
                load_fm(xr[:, 0, :], "xr", 48, 8, lambda c: hs_rnn[:, c, :], lambda c: ("hs_rnn", c))
                load_fm(xr[:, 1, :], "xr", 16, 8, lambda c: hs_h[:, c, :], lambda c: ("hs_h", c))
                P.op("dve", lambda e: e.memset(ltmp[:, :], 0.0), reads=["xr"] + [("hs_rnn", c) for c in range(8)] + [("hs_h", c) for c in range(8)],
                     writes=["xr", "ltmp"])
                for part in range(3):
                    for j in range(2):
                        P.dma("sp", xr[j * 16:(j + 1) * 16, 0, :], st_fconv[l][:, j, part * 1024:(part + 1) * 1024],
                              reads=["xr"], writes=[("xrs", j)])
                    P.op("dve", lambda e: e.memset(ltmp[:, :], 0.0), reads=[("xrs", 0), ("xrs", 1), "xr"], writes=["xr", "ltmp"])
                    load_fm(xr[:, 0, :], "xr", 32, 8, lambda c, part=part: hs_ffn[:, part * 8 + c, :],
                            lambda c, part=part: ("hs_ffn", part * 8 + c))
                    P.op("dve", lambda e: e.memset(ltmp[:, :], 0.0), reads=["xr"] + [("hs_ffn", part * 8 + c) for c in range(8)],
                         writes=["xr", "ltmp"])

            rmsnorm(lambda c: hT[:, c, 0:N], lambda c: ("hT", c), N, "g_mix", l,
                    lambda c: xn[:, c, 0:N], lambda c: ("xn", c))

            for zb in (3, 1, 2, 0):
                wt, wr = w_get("win%d" % zb, l)
                w3 = wt[:, 0:4096].rearrange("p (k f) -> p k f", k=8)
                for zi in range(4):
                    zc = zb * 4 + zi
                    bi = bank()
                    mm_group(bi, banks[bi][:, 0:N], [(w3[:, k, zi * 128:(zi + 1) * 128], xn[:, k, 0:N]) for k in range(8)],
                             reads=[wr] + xnr)
                    if zc < 4:
                        anycopy(upx[:, zc, HP:HP + N], banks[bi][:, 0:N], [BK(bi)], [("upx", zc)])
                    elif zc < 12:
                        anycopy(urx[:, zc - 4, HR:HR + N], banks[bi][:, 0:N], [BK(bi)], [("urx", zc - 4)])
                    else:
                        h = zc - 12
                        if samp:
                            o_ = qb[:, h, 0:N].rearrange("p (b t) -> p t b", t=ST_)
                            i_ = banks[bi][:, 0:N].rearrange("p (t b) -> p t b", b=SB_)
                        else:
                            o_ = qb[:, h, 0:N]
                            i_ = banks[bi][:, 0:N]
                        P.op("act", lambda e, o_=o_, i_=i_: e.activation(out=o_, in_=i_, func=AF.Copy, scale=128.0 ** -0.5),
                             reads=[BK(bi)], writes=[("qb", h)])
                w_done()

            for g in range(4):
                if samp:
                    vcopy(upx[:, g, 0:HP], hs_pool[:, g, :], [("hs_pool", g)], [("upx", g)])
                else:
                    vcopy(upx[:, g, 0:HP], h_pool[:, l, g, :], [("h_pool%d" % l)], [("upx", g)])
            for n in range(8):
                if samp:
                    vcopy(urx[:, n, 0:HR], hs_rnn[:, n, :], [("hs_rnn", n)], [("urx", n)])
                else:
                    vcopy(urx[:, n, 0:HR], h_rnn[:, l, n, :], [("h_rnn%d" % l)], [("urx", n)])

            def gen_attn():
                if not samp:
                    for h in range(4):
                        pt = PTb[h % 2]
                        ptr = ("PT", h % 2)
                        for mc in range(2):
                            bi = bank()
                            mm_group(bi, banks[bi][:, 0:N], [(kT[:, l, h, mc * 128:(mc + 1) * 128], qb[:, h, 0:N])],
                                     reads=[("kT", l), ("qb", h)])
                            act(pt[:, mc, 0:N], banks[bi][:, 0:N], AF.Exp, [BK(bi)], [ptr + (mc,)])
                        bo = bank()
                        mm_group(bo, banks[bo][:, 0:N], [(vtm[:, l, mc, h * 128:(h + 1) * 128], pt[:, mc, 0:N]) for mc in range(2)],
                                 reads=[("vtm", l), ptr + (0,), ptr + (1,)])
                        bd = bank()
                        mm_group(bd, banks[bd][:, 0:N], [(ones1[:, :], pt[:, mc, 0:N]) for mc in range(2)],
                                 reads=["ones1", ptr + (0,), ptr + (1,)])
                        rs = rsb[h % 2]
                        rsr = ("rs", h % 2)
                        act(rs[:, 0:N], banks[bd][:, 0:N], AF.Ln, [BK(bd)], [rsr])
                        act(rs[:, 0:N], rs[:, 0:N], AF.Exp, [rsr], [rsr], scale=-1.0)
                        vtt(yattn[:, h, 0:N], banks[bo][:, 0:N], rs[:, 0:N], ALU.mult, [BK(bo), rsr], [("yattn", h)])
                        yield
                else:
                    barrier(KEYS_M)
                    bO = bank(hold=True)
                    bD = bank(hold=True)
                    for b in range(SB_):
                        ks = Kst[b % 2]; vs = Vst[0]; kb = kTb[0]; ps = PTs[b % 2]
                        P.dma("sp", ks[:, :, :], ck[l][b].rearrange("(a p) f -> p a f", p=128), writes=[("Kst", b % 2)])
                        P.dma("pool", vs[:, :, :], cv[l][b].rearrange("(a p) f -> p a f", p=128), writes=[("Vst", 0)])
                        for hp in range(2):
                            bi = bank()
                            transposes(bi, [(banks[bi][:, (hh * 2 + mc) * 128:(hh * 2 + mc + 1) * 128],
                                             ks[:, mc, (hp * 2 + hh) * 128:(hp * 2 + hh + 1) * 128], ident)
                                            for hh in range(2) for mc in range(2)], reads=[("Kst", b % 2), "cst"])
                            anycopy(kb[:, hp * 2:hp * 2 + 2, :], banks[bi][:, :].rearrange("p (h m) -> p h m", h=2),
                                    [BK(bi)], [("kTb", 0, hp)])
                        bs = bank()
                        def fn_s(e, kb=kb, bs=bs, b=b):
                            inst = None
                            for mc in range(2):
                                for h in range(4):
                                    inst = e.matmul(banks[bs][:, (mc * 4 + h) * 8:(mc * 4 + h + 1) * 8],
                                                    kb[:, h, mc * 128:(mc + 1) * 128], qb[:, h, b * 8:(b + 1) * 8],
                                                    start=True, stop=True)
                            return inst
                        P.op("pe", fn_s, reads=[("kTb", 0, 0), ("kTb", 0, 1)] + [("qb", h) for h in range(4)], writes=[BK(bs)])
                        act(ps[:, 0:64], banks[bs][:, 0:64], AF.Exp, [BK(bs)], [("PTs", b % 2)])

                        def fn_o(e, vs=vs, ps=ps, b=b):
                            inst = None
                            for h in range(4):
                                for mc in range(2):
                                    inst = e.matmul(banks[bO][:, h * 128 + b * 8: h * 128 + (b + 1) * 8],
                                                    vs[:, mc, h * 128:(h + 1) * 128], ps[:, (mc * 4 + h) * 8:(mc * 4 + h + 1) * 8],
                                                    start=(mc == 0), stop=(mc == 1))
                            return inst
                        P.op("pe", fn_o, reads=[("Vst", 0), ("PTs", b % 2)], writes=[BK(bO)])

                        def fn_d(e, ps=ps, b=b):
                            inst = None
                            for mc in range(2):
                                inst = e.matmul(banks[bD][:, b * 32:(b + 1) * 32], ones1[:, :], ps[:, mc * 32:(mc + 1) * 32],
                                                start=(mc == 0), stop=(mc == 1))
                            return inst
                        P.op("pe", fn_d, reads=["ones1", ("PTs", b % 2)], writes=[BK(bD)])
                    rs = rsb[0]
                    act(rs[:, 0:512], banks[bD][:, :], AF.Ln, [BK(bD)], [("rs", 0)])
                    act(rs[:, 0:512], rs[:, 0:512], AF.Exp, [("rs", 0)], [("rs", 0)], scale=-1.0)
                    rs4 = rs[:, 0:512].rearrange("p (b h t) -> p h b t", h=4, t=ST_)
                    for h in range(4):
                        vtt(yattn[:, h, 0:N].rearrange("p (t b) -> p b t", b=SB_),
                            banks[bO][:, h * 128:(h + 1) * 128].rearrange("p (b t) -> p b t", t=ST_),
                            rs4[:, h, :, :], ALU.mult, [BK(bO), ("rs", 0)], [("yattn", h)])
                    unhold(bO); unhold(bD)
                    barrier(KEYS_M)

                yield
            def gen_pool():
                if samp:
                    ro_pool = RowOut(128, 4, lambda grp, w: [(s_pool[l][:, 7 + t, :], t * 16, 16) for t in range(ST_)])
                elif last:
                    ro_pool = RowOut(15, 4, lambda grp, w: [(p_pool[l][:, :], 0, 15)])
                else:
                    ro_pool = None
                for g in range(4):
                    wdw = 2 ** (g + 1)
                    e_ = upx[:, g, :]
                    er = ("upx", g)
                    L = HP + N
                    if g == 0:
                        vtt(sA[:, HP:L], e_[:, HP:L], e_[:, HP - inner:L - inner], ALU.add, [er, "sA"], ["sA"])
                        win = sA
                        winr = "sA"
                    else:
                        lo = HP - (wdw - 2) * inner
                        vtt(sA[:, lo:L], e_[:, lo:L], e_[:, lo - inner:L - inner], ALU.add, [er, "sA"], ["sA"])
                        cur, curr, oth, othr = sA, "sA", sB, "sB"
                        step = 2
                        while step < wdw:
                            lo = lo + step * inner
                            vtt(oth[:, lo:L], cur[:, lo:L], cur[:, lo - step * inner:L - step * inner], ALU.add, [curr, othr], [othr])
                            cur, curr, oth, othr = oth, othr, cur, curr
                            step *= 2
                        win, winr = cur, curr
                    dq = g % 2
                    vstt(dpl[:, dq, 0:N], win[:, HP:L], 1.0 / wdw, e_[:, HP:L], ALU.mult, ALU.subtract,
                         [winr, er, ("dpl", dq)], [("dpl", dq)])
                    if first:
                        k = wdw - 1
                        vtt(mtmp[:, 0:k], win[:, HP:HP + k], invcnt[:, 0:k], ALU.mult, [winr, "cst", "mtmp"], ["mtmp"])
                        vtt(dpl[:, dq, 0:k], mtmp[:, 0:k], e_[:, HP:HP + k], ALU.subtract, ["mtmp", er, ("dpl", dq)], [("dpl", dq)])
                    bi = bank()
                    mm_group(bi, banks[bi][:, 0:N], [(poolw[:, l, g, :], dpl[:, dq, 0:N])], reads=[("poolw", l), ("dpl", dq)])
                    vts(ypool[:, g, 0:N], banks[bi][:, 0:N], PVc("pool_scale", l, g), None, ALU.mult, None,
                        [BK(bi), "pv"], [("ypool", g)])
                    if ro_pool is not None:
                        if samp:
                            ro_pool.add(g, e_[:, HP:HP + 128], [er])
                        else:
                            ro_pool.add(g, e_[:, L - 15:L], [er])
                    if not samp and not last:
                        vcopy(h_pool[:, l, g, :], e_[:, L - 15:L], [er], ["h_pool%d" % l])

                    yield
                yield
            def mk_ro_r():
                if samp:
                    ro_rc = RowOut(48, 8, lambda grp, w: [(s_rconv[l][:, t, grp * 512:grp * 512 + w], t * 16, 16) for t in range(3)])
                    ro_rh = RowOut(16, 8, lambda grp, w: [(s_rh[l][:, grp * 512:grp * 512 + w], 0, 16)])
                elif last:
                    ro_rc = RowOut(3, 8, lambda grp, w: [(p_rconv[l][:, grp * 512:grp * 512 + w], 0, 3)])
                    ro_rh = RowOut(1, 8, lambda grp, w: [(p_rh[l][:, grp * 512:grp * 512 + w], 0, 1)])
                else:
                    ro_rc = ro_rh = None
                return ro_rc, ro_rh
            ro_rc, ro_rh = mk_ro_r()

            def gen_rglru(ns=range(8)):
                cw0 = pcol[("rnn_conv_w", l)]
                for n in ns:
                    q = n % NRB
                    e_ = urx[:, n, :]
                    er = ("urx", n)
                    xc, xb, rr, tt, ii, hh = r_xc[q], r_xb[q], r_r[q], r_t[q], r_i[q], r_h[q]
                    R = lambda s, q=q: (s, q)
                    if pool_ok["v"]:
                        P.op("pool", lambda e, xc=xc, e_=e_, n=n: e.tensor_scalar(
                            out=xc[:, 0:N], in0=e_[:, 0:N], scalar1=pv[:, cw0 + n:cw0 + n + 1], scalar2=PVc("rnn_conv_b", l, n),
                            op0=ALU.mult, op1=ALU.add), reads=[er, "pv", R("xc")], writes=[R("xc")])
                    else:
                        act(xc[:, 0:N], e_[:, 0:N], AF.Identity, [er, "pv", R("xc")], [R("xc")],
                            bias=PVc("rnn_conv_b", l, n), scale=pv[:, cw0 + n:cw0 + n + 1])
                    for j in range(1, 4):
                        vstt(xc[:, 0:N], e_[:, j * inner:j * inner + N], pv[:, cw0 + j * 8 + n:cw0 + j * 8 + n + 1], xc[:, 0:N],
                             ALU.mult, ALU.add, [er, "pv", R("xc")], [R("xc")])
                    pcopy(xb[:, 0:N], xc[:, 0:N], [R("xc"), R("xb")], [R("xb")], alt="act")
                    br_ = bank()
                    mm_group(br_, banks[br_][:, 0:N], [(rwa[:, l, n, :], xb[:, 0:N])], reads=[("rwa", l), R("xb")])
                    bx_ = bank()
                    mm_group(bx_, banks[bx_][:, 0:N], [(rwx[:, l, n, :], xb[:, 0:N])], reads=[("rwx", l), R("xb")])
                    act(rr[:, 0:N], banks[br_][:, 0:N], AF.Sigmoid, [BK(br_), "pv", R("r")], [R("r")], bias=PVc("rnn_ba", l, n))
                    act(ii[:, 0:N], banks[bx_][:, 0:N], AF.Sigmoid, [BK(bx_), "pv", R("i")], [R("i")], bias=PVc("rnn_bx", l, n))
                    act(tt[:, 0:N], rr[:, 0:N], AF.Exp, [R("r"), "clam", R("t")], [R("t")], scale=clam2[:, l, n:n + 1])
                    act(rr[:, 0:N], rr[:, 0:N], AF.Exp, [R("r"), "clam"], [R("r")], scale=clam[:, l, n:n + 1])
                    act(tt[:, 0:N], tt[:, 0:N], AF.Ln, [R("t"), "cst"], [R("t")], bias=c_one, scale=-1.0)
                    act(tt[:, 0:N], tt[:, 0:N], AF.Exp, [R("t")], [R("t")], scale=0.5)
                    vtt(ii[:, 0:N], ii[:, 0:N], xc[:, 0:N], ALU.mult, [R("i"), R("xc")], [R("i")])
                    vtt(ii[:, 0:N], ii[:, 0:N], tt[:, 0:N], ALU.mult, [R("i"), R("t")], [R("i")])
                    if not samp:
                        P.op("dve", lambda e, hh=hh, rr=rr, ii=ii, n=n: e.tensor_tensor_scan(
                            out=hh[:, 0:N], data0=rr[:, 0:N], data1=ii[:, 0:N], initial=h_h[:, l, n:n + 1],
                            op0=ALU.mult, op1=ALU.add), reads=[R("r"), R("i"), "h_h%d" % l, R("h")], writes=[R("h")])
                        vcopy(h_h[:, l, n:n + 1], hh[:, N - 1:N], [R("h")], ["h_h%d" % l])
                    else:
                        for t in range(ST_):
                            prev = hs_h[:, n, :] if t == 0 else hh[:, (t - 1) * 16:t * 16]
                            vtt(hh[:, t * 16:(t + 1) * 16], rr[:, t * 16:(t + 1) * 16], prev, ALU.mult,
                                [R("r"), R("h"), ("hs_h", n)], [R("h")])
                            vtt(hh[:, t * 16:(t + 1) * 16], hh[:, t * 16:(t + 1) * 16], ii[:, t * 16:(t + 1) * 16], ALU.add,
                                [R("i"), R("h")], [R("h")])
                    pcopy(yrnn[:, n, 0:N], hh[:, 0:N], [R("h")], [("yrnn", n)], alt="act")
                    if ro_rc is not None:
                        if samp:
                            ro_rc.add(n, e_[:, HR + 5 * 16:HR + 8 * 16], [er])
                            ro_rh.add(n, hh[:, 7 * 16:8 * 16], [R("h")])
                        else:
                            ro_rc.add(n, e_[:, HR + N - 3:HR + N], [er])
                            ro_rh.add(n, hh[:, N - 1:N], [R("h")])
                    if not samp and not last:
                        vcopy(h_rnn[:, l, n, :], e_[:, HR + N - 3:HR + N], [er], ["h_rnn%d" % l])

                    yield
                yield
            def drain(g_):
                for _ in g_:
                    pass
            if (samp and not INTERLEAVE_SAMPLE) or not INTERLEAVE:
                drain(gen_attn()); drain(gen_pool()); drain(gen_rglru())
            else:
                P.interleave([lambda: drain(gen_rglru(range(0, 8, 2))), lambda: drain(gen_rglru(range(1, 8, 2))),
                              lambda: drain(gen_attn()), lambda: drain(gen_pool())])

            bcol = pcol[("b_gate", l)]
            yp_r = [("ypool", g) for g in range(4)]
            yr_r = [("yrnn", n) for n in range(8)]
            ya_r = [("yattn", h) for h in range(4)]
            gi = 0
            for f in range(8):
                wtg, wrg = w_get("gg%d" % f, l)
                wg3 = wtg[:, 0:3072].rearrange("p (k c) -> p k c", c=128)
                gates_ps = {}
                for br in (0, 2, 1):
                    bg = bank()
                    mm_group(bg, banks[bg][:, 0:N], [(wg3[:, br * 8 + k, :], xn[:, k, 0:N]) for k in range(8)], reads=[wrg] + xnr)
                    gs = gsb[gi % NGS]
                    gr = ("gs", gi % NGS)
                    gi += 1
                    act(gs[:, 0:N], banks[bg][:, 0:N], AF.Sigmoid, [BK(bg), "pv", gr], [gr],
                        bias=pv[:, bcol + br * 8 + f: bcol + br * 8 + f + 1])
                    gates_ps[br] = (gs, gr)
                w_done()
                wt, wr = w_get("gr%d" % f, l)
                w3 = wt[:, 0:2048].rearrange("p (k c) -> p k c", c=128)
                first_term = True
                for (br, k0, nk, ysrc, yres) in ((0, 0, 4, ypool, yp_r), (2, 12, 4, yattn, ya_r), (1, 4, 8, yrnn, yr_r)):
                    gs, gr = gates_ps[br]
                    bp = bank()
                    mm_group(bp, banks[bp][:, 0:N], [(w3[:, k0 + k, :], ysrc[:, k, 0:N]) for k in range(nk)], reads=[wr] + yres)
                    if first_term:
                        vtt(macc[:, 0:N], gs[:, 0:N], banks[bp][:, 0:N], ALU.mult, [gr, BK(bp), "macc"], ["macc"])
                        first_term = False
                    elif br == 2:
                        vtt(mtmp[:, 0:N], gs[:, 0:N], banks[bp][:, 0:N], ALU.mult, [gr, BK(bp), "mtmp"], ["mtmp"])
                        vtt(macc[:, 0:N], macc[:, 0:N], mtmp[:, 0:N], ALU.add, ["macc", "mtmp"], ["macc"])
                    else:
                        vtt(mtmp[:, 0:N], gs[:, 0:N], banks[bp][:, 0:N], ALU.mult, [gr, BK(bp), "mtmp"], ["mtmp"])
                        vtt(merged[:, f, 0:N], macc[:, 0:N], mtmp[:, 0:N], ALU.add, ["macc", "mtmp"], [("merged", f)])
                w_done()

            mr = [("merged", f) for f in range(8)]
            for ob in range(2):
                wt, wr = w_get("wout%d" % ob, l)
                w3 = wt[:, 0:4096].rearrange("p (k f) -> p k f", k=8)
                for fi in range(4):
                    f = ob * 4 + fi
                    bi = bank()
                    mm_group(bi, banks[bi][:, 0:N], [(w3[:, k, fi * 128:(fi + 1) * 128], merged[:, k, 0:N]) for k in range(8)],
                             reads=[wr] + mr)
                    vtt(hT[:, f, 0:N], hT[:, f, 0:N], banks[bi][:, 0:N], ALU.add, [("hT", f), BK(bi)], [("hT", f)])
                w_done()

            barrier(KEYS_A)
            rmsnorm(lambda c: hT[:, c, 0:N], lambda c: ("hT", c), N, "g_ffn", l,
                    lambda c: xn[:, c, 0:N], lambda c: ("xn", c))
            if samp:
                ro_fc = RowOut(32, 24, lambda grp, w: [(s_fconv[l][:, t, grp * 512:grp * 512 + w], t * 16, 16) for t in range(2)])
            elif last:
                ro_fc = RowOut(2, 24, lambda grp, w: [(p_fconv[l][:, grp * 512:grp * 512 + w], 0, 2)])
            else:
                ro_fc = None
            fw0 = pcol[("ffn_conv_w", l)]
            for ub in range(12):
                wt, wr = w_get("wup%d" % ub, l)
                w4 = wt[:, 0:4096].rearrange("p (g k f) -> p g k f", g=2, k=8)
                for jj in range(2):
                    j = ub * 2 + jj
                    q = j % NFB
                    ex, gc, ge = f_ext[q], f_gc[q], f_ge[q]
                    R = lambda s, q=q: (s, q)
                    bg = bank()
                    mm_group(bg, banks[bg][:, 0:N], [(w4[:, 0, k, jj * 128:(jj + 1) * 128], xn[:, k, 0:N]) for k in range(8)],
                             reads=[wr] + xnr)
                    bv = bank()
                    mm_group(bv, banks[bv][:, 0:N], [(w4[:, 1, k, jj * 128:(jj + 1) * 128], xn[:, k, 0:N]) for k in range(8)],
                             reads=[wr] + xnr)
                    if samp:
                        pcopy(ex[:, 0:HF], hs_ffn[:, j, :], [("hs_ffn", j), R("fext")], [R("fext")])
                    else:
                        pcopy(ex[:, 0:HF], h_ffn[:, l, j, :], ["h_ffn%d" % l, R("fext")], [R("fext")])
                    acopy(ex[:, HF:HF + N], banks[bg][:, 0:N], [BK(bg), R("fext")], [R("fext")])
                    act(gc[:, 0:N], banks[bg][:, 0:N], AF.Identity, [BK(bg), "pv", R("fgc")], [R("fgc")],
                        bias=PVc("ffn_conv_b", l, j), scale=pv[:, fw0 + 2 * 24 + j:fw0 + 2 * 24 + j + 1])
                    for tap in range(0, 2):
                        vstt(gc[:, 0:N], ex[:, tap * inner:tap * inner + N], pv[:, fw0 + tap * 24 + j:fw0 + tap * 24 + j + 1],
                             gc[:, 0:N], ALU.mult, ALU.add, [R("fext"), "pv", R("fgc")], [R("fgc")])
                    act(ge[:, 0:N], gc[:, 0:N], AF.Gelu_apprx_tanh, [R("fgc"), R("fge")], [R("fge")])
                    vtt(actb[:, j, 0:N], ge[:, 0:N], banks[bv][:, 0:N], ALU.mult, [R("fge"), BK(bv)], [("actb", j)])
                    if ro_fc is not None:
                        ro_fc.add(j, ex[:, HF + N - 2 * inner:HF + N], [R("fext")])
                    if not samp and not last:
                        pcopy(h_ffn[:, l, j, :], ex[:, HF + N - 2:HF + N], [R("fext")], ["h_ffn%d" % l])
                w_done()
            ar = [("actb", j) for j in range(24)]
            for f in range(8):
                wt, wr = w_get("wdn%d" % f, l)
                w3 = wt[:, 0:3072].rearrange("p (k c) -> p k c", c=128)
                bi = bank()
                mm_group(bi, banks[bi][:, 0:N], [(w3[:, k, :], actb[:, k, 0:N]) for k in range(24)], reads=[wr] + ar)
                vtt(hT[:, f, 0:N], hT[:, f, 0:N], banks[bi][:, 0:N], ALU.add, [("hT", f), BK(bi)], [("hT", f)])
                w_done()

        if stage < 3:
            P.finalize()
            return nc
        import os as _os
        _sel = _os.environ.get('TILESEL')
        _tl = [tiles[int(x)] for x in _sel.split(',')] if _sel else (tiles[:ntiles] + (tiles[-1:] if ntiles < 0 else []))
        for (kind, ti) in _tl:
            samp = kind == "s"
            N = 128 if samp else NT
            if samp:
                for t in range(ST_):
                    P.dma("sp", xr[t * 16:(t + 1) * 16, 0, :], xs[:, t, :], reads=["xr"], writes=[("xrs", t)])
                P.op("dve", lambda e: e.memset(ltmp[:, :], 0.0), reads=[("xrs", t) for t in range(ST_)] + ["xr"], writes=["xr", "ltmp"])
            for a in range(N // 128):
                q = a % 2
                if samp:
                    xres = "xr"
                else:
                    xres = ("xrg", q)
                    r0 = ti * NT + a * 128
                    P.dma("sp", xr[:, q, :], xp[r0:r0 + 128, :], reads=["xr"], writes=[xres])
                for half in range(2):
                    bi = bank()
                    transposes(bi, [(banks[bi][:, k * 128:(k + 1) * 128], xr[:, q, (half * 4 + k) * 128:(half * 4 + k + 1) * 128], ident)
                                    for k in range(4)], reads=[xres, "cst"])
                    anycopy(hT[:, half * 4:(half + 1) * 4, a * 128:(a + 1) * 128],
                            banks[bi][:, :].rearrange("p (k t) -> p k t", k=4), [BK(bi)],
                            [("hT", c) for c in range(half * 4, half * 4 + 4)])
            P.op("dve", lambda e: e.memset(ltmp[:, :], 0.0), reads=[("hT", c) for c in range(8)],
                 writes=["xr", ("xrg", 0), ("xrg", 1), "ltmp"])
            pool_ok["v"] = not (kind == "p" and ti == 0)
            for l in range(2):
                tile_layer(kind, ti, l)
            rmsnorm(lambda c: hT[:, c, 0:N], lambda c: ("hT", c), N, "g_final", 0,
                    lambda c: hT[:, c, 0:N], lambda c: ("hT", c))
            for a in range(N // 128):
                for half in range(2):
                    bi = bank()
                    transposes(bi, [(banks[bi][:, k * 128:(k + 1) * 128], hT[:, half * 4 + k, a * 128:(a + 1) * 128], ident)
                                    for k in range(4)], reads=[("hT", half * 4 + k) for k in range(4)] + ["cst"])
                    anycopy(yrow[:, half * 512:(half + 1) * 512], banks[bi][:, :], [BK(bi)], [("yrow", half)])
                if samp:
                    for t in range(ST_):
                        P.dma("sp", y_s[:, t, :], yrow[t * 16:(t + 1) * 16, :], reads=[("yrow", 0), ("yrow", 1)])
                else:
                    r0 = ti * NT + a * 128
                    P.dma("sp", y_p[r0:r0 + 128, :], yrow[:, :], reads=[("yrow", 0), ("yrow", 1)])
        assert stage < 99 or ntiles < 99 or wst["pos"] == len(seq), (wst["pos"], len(seq))
        print('OPCOUNTS', P.cnt, P.dcnt, flush=True)
        P.finalize()
    return nc


_CACHE = {}


def _consts():
    c = np.zeros((128, 160), np.float32)
    c[:, 0:128] = np.eye(128, dtype=np.float32)
    c[:, 128:143] = (1.0 / np.arange(1, 16, dtype=np.float32))[None, :]
    c[:, 143] = 1.0
    c[:, 144] = EPS
    return c


def kernel(x_prompt, x_sample, mem_prompt, cache_mem_k, cache_mem_v, state_pool, state_rnn_conv,
           state_rnn_h, state_ffn_conv, g_mix, w_in, w_gate, b_gate, pool_w, pool_scale,
           rnn_conv_w, rnn_conv_b, rnn_wa, rnn_ba, rnn_wx, rnn_bx, rnn_lambda, g_mem, w_k, w_v,
           w_br_pool, w_br_rnn, w_br_attn, w_out, g_ffn, w_up, ffn_conv_w, ffn_conv_b, w_down, g_final):
    f32 = lambda a: np.ascontiguousarray(np.asarray(a, dtype=np.float32))
    if "nc" not in _CACHE:
        _CACHE["nc"] = build_program()
    nc = _CACHE["nc"]
    shared = dict(consts=_consts(), g_mix=f32(g_mix), w_in=f32(w_in), w_gate=f32(w_gate), b_gate=f32(b_gate),
                  pool_w=f32(pool_w), pool_scale=f32(pool_scale), rnn_conv_w=f32(rnn_conv_w), rnn_conv_b=f32(rnn_conv_b),
                  rnn_wa=f32(rnn_wa), rnn_ba=f32(rnn_ba), rnn_wx=f32(rnn_wx), rnn_bx=f32(rnn_bx), rnn_lambda=f32(rnn_lambda),
                  g_mem=f32(g_mem), w_k=f32(w_k), w_v=f32(w_v), w_br_pool=f32(w_br_pool), w_br_rnn=f32(w_br_rnn),
                  w_br_attn=f32(w_br_attn), w_out=f32(w_out), g_ffn=f32(g_ffn), w_up=f32(w_up), ffn_conv_w=f32(ffn_conv_w),
                  ffn_conv_b=f32(ffn_conv_b), w_down=f32(w_down), g_final=f32(g_final))
    xpr = f32(x_prompt); xsa = f32(x_sample); mp = f32(mem_prompt)
    ckk = f32(cache_mem_k).reshape(2, 128, NMEM, 512); cvv = f32(cache_mem_v).reshape(2, 128, NMEM, 512)
    sp_ = f32(state_pool); src_ = f32(state_rnn_conv); srh_ = f32(state_rnn_h); sfc_ = f32(state_ffn_conv)
    in_maps = []
    for c in range(NCORES):
        b0, b1 = c * SB_, (c + 1) * SB_
        m = dict(shared)
        m.update(xp=xpr[c], xs=np.ascontiguousarray(xsa[b0:b1]), memp=mp[c],
                 ck=np.ascontiguousarray(ckk[:, b0:b1]), cv=np.ascontiguousarray(cvv[:, b0:b1]),
                 st_pool=np.ascontiguousarray(sp_[:, b0:b1]), st_rconv=np.ascontiguousarray(src_[:, b0:b1]),
                 st_rh=np.ascontiguousarray(srh_[:, b0:b1]), st_fconv=np.ascontiguousarray(sfc_[:, b0:b1]))
        in_maps.append(m)
    res = run_bass_kernel_spmd(nc, in_maps, core_ids=list(range(NCORES)))
    R = res.results
    g = lambda k: [np.asarray(R[c][k], dtype=np.float32) for c in range(NCORES)]
    y_prompt = np.stack(g("y_p"), 0)
    y_sample = np.concatenate(g("y_s"), 0)
    p_pool = np.stack(g("p_pool"), 1)
    p_rconv = np.stack(g("p_rconv"), 1)
    p_rh = np.stack([a.reshape(2, D) for a in g("p_rh")], 1)
    p_fconv = np.stack(g("p_fconv"), 1)
    p_mk = np.stack(g("p_mk"), 1).reshape(2, NCORES, NMEM, 4, 128)
    p_mv = np.stack(g("p_mv"), 1).reshape(2, NCORES, NMEM, 4, 128)
    s_pool = np.concatenate(g("s_pool"), 1)
    s_rconv = np.concatenate(g("s_rconv"), 1)
    s_rh = np.concatenate(g("s_rh"), 1)
    s_fconv = np.concatenate(g("s_fconv"), 1)
    return (y_prompt, y_sample, p_pool, p_rconv, p_rh, p_fconv, p_mk, p_mv, s_pool, s_rconv, s_rh, s_fconv)
```

```python
import contextlib
import numpy as np
import concourse.bass as bass
import concourse.mybir as mybir
from concourse.bass_utils import run_bass_kernel_spmd

F32 = mybir.dt.float32
BF16 = mybir.dt.bfloat16
AF = mybir.ActivationFunctionType
ALU = mybir.AluOpType

NCORES = 8
D = 1024
SEQ = 2048
NT = 512
NPT = SEQ // NT
SB_ = 16
ST_ = 8
NMEM = 256
DFF = 3072
EPS = 1e-6
NSLOT = 4
WSC_KIND = "Internal"
SLOTC = 4096
NDS = 24
NDS_POOL = 6
SAME_ENGINE_SYNC = True
INTERLEAVE = True
INTERLEAVE_SAMPLE = True
import os as _os0
SERIALIZE = _os0.environ.get('SERIALIZE', '0') == '1'


class Prog:
    ENG = ["pe", "act", "dve", "pool", "sp"]

    def __init__(self, nc, es):
        self.nc = nc
        self.sem = {e: es.enter_context(nc.semaphore("sem_" + e)) for e in self.ENG}
        self.cnt = {e: 0 for e in self.ENG}
        self.waited = {e: {} for e in self.ENG}
        self.code = {e: [] for e in self.ENG}
        self.lastw = {}
        self.readers = {}
        self.dsem = [es.enter_context(nc.semaphore("dsem%d" % i)) for i in range(NDS + NDS_POOL)]
        self.dcnt = [0] * (NDS + NDS_POOL)
        self.dnext = 0
        self.dnext_pool = 0
        self.nops = 0

    def _deps(self, engine, reads, writes):
        deps = {}

        def add(tok):
            if tok is None:
                return
            key, sem, val, eng = tok
            if eng == engine and (engine == "pe" or not SAME_ENGINE_SYNC) and not key.startswith("d"):
                return
            if self.waited[engine].get(key, 0) >= val:
                return
            if key not in deps or deps[key][1] < val:
                deps[key] = (sem, val)

        for r in reads:
            add(self.lastw.get(r))
            if isinstance(r, tuple) and r[0] == "bank":
                for tok in self.readers.get(r, {}).values():
                    if tok[3] != engine:
                        add(tok)
        for w in writes:
            add(self.lastw.get(w))
            for tok in self.readers.get(w, {}).values():
                add(tok)
        for key, (sem, val) in deps.items():
            self.waited[engine][key] = val
        return list(deps.values())

    def _commit(self, tok, reads, writes):
        key = tok[0]
        for w in writes:
            self.lastw[w] = tok
            self.readers[w] = {}
        for r in reads:
            if r in writes:
                continue
            d = self.readers.setdefault(r, {})
            if key not in d or d[key][2] < tok[2]:
                d[key] = tok

    def op(self, engine, fn, reads=(), writes=()):
        reads = tuple(reads)
        writes = tuple(writes)
        deps = self._deps(engine, reads, writes)
        if SERIALIZE and getattr(self, "lastop", None) is not None:
            lk, ls, lv, le = self.lastop
            if le != engine and self.waited[engine].get(lk, 0) < lv:
                deps.append((ls, lv))
                self.waited[engine][lk] = lv
        self.cnt[engine] += 1
        val = self.cnt[engine]
        sem = self.sem[engine]
        tok = ("c_" + engine, sem, val, engine)
        self.lastop = tok

        def emit(e, deps=deps, fn=fn, sem=sem):
            for s, v in deps:
                e.wait_ge(s, v)
            inst = fn(e)
            inst.then_inc(sem, 1)

        self.code[engine].append(emit)
        self._commit(tok, reads, writes)
        self.nops += 1
        self._coop_yield()

    def _coop_yield(self):
        import threading
        st = getattr(self, "_coop", None)
        if st is None or getattr(self, "_noyield", 0):
            return
        me = threading.current_thread()
        if me not in st["go"]:
            return
        st["yielded"].set()
        st["go"][me].wait()
        st["go"][me].clear()

    def interleave(self, chains):
        import threading
        st = {"go": {}, "yielded": threading.Event()}
        done = {}
        errs = []
        threads = []
        for fn in chains:
            def target(fn=fn):
                me = threading.current_thread()
                st["go"][me].wait()
                st["go"][me].clear()
                try:
                    fn()
                except BaseException as ex:
                    errs.append(ex)
                done[me] = True
                st["yielded"].set()
            t = threading.Thread(target=target)
            st["go"][t] = threading.Event()
            done[t] = False
            threads.append(t)
        self._coop = st
        for t in threads:
            t.start()
        active = list(threads)
        while active:
            for t in list(active):
                st["yielded"].clear()
                st["go"][t].set()
                st["yielded"].wait()
                if done[t]:
                    active.remove(t)
                if errs:
                    break
            if errs:
                break
        self._coop = None
        if errs:
            for t in threads:
                st["go"][t].set()
            raise errs[0]
        for t in threads:
            t.join()

    def dma(self, engine, out, in_, reads=(), writes=()):
        reads = tuple(reads)
        writes = tuple(writes)
        deps = self._deps(engine, reads, writes)
        if engine == "pool":
            i = NDS + self.dnext_pool
            self.dnext_pool = (self.dnext_pool + 1) % NDS_POOL
        else:
            i = self.dnext
            self.dnext = (self.dnext + 1) % NDS
        prev = self.dcnt[i]
        self.dcnt[i] += 16
        val = self.dcnt[i]
        sem = self.dsem[i]
        key = "d%d" % i
        if prev > 0 and self.waited[engine].get(key, 0) < prev:
            deps.append((sem, prev))
            self.waited[engine][key] = prev
        tok = (key, sem, val, None)

        def emit(e, deps=deps, sem=sem, out=out, in_=in_):
            for s, v in deps:
                e.wait_ge(s, v)
            e.dma_start(out=out, in_=in_).then_inc(sem, 16)

        self.code[engine].append(emit)
        self._commit(tok, reads, writes)

    def finalize(self):
        final = [(self.dsem[i], self.dcnt[i]) for i in range(NDS + NDS_POOL) if self.dcnt[i] > 0]
        allc = [(self.sem[e], self.cnt[e]) for e in self.ENG if self.cnt[e] > 0]
        code = self.code
        with self.nc.Block() as block:
            @block.tensor
            def _(e):
                for f in code["pe"]:
                    f(e)

            @block.scalar
            def _(e):
                for f in code["act"]:
                    f(e)

            @block.vector
            def _(e):
                for f in code["dve"]:
                    f(e)

            @block.gpsimd
            def _(e):
                for f in code["pool"]:
                    f(e)
                for s, v in final:
                    e.wait_ge(s, v)

            @block.sync
            def _(e):
                for f in code["sp"]:
                    f(e)
                for s, v in final:
                    e.wait_ge(s, v)
                for s, v in allc:
                    e.wait_ge(s, v)


def build_program(stage=99, ntiles=99, conv_only=None):
    nc = bass.Bass("TRN2", target_bir_lowering=False)
    es = contextlib.ExitStack()
    with es:
        es.enter_context(nc.allow_low_precision("bf16 matmul operands, fp32 accumulation"))
        es.enter_context(nc.allow_non_contiguous_dma("small strided state rows"))
        P = Prog(nc, es)

        def dram(name, shape, dt=F32, kind="ExternalInput"):
            return nc.dram_tensor(name, list(shape), dt, kind=kind).ap()

        xp = dram("xp", [SEQ, D])
        xs = dram("xs", [SB_, ST_, D])
        memp = dram("memp", [NMEM, D])
        ck = dram("ck", [2, SB_, NMEM, 512])
        cv = dram("cv", [2, SB_, NMEM, 512])
        st_pool = dram("st_pool", [2, SB_, 15, 512])
        st_rconv = dram("st_rconv", [2, SB_, 3, D])
        st_rh = dram("st_rh", [2, SB_, D])
        st_fconv = dram("st_fconv", [2, SB_, 2, DFF])
        consts = dram("consts", [128, 160])
        g_mix = dram("g_mix", [2, D]); w_in = dram("w_in", [2, D, 2048]); w_gate = dram("w_gate", [2, D, 3072])
        b_gate = dram("b_gate", [2, 3072]); pool_w = dram("pool_w", [2, 4, 128, 128]); pool_scale = dram("pool_scale", [2, 512])
        rnn_conv_w = dram("rnn_conv_w", [2, 4, D]); rnn_conv_b = dram("rnn_conv_b", [2, D])
        rnn_wa = dram("rnn_wa", [2, 8, 128, 128]); rnn_ba = dram("rnn_ba", [2, D])
        rnn_wx = dram("rnn_wx", [2, 8, 128, 128]); rnn_bx = dram("rnn_bx", [2, D]); rnn_lambda = dram("rnn_lambda", [2, D])
        g_mem = dram("g_mem", [2, D]); w_k = dram("w_k", [2, D, 512]); w_v = dram("w_v", [2, D, 512])
        w_br_pool = dram("w_br_pool", [2, 512, D]); w_br_rnn = dram("w_br_rnn", [2, D, D]); w_br_attn = dram("w_br_attn", [2, 512, D])
        w_out = dram("w_out", [2, D, D]); g_ffn = dram("g_ffn", [2, D]); w_up = dram("w_up", [2, D, 2 * DFF])
        ffn_conv_w = dram("ffn_conv_w", [2, 3, DFF]); ffn_conv_b = dram("ffn_conv_b", [2, DFF])
        w_down = dram("w_down", [2, DFF, D]); g_final = dram("g_final", [D])

        O = "ExternalOutput"
        y_p = dram("y_p", [SEQ, D], kind=O); y_s = dram("y_s", [SB_, ST_, D], kind=O)
        p_pool = dram("p_pool", [2, 15, 512], kind=O); p_rconv = dram("p_rconv", [2, 3, D], kind=O)
        p_rh = dram("p_rh", [2, 1, D], kind=O); p_fconv = dram("p_fconv", [2, 2, DFF], kind=O)
        p_mk = dram("p_mk", [2, NMEM, 512], kind=O); p_mv = dram("p_mv", [2, NMEM, 512], kind=O)
        s_pool = dram("s_pool", [2, SB_, 15, 512], kind=O); s_rconv = dram("s_rconv", [2, SB_, 3, D], kind=O)
        s_rh = dram("s_rh", [2, SB_, D], kind=O); s_fconv = dram("s_fconv", [2, SB_, 2, DFF], kind=O)

        blocks = []
        for i in (3, 1, 2, 0):
            blocks.append(("win%d" % i, 4096))
        blocks.append(("wk", 4096)); blocks.append(("wv", 4096))
        for f in range(8):
            blocks.append(("gg%d" % f, 3072))
            blocks.append(("gr%d" % f, 2048))
        for i in range(2):
            blocks.append(("wout%d" % i, 4096))
        for b in range(12):
            blocks.append(("wup%d" % b, 4096))
        for f in range(8):
            blocks.append(("wdn%d" % f, 3072))
        boff = {}
        off = 0
        for n, c in blocks:
            boff[n] = (off, c)
            off += c
        LCOLS = off
        wsc = nc.dram_tensor("wsc", [2, 128, LCOLS], BF16, kind=WSC_KIND).ap()

        def sb(name, shape, dt=F32):
            return es.enter_context(nc.sbuf_tensor(name, list(shape), dt))

        cst = sb("cst", [128, 160])
        ident = cst[:, 0:128]
        invcnt = cst[:, 128:143]
        c_one = cst[:, 143:144]
        c_eps = cst[:, 144:145]
        onesD = sb("onesD", [128, 128], BF16)
        ones1 = sb("ones1", [128, 128], BF16)
        NPV = 440
        pv = sb("pv", [128, NPV])
        clam = sb("clam", [128, 2, 8])
        clam2 = sb("clam2", [128, 2, 8])
        ltmp = sb("ltmp", [128, 8])
        poolw = sb("poolw", [128, 2, 4, 128], BF16)
        rwa = sb("rwa", [128, 2, 8, 128], BF16)
        rwx = sb("rwx", [128, 2, 8, 128], BF16)
        kT = sb("kT", [128, 2, 4, NMEM], BF16)
        vtm = sb("vtm", [128, 2, 2, 512], BF16)
        h_pool = sb("h_pool", [128, 2, 4, 15])
        h_rnn = sb("h_rnn", [128, 2, 8, 3])
        h_h = sb("h_h", [128, 2, 8])
        h_ffn = sb("h_ffn", [128, 2, 24, 2])
        def view(reg, off, shape, dt):
            n = 1
            for s_ in shape[1:]:
                n *= s_
            nb = n * (4 if dt == F32 else 2)
            ap = reg[:, off // 2:(off + nb) // 2]
            if dt == F32:
                ap = ap.bitcast(F32)
            if len(shape) == 3:
                ap = ap.rearrange("p (a b) -> p a b", a=shape[1])
            return ap

        hs_ffn = sb("hs_ffn", [128, 24, 32])
        hT = sb("hT", [128, 8, NT])
        xn = sb("xn", [128, 8, NT], BF16)
        xsq = sb("xsq", [128, 2, NT], BF16)
        rstd = sb("rstd", [128, NT])
        lnb = sb("lnb", [128, NT])
        PEXT = max(15 + NT, 23 * 16)
        REXT = max(3 + NT, 11 * 16)
        FEXT = max(2 + NT, 10 * 16)
        NRB = 2
        NFB = 2
        al = lambda x: (x + 63) // 64 * 64
        oA = {}
        o = 0
        for nm, nb in (("upx", 4 * PEXT * 4), ("urx", 8 * REXT * 4), ("qb", 4 * NT * 2), ("sA", PEXT * 4), ("sB", PEXT * 4),
                       ("dpl", 2 * NT * 2)):
            oA[nm] = o
            o = al(o + nb)
        szA = o
        o = 0
        for nm, nb in (("actb", 24 * NT * 2), ("fext0", FEXT * 4), ("fext1", FEXT * 4), ("fgc0", NT * 4), ("fgc1", NT * 4)):
            oA[nm] = o
            o = al(o + nb)
        szA = max(szA, o)
        regA = sb("regA", [128, szA // 2], BF16)
        upx = view(regA, oA["upx"], [128, 4, PEXT], F32)
        urx = view(regA, oA["urx"], [128, 8, REXT], F32)
        qb = view(regA, oA["qb"], [128, 4, NT], BF16)
        sA = view(regA, oA["sA"], [128, PEXT], F32)
        sB = view(regA, oA["sB"], [128, PEXT], F32)
        dpl = view(regA, oA["dpl"], [128, 2, NT], BF16)
        actb = view(regA, oA["actb"], [128, 24, NT], BF16)
        f_ext = [view(regA, oA["fext%d" % i], [128, FEXT], F32) for i in range(NFB)]
        f_gc = [view(regA, oA["fgc%d" % i], [128, NT], F32) for i in range(NFB)]
        ypool = sb("ypool", [128, 4, NT], BF16)
        yrnn = sb("yrnn", [128, 8, NT], BF16)
        yattn = sb("yattn", [128, 4, NT], BF16)
        r_xc = [sb("r_xc%d" % i, [128, NT]) for i in range(NRB)]
        r_xb = [sb("r_xb%d" % i, [128, NT], BF16) for i in range(NRB)]
        r_r = [sb("r_r%d" % i, [128, NT]) for i in range(NRB)]
        r_t = [sb("r_t%d" % i, [128, NT]) for i in range(NRB)]
        r_i = [sb("r_i%d" % i, [128, NT]) for i in range(NRB)]
        r_h = [sb("r_h%d" % i, [128, NT]) for i in range(NRB)]
        f_ge = r_xc
        regP = sb("regP", [128, 8192 // 2], BF16)
        PTb = [view(regP, i * 2048, [128, 2, 512], BF16) for i in range(2)]
        rsb = [view(regP, 4096 + i * 2048, [128, 512], F32) for i in range(2)]
        hs_pool = view(regP, 0, [128, 4, 240], F32)
        hs_rnn = view(regP, 6144, [128, 8, 48], F32)
        hs_h = view(regP, 6144 + 1536, [128, 8, 16], F32)
        NGS = 3
        regM = sb("regM", [128, (8 * NT * 2 + NGS * NT * 4) // 2], BF16)
        merged = view(regM, 0, [128, 8, NT], BF16)
        gsb = [view(regM, 8 * NT * 2 + i * NT * 4, [128, NT], F32) for i in range(NGS)]
        Kst = [view(regM, i * 4096, [128, 2, 512], F32) for i in range(2)]
        Vst = [view(regM, 8192, [128, 2, 512], BF16)]
        kTb = [view(regM, 10240, [128, 4, NMEM], BF16)]
        PTs = [view(regM, 12288 + i * 128, [128, 64], BF16) for i in range(2)]
        macc = sb("macc", [128, NT])
        mtmp = sb("mtmp", [128, NT])
        xr = sb("xr", [128, 2, D])
        yrow = sb("yrow", [128, D])
        orow = [sb("orow%d" % i, [128, 512]) for i in range(2)]
        kvrow = orow
        ring = [sb("ring%d" % i, [128, SLOTC], BF16) for i in range(NSLOT)]
        banks = [es.enter_context(nc.psum_tensor("psb%d" % i, [128, 512], F32)) for i in range(8)]

        bstate = {"next": 0, "held": set()}

        def bank_free(i):
            if i in bstate["held"]:
                return False
            key = ("bank", i)
            if P.lastw.get(key) is None:
                return True
            return len(P.readers.get(key, {})) > 0

        def bank(hold=False):
            tries = 0
            while True:
                for _ in range(8):
                    i = bstate["next"]
                    bstate["next"] = (i + 1) % 8
                    if bank_free(i):
                        if hold:
                            bstate["held"].add(i)
                        return i
                tries += 1
                if getattr(P, "_coop", None) is None or tries > 5000:
                    raise RuntimeError("no psum bank available (tries=%d)" % tries)
                P._coop_yield()

        def unhold(i):
            bstate["held"].discard(i)

        def BK(i):
            return ("bank", i)

        def mm_group(bi, out_ap, pairs, reads, extra_writes=()):
            n = len(pairs)

            def fn(e):
                inst = None
                for k, (l, r) in enumerate(pairs):
                    inst = e.matmul(out_ap, l, r, start=(k == 0), stop=(k == n - 1))
                return inst
            P.op("pe", fn, reads=reads, writes=(BK(bi),) + tuple(extra_writes))

        def transposes(bi, items, reads):
            def fn(e):
                inst = None
                for (o, i_, idn) in items:
                    inst = e.transpose(o, i_, idn)
                return inst
            P.op("pe", fn, reads=reads, writes=(BK(bi),))

        def act(out, in_, func, reads, writes, bias=None, scale=None):
            kw = {}
            if bias is not None:
                kw["bias"] = bias
            if scale is not None:
                kw["scale"] = scale
            P.op("act", lambda e: e.activation(out=out, in_=in_, func=func, **kw), reads=reads, writes=writes)

        def acopy(out, in_, reads, writes):
            P.op("act", lambda e: e.copy(out=out, in_=in_), reads=reads, writes=writes)

        def vcopy(out, in_, reads, writes):
            P.op("dve", lambda e: e.tensor_copy(out=out, in_=in_), reads=reads, writes=writes)

        pool_ok = {"v": False}

        def pcopy(out, in_, reads, writes, alt="dve"):
            if pool_ok["v"]:
                P.op("pool", lambda e: e.tensor_copy(out=out, in_=in_), reads=reads, writes=writes)
            elif alt == "act":
                acopy(out, in_, reads, writes)
            else:
                vcopy(out, in_, reads, writes)

        cp_toggle = {"i": 0}

        def anycopy(out, in_, reads, writes):
            cp_toggle["i"] ^= 1
            if cp_toggle["i"]:
                acopy(out, in_, reads, writes)
            else:
                vcopy(out, in_, reads, writes)

        def vtt(out, in0, in1, op, reads, writes):
            P.op("dve", lambda e: e.tensor_tensor(out=out, in0=in0, in1=in1, op=op), reads=reads, writes=writes)

        def vts(out, in0, s1, s2, op0, op1, reads, writes):
            if op1 is None:
                P.op("dve", lambda e: e.tensor_scalar(out=out, in0=in0, scalar1=s1, scalar2=None, op0=op0),
                     reads=reads, writes=writes)
            else:
                P.op("dve", lambda e: e.tensor_scalar(out=out, in0=in0, scalar1=s1, scalar2=s2, op0=op0, op1=op1),
                     reads=reads, writes=writes)

        def vstt(out, in0, scalar, in1, op0, op1, reads, writes):
            P.op("dve", lambda e: e.scalar_tensor_tensor(out=out, in0=in0, scalar=scalar, in1=in1, op0=op0, op1=op1),
                 reads=reads, writes=writes)

        P.dma("sp", cst[:, :], consts, writes=["cst"])
        P.op("dve", lambda e: e.memset(onesD[:, :], 1.0 / D), writes=["onesD"])
        P.op("dve", lambda e: e.memset(ones1[:, :], 1.0), writes=["ones1"])
        P.op("dve", lambda e: e.memset(h_pool[:, :, :, :], 0.0), writes=["h_pool0", "h_pool1"])
        P.op("dve", lambda e: e.memset(h_rnn[:, :, :, :], 0.0), writes=["h_rnn0", "h_rnn1"])
        P.op("dve", lambda e: e.memset(h_h[:, :, :], 0.0), writes=["h_h0", "h_h1"])
        P.op("dve", lambda e: e.memset(h_ffn[:, :, :, :], 0.0), writes=["h_ffn0", "h_ffn1"])

        plist = []
        for l in range(2):
            plist += [
                (("g_mix", l), g_mix[l].rearrange("(c p) -> c p", p=128), 8),
                (("g_ffn", l), g_ffn[l].rearrange("(c p) -> c p", p=128), 8),
                (("b_gate", l), b_gate[l].rearrange("(c p) -> c p", p=128), 24),
                (("pool_scale", l), pool_scale[l].rearrange("(c p) -> c p", p=128), 4),
                (("rnn_conv_w", l), rnn_conv_w[l].rearrange("j (c p) -> (j c) p", p=128), 32),
                (("rnn_conv_b", l), rnn_conv_b[l].rearrange("(c p) -> c p", p=128), 8),
                (("rnn_ba", l), rnn_ba[l].rearrange("(c p) -> c p", p=128), 8),
                (("rnn_bx", l), rnn_bx[l].rearrange("(c p) -> c p", p=128), 8),
                (("rnn_lambda", l), rnn_lambda[l].rearrange("(c p) -> c p", p=128), 8),
                (("g_mem", l), g_mem[l].rearrange("(c p) -> c p", p=128), 8),
                (("ffn_conv_w", l), ffn_conv_w[l].rearrange("j (c p) -> (j c) p", p=128), 72),
                (("ffn_conv_b", l), ffn_conv_b[l].rearrange("(c p) -> c p", p=128), 24),
            ]
        plist.append((("g_final", 0), g_final.rearrange("(c p) -> c p", p=128), 8))
        pcol = {}
        col = 0
        segs = []
        for key, ap, C in plist:
            pcol[key] = col
            done = 0
            while done < C:
                ti, r0 = divmod(col + done, 128)
                n = min(C - done, 128 - r0)
                segs.append((ti, r0, n, ap[done:done + n, :]))
                done += n
            col += C
        assert col <= NPV
        nstage = (col + 127) // 128
        for (ti, r0, n, src) in segs:
            P.dma("sp", xr[r0:r0 + n, 0, ti * 128:(ti + 1) * 128], src, writes=[("pstage", ti)])
        for ti in range(nstage):
            R = min(128, col - ti * 128)
            bi = bank()
            transposes(bi, [(banks[bi][:, 0:R], xr[0:R, 0, ti * 128:(ti + 1) * 128], ident[0:R, 0:R])],
                       reads=[("pstage", ti), "cst"])
            vcopy(pv[:, ti * 128:ti * 128 + R], banks[bi][:, 0:R], [BK(bi)], ["pv"])
        P.op("dve", lambda e: e.memset(ltmp[:, :], 0.0), reads=[("pstage", t) for t in range(nstage)] + ["pv"],
             writes=["xr", "ltmp"])

        def PVc(name, l, c, n=1):
            o = pcol[(name, l)] + c
            return pv[:, o:o + n]

        for l in range(2):
            act(ltmp[:, :], PVc("rnn_lambda", l, 0, 8), AF.Exp, ["pv", "ltmp"], ["ltmp"], scale=-1.0)
            act(ltmp[:, :], ltmp[:, :], AF.Ln, ["ltmp", "cst"], ["ltmp"], bias=c_one)
            P.op("act", lambda e, l=l: e.activation(out=clam[:, l, :], in_=ltmp[:, :], func=AF.Copy, scale=-8.0),
                 reads=["ltmp"], writes=["clam"])
            P.op("act", lambda e, l=l: e.activation(out=clam2[:, l, :], in_=ltmp[:, :], func=AF.Copy, scale=-16.0),
                 reads=["ltmp"], writes=["clam"])

        for l in range(2):
            P.dma("pool", poolw[:, l, :, :], pool_w[l].rearrange("g c d -> c g d"), writes=[("poolw", l)])
            P.dma("pool", rwa[:, l, :, :], rnn_wa[l].rearrange("g c d -> c g d"), writes=[("rwa", l)])
            P.dma("pool", rwx[:, l, :, :], rnn_wx[l].rearrange("g c d -> c g d"), writes=[("rwx", l)])

        def conv_block(l, name):
            o, c = boff[name]
            dst = wsc[l][:, o:o + c]
            res = ("wsc", l, name)
            parts = []
            if name.startswith("win"):
                i = int(name[3:])
                parts.append((dst.rearrange("p (k f) -> p k f", k=8),
                              w_in[l][:, i * 512:(i + 1) * 512].rearrange("(k p) f -> p k f", p=128)))
            elif name == "wk":
                parts.append((dst.rearrange("p (k f) -> p k f", k=8), w_k[l].rearrange("(k p) f -> p k f", p=128)))
            elif name == "wv":
                parts.append((dst.rearrange("p (k f) -> p k f", k=8), w_v[l].rearrange("(k p) f -> p k f", p=128)))
            elif name.startswith("gg"):
                f = int(name[2:])
                d3 = dst.rearrange("p (k c) -> p k c", c=128)
                for br in range(3):
                    parts.append((d3[:, br * 8:(br + 1) * 8, :],
                                  w_gate[l][:, br * 1024 + f * 128: br * 1024 + (f + 1) * 128].rearrange("(k p) c -> p k c", p=128)))
            elif name.startswith("gr"):
                f = int(name[2:])
                d3 = dst.rearrange("p (k c) -> p k c", c=128)
                parts.append((d3[:, 0:4, :], w_br_pool[l][:, f * 128:(f + 1) * 128].rearrange("(k p) c -> p k c", p=128)))
                parts.append((d3[:, 4:12, :], w_br_rnn[l][:, f * 128:(f + 1) * 128].rearrange("(k p) c -> p k c", p=128)))
                parts.append((d3[:, 12:16, :], w_br_attn[l][:, f * 128:(f + 1) * 128].rearrange("(k p) c -> p k c", p=128)))
            elif name.startswith("wout"):
                i = int(name[4:])
                parts.append((dst.rearrange("p (k f) -> p k f", k=8),
                              w_out[l][:, i * 512:(i + 1) * 512].rearrange("(k p) f -> p k f", p=128)))
            elif name.startswith("wup"):
                b = int(name[3:])
                d4 = dst.rearrange("p (g k f) -> p g k f", g=2, k=8)
                for gv in range(2):
                    parts.append((d4[:, gv, :, :],
                                  w_up[l][:, gv * DFF + b * 256: gv * DFF + (b + 1) * 256].rearrange("(k p) f -> p k f", p=128)))
            elif name.startswith("wdn"):
                f = int(name[3:])
                parts.append((dst.rearrange("p (k c) -> p k c", c=128),
                              w_down[l][:, f * 128:(f + 1) * 128].rearrange("(k p) c -> p k c", p=128)))
            for pi, (d_, s_) in enumerate(parts):
                P.dma("pool", d_, s_, writes=[res + (pi,)])
            return [res + (pi,) for pi in range(len(parts))]

        wsc_parts = {}
        if stage < 1:
            conv_order_skip = True
        else:
            conv_order_skip = False
        conv_order = [(0, "wk"), (0, "wv"), (1, "wk"), (1, "wv")]
        for l in range(2):
            for n, _ in blocks:
                if n not in ("wk", "wv"):
                    conv_order.append((l, n))
        for (l, n) in conv_order:
            wsc_parts[(l, n)] = [] if (conv_order_skip or (conv_only is not None and not n.startswith(conv_only))) else conv_block(l, n)

        tiles = [("p", i) for i in range(NPT)] + [("s", 0)]
        seq = [(0, "wk"), (0, "wv"), (1, "wk"), (1, "wv")]
        for tl in tiles:
            for l in range(2):
                for n, _ in blocks:
                    if n not in ("wk", "wv"):
                        seq.append((l, n))
        wst = {"loaded": 0, "pos": 0}

        def w_issue(k):
            l, n = seq[k]
            o, c = boff[n]
            s = k % NSLOT
            P.dma("sp", ring[s][:, 0:c], wsc[l][:, o:o + c], reads=wsc_parts[(l, n)], writes=[("slot", s)])

        def w_get(name_expected, l_expected):
            k = wst["pos"]
            assert seq[k] == (l_expected, name_expected), (seq[k], l_expected, name_expected)
            while wst["loaded"] <= min(k + NSLOT - 1, len(seq) - 1) and wst["loaded"] < k + NSLOT:
                if wst["loaded"] >= len(seq):
                    break
                w_issue(wst["loaded"])
                wst["loaded"] += 1
            s = k % NSLOT
            return ring[s], ("slot", s)

        def w_done():
            wst["pos"] += 1
            k = wst["pos"]
            nxt = k - 1 + NSLOT
            if nxt < len(seq) and wst["loaded"] == nxt:
                w_issue(nxt)
                wst["loaded"] += 1

        def rmsnorm(src, src_res, N, gname, l, out_fn, out_res):
            bi = bank()
            for c in range(8):
                q = c % 2
                act(xsq[:, q, 0:N], src(c), AF.Square, [src_res(c), ("xsq", q)], [("xsq", q)])
                P.op("pe", lambda e, c=c, q=q, bi=bi: e.matmul(banks[bi][:, 0:N], onesD[:, :], xsq[:, q, 0:N],
                                                              start=(c == 0), stop=(c == 7)),
                     reads=[("xsq", q), "onesD"], writes=[BK(bi)])
            act(lnb[:, 0:N], banks[bi][:, 0:N], AF.Ln, [BK(bi), "cst"], ["lnb"], bias=c_eps)
            act(rstd[:, 0:N], lnb[:, 0:N], AF.Exp, ["lnb"], ["rstd"], scale=-0.5)
            for c in range(8):
                vstt(out_fn(c), src(c), PVc(gname, l, c), rstd[:, 0:N], ALU.mult, ALU.mult,
                     [src_res(c), "pv", "rstd"], [out_res(c)])

        class RowOut:
            def __init__(self, R, nchunks, dst_fn):
                self.R, self.n, self.dst_fn = R, nchunks, dst_fn
                self.bi = None
                self.k = 0
                self.flip = 0

            def add(self, c, src_ap, src_reads):
                P._noyield = getattr(P, "_noyield", 0) + 1
                try:
                    self._add(c, src_ap, src_reads)
                finally:
                    P._noyield -= 1

            def _add(self, c, src_ap, src_reads):
                R = self.R
                if self.bi is None:
                    self.bi = bank(hold=True)
                bi = self.bi
                j = c % 4
                transposes(bi, [(banks[bi][0:R, j * 128:(j + 1) * 128], src_ap, ident)], reads=list(src_reads) + ["cst"])
                self.k += 1
                if j == 3 or c == self.n - 1:
                    grp = c // 4
                    w = (j + 1) * 128
                    ob = orow[self.flip]
                    ores = ("orow", self.flip)
                    self.flip ^= 1
                    anycopy(ob[0:R, 0:w], banks[bi][0:R, 0:w], [BK(bi)], [ores])
                    for (dap, r0, nr) in self.dst_fn(grp, w):
                        P.dma("sp", dap, ob[r0:r0 + nr, 0:w], reads=[ores], writes=[])
                    unhold(bi)
                    self.bi = None

        def load_fm(rows_ap, rows_res, R, nch, dst_fn, dst_res):
            c = 0
            while c < nch:
                g = min(4, nch - c)
                bi = bank()
                transposes(bi, [(banks[bi][:, j * 128:j * 128 + R], rows_ap[0:R, (c + j) * 128:(c + j + 1) * 128],
                                 ident[0:R, 0:R]) for j in range(g)], reads=[rows_res, "cst"])
                cp_toggle["i"] ^= 1
                cpf = acopy if cp_toggle["i"] else vcopy
                for j in range(g):
                    cpf(dst_fn(c + j), banks[bi][:, j * 128:j * 128 + R], [BK(bi)], [dst_res(c + j)])
                c += g

        if stage < 2:
            P.finalize()
            return nc
        P.dma("sp", xr[:, 0:2, :], memp.rearrange("(a p) d -> p a d", p=128), reads=[], writes=["xr"])
        for a in range(2):
            for half in range(2):
                bi = bank()
                transposes(bi, [(banks[bi][:, k * 128:(k + 1) * 128], xr[:, a, (half * 4 + k) * 128:(half * 4 + k + 1) * 128], ident)
                                for k in range(4)], reads=["xr", "cst"])
                anycopy(hT[:, half * 4:(half + 1) * 4, a * 128:(a + 1) * 128],
                        banks[bi][:, :].rearrange("p (k t) -> p k t", k=4), [BK(bi)],
                        [("hT", c) for c in range(half * 4, half * 4 + 4)])
        for l in range(2):
            rmsnorm(lambda c: hT[:, c, 0:NMEM], lambda c: ("hT", c), NMEM, "g_mem", l,
                    lambda c: xn[:, c, 0:NMEM], lambda c: ("xn", c))
            wk_t, wk_r = w_get("wk", l)
            wk3 = wk_t[:, 0:4096].rearrange("p (k f) -> p k f", k=8)
            xnr = [("xn", c) for c in range(8)]
            for h in range(4):
                bi = bank()
                mm_group(bi, banks[bi][:, 0:NMEM], [(wk3[:, k, h * 128:(h + 1) * 128], xn[:, k, 0:NMEM]) for k in range(8)],
                         reads=[wk_r] + xnr)
                anycopy(kT[:, l, h, :], banks[bi][:, 0:NMEM], [BK(bi)], [("kT", l)])
            for mc in range(2):
                bi = bank()
                mm_group(bi, banks[bi][:, :], [(xn[:, k, mc * 128:(mc + 1) * 128], wk3[:, k, :]) for k in range(8)],
                         reads=[wk_r] + xnr)
                acopy(kvrow[mc][:, :], banks[bi][:, :], [BK(bi)], [("orow", mc)])
                P.dma("sp", p_mk[l][mc * 128:(mc + 1) * 128, :], kvrow[mc][:, :], reads=[("orow", mc)])
            w_done()
            wv_t, wv_r = w_get("wv", l)
            wv3 = wv_t[:, 0:4096].rearrange("p (k f) -> p k f", k=8)
            for mc in range(2):
                bi = bank()
                mm_group(bi, banks[bi][:, :], [(xn[:, k, mc * 128:(mc + 1) * 128], wv3[:, k, :]) for k in range(8)],
                         reads=[wv_r] + xnr)
                acopy(kvrow[mc][:, :], banks[bi][:, :], [BK(bi)], [("orow", mc)])
                acopy(vtm[:, l, mc, :], banks[bi][:, :], [BK(bi)], [("vtm", l)])
                P.dma("sp", p_mv[l][mc * 128:(mc + 1) * 128, :], kvrow[mc][:, :], reads=[("orow", mc)])
            w_done()

        KEYS_A = ([("upx", g) for g in range(4)] + [("urx", n) for n in range(8)] + [("qb", h) for h in range(4)]
                  + ["sA", "sB", ("dpl", 0), ("dpl", 1)] + [("xc", q) for q in range(NRB)]
                  + [("actb", j) for j in range(24)] + [(nm, q) for nm in ("fext", "fgc", "fge") for q in range(NFB)])
        KEYS_M = ([("merged", f) for f in range(8)] + [("gs", i) for i in range(NGS)] + [("Kst", 0), ("Kst", 1), ("Vst", 0),
                  ("kTb", 0, 0), ("kTb", 0, 1), ("PTs", 0), ("PTs", 1)])
        KEYS_P = ([("PT", i, mc) for i in range(2) for mc in range(2)] + [("rs", 0), ("rs", 1)]
                  + [("hs_pool", c) for c in range(4)] + [("hs_rnn", c) for c in range(8)] + [("hs_h", c) for c in range(8)])

        def barrier(keys):
            P.op("dve", lambda e: e.memset(ltmp[:, :], 0.0), reads=[], writes=list(keys) + ["ltmp"])

        def tile_layer(kind, ti, l):
            samp = (kind == "s")
            N = 128 if samp else NT
            inner = 16 if samp else 1
            T = N // inner
            import os as _os2
            first = (not samp) and (ti == 0 or _os2.environ.get('FORCE_FIRST') == '1')
            last = (not samp) and ti == NPT - 1
            hTr = [("hT", c) for c in range(8)]
            xnr = [("xn", c) for c in range(8)]
            HP, HR, HF = 15 * inner, 3 * inner, 2 * inner

            barrier(KEYS_A + KEYS_M + KEYS_P)
            if samp:
                for j in range(15):
                    rr = (j % 8) * 16
                    if j % 8 == 0:
                        pass
                    P.dma("sp", xr[rr:rr + 16, 0 if j < 8 else 1, 0:512], st_pool[l][:, j, :], reads=["xr"], writes=[("xrs", j)])
                P.op("dve", lambda e: e.memset(ltmp[:, :], 0.0), reads=[("xrs", j) for j in range(15)] + ["xr"], writes=["xr", "ltmp"])
                load_fm(xr[:, 0, :], "xr", 128, 4, lambda c: hs_pool[:, c, 0:128], lambda c: ("hs_pool", c))
                load_fm(xr[:, 1, :], "xr", 112, 4, lambda c: hs_pool[:, c, 128:240], lambda c: ("hs_pool", c))
                P.dma("sp", s_pool[l][:, 0:7, :], st_pool[l][:, 8:15, :])
                P.op("dve", lambda e: e.memset(ltmp[:, :], 0.0), reads=["xr", ("hs_pool", 0), ("hs_pool", 1), ("hs_pool", 2), ("hs_pool", 3)],
                     writes=["xr", "ltmp"])
                for j in range(3):
                    P.dma("sp", xr[j * 16:(j + 1) * 16, 0, :], st_rconv[l][:, j, :], reads=["xr"], writes=[("xrs", j)])
                P.dma("sp", xr[0:16, 1, :], st_rh[l][:, :], reads=["xr"], writes=[("xrs", 3)])
                P.op("dve", lambda e: e.memset(ltmp[:, :], 0.0), reads=[("xrs", j) for j in range(4)] + ["xr"], writes=["xr", "ltmp"])
                load_fm(xr[:, 0, :], "xr", 48, 8, lambda c: hs_rnn[:, c, :], lambda c: ("hs_rnn", c))
                load_fm(xr[:, 1, :], "xr", 16, 8, lambda c: hs_h[:, c, :], lambda c: ("hs_h", c))
                P.op("dve", lambda e: e.memset(ltmp[:, :], 0.0), reads=["xr"] + [("hs_rnn", c) for c in range(8)] + [("hs_h", c) for c in range(8)],
                     writes=["xr", "ltmp"])
                for part in range(3):
                    for j in range(2):
                        P.dma("sp", xr[j * 16:(j + 1) * 16, 0, :], st_fconv[l][:, j, part * 1024:(part + 1) * 1024],
                              reads=["xr"], writes=[("xrs", j)])
                    P.op("dve", lambda e: e.memset(ltmp[:, :], 0.0), reads=[("xrs", 0), ("xrs", 1), "xr"], writes=["xr", "ltmp"])
                    load_fm(xr[:, 0, :], "xr", 32, 8, lambda c, part=part: hs_ffn[:, part * 8 + c, :],
                            lambda c, part=part: ("hs_ffn", part * 8 + c))
                    P.op("dve", lambda e: e.memset(ltmp[:, :], 0.0), reads=["xr"] + [("hs_ffn", part * 8 + c) for c in range(8)],
                         writes=["xr", "ltmp"])

            rmsnorm(lambda c: hT[:, c, 0:N], lambda c: ("hT", c), N, "g_mix", l,
                    lambda c: xn[:, c, 0:N], lambda c: ("xn", c))

            for zb in (3, 1, 2, 0):
                wt, wr = w_get("win%d" % zb, l)
                w3 = wt[:, 0:4096].rearrange("p (k f) -> p k f", k=8)
                for zi in range(4):
                    zc = zb * 4 + zi
                    bi = bank()
                    mm_group(bi, banks[bi][:, 0:N], [(w3[:, k, zi * 128:(zi + 1) * 128], xn[:, k, 0:N]) for k in range(8)],
                             reads=[wr] + xnr)
                    if zc < 4:
                        anycopy(upx[:, zc, HP:HP + N], banks[bi][:, 0:N], [BK(bi)], [("upx", zc)])
                    elif zc < 12:
                        anycopy(urx[:, zc - 4, HR:HR + N], banks[bi][:, 0:N], [BK(bi)], [("urx", zc - 4)])
                    else:
                        h = zc - 12
                        if samp:
                            o_ = qb[:, h, 0:N].rearrange("p (b t) -> p t b", t=ST_)
                            i_ = banks[bi][:, 0:N].rearrange("p (t b) -> p t b", b=SB_)
                        else:
                            o_ = qb[:, h, 0:N]
                            i_ = banks[bi][:, 0:N]
                        P.op("act", lambda e, o_=o_, i_=i_: e.activation(out=o_, in_=i_, func=AF.Copy, scale=128.0 ** -0.5),
                             reads=[BK(bi)], writes=[("qb", h)])
                w_done()

            for g in range(4):
                if samp:
                    vcopy(upx[:, g, 0:HP], hs_pool[:, g, :], [("hs_pool", g)], [("upx", g)])
                else:
                    vcopy(upx[:, g, 0:HP], h_pool[:, l, g, :], [("h_pool%d" % l)], [("upx", g)])
            for n in range(8):
                if samp:
                    vcopy(urx[:, n, 0:HR], hs_rnn[:, n, :], [("hs_rnn", n)], [("urx", n)])
                else:
                    vcopy(urx[:, n, 0:HR], h_rnn[:, l, n, :], [("h_rnn%d" % l)], [("urx", n)])

            def gen_attn():
                if not samp:
                    for h in range(4):
                        pt = PTb[h % 2]
                        ptr = ("PT", h % 2)
                        for mc in range(2):
                            bi = bank()
                            mm_group(bi, banks[bi][:, 0:N], [(kT[:, l, h, mc * 128:(mc + 1) * 128], qb[:, h, 0:N])],
                                     reads=[("kT", l), ("qb", h)])
                            act(pt[:, mc, 0:N], banks[bi][:, 0:N], AF.Exp, [BK(bi)], [ptr + (mc,)])
                        bo = bank()
                        mm_group(bo, banks[bo][:, 0:N], [(vtm[:, l, mc, h * 128:(h + 1) * 128], pt[:, mc, 0:N]) for mc in range(2)],
                                 reads=[("vtm", l), ptr + (0,), ptr + (1,)])
                        bd = bank()
                        mm_group(bd, banks[bd][:, 0:N], [(ones1[:, :], pt[:, mc, 0:N]) for mc in range(2)],
                                 reads=["ones1", ptr + (0,), ptr + (1,)])
                        rs = rsb[h % 2]
                        rsr = ("rs", h % 2)
                        act(rs[:, 0:N], banks[bd][:, 0:N], AF.Ln, [BK(bd)], [rsr])
                        act(rs[:, 0:N], rs[:, 0:N], AF.Exp, [rsr], [rsr], scale=-1.0)
                        vtt(yattn[:, h, 0:N], banks[bo][:, 0:N], rs[:, 0:N], ALU.mult, [BK(bo), rsr], [("yattn", h)])
                        yield
                else:
                    barrier(KEYS_M)
                    bO = bank(hold=True)
                    bD = bank(hold=True)
                    for b in range(SB_):
                        ks = Kst[b % 2]; vs = Vst[0]; kb = kTb[0]; ps = PTs[b % 2]
                        P.dma("sp", ks[:, :, :], ck[l][b].rearrange("(a p) f -> p a f", p=128), writes=[("Kst", b % 2)])
                        P.dma("pool", vs[:, :, :], cv[l][b].rearrange("(a p) f -> p a f", p=128), writes=[("Vst", 0)])
                        for hp in range(2):
                            bi = bank()
                            transposes(bi, [(banks[bi][:, (hh * 2 + mc) * 128:(hh * 2 + mc + 1) * 128],
                                             ks[:, mc, (hp * 2 + hh) * 128:(hp * 2 + hh + 1) * 128], ident)
                                            for hh in range(2) for mc in range(2)], reads=[("Kst", b % 2), "cst"])
                            anycopy(kb[:, hp * 2:hp * 2 + 2, :], banks[bi][:, :].rearrange("p (h m) -> p h m", h=2),
                                    [BK(bi)], [("kTb", 0, hp)])
                        bs = bank()
                        def fn_s(e, kb=kb, bs=bs, b=b):
                            inst = None
                            for mc in range(2):
                                for h in range(4):
                                    inst = e.matmul(banks[bs][:, (mc * 4 + h) * 8:(mc * 4 + h + 1) * 8],
                                                    kb[:, h, mc * 128:(mc + 1) * 128], qb[:, h, b * 8:(b + 1) * 8],
                                                    start=True, stop=True)
                            return inst
                        P.op("pe", fn_s, reads=[("kTb", 0, 0), ("kTb", 0, 1)] + [("qb", h) for h in range(4)], writes=[BK(bs)])
                        act(ps[:, 0:64], banks[bs][:, 0:64], AF.Exp, [BK(bs)], [("PTs", b % 2)])

                        def fn_o(e, vs=vs, ps=ps, b=b):
                            inst = None
                            for h in range(4):
                                for mc in range(2):
                                    inst = e.matmul(banks[bO][:, h * 128 + b * 8: h * 128 + (b + 1) * 8],
                                                    vs[:, mc, h * 128:(h + 1) * 128], ps[:, (mc * 4 + h) * 8:(mc * 4 + h + 1) * 8],
                                                    start=(mc == 0), stop=(mc == 1))
                            return inst
                        P.op("pe", fn_o, reads=[("Vst", 0), ("PTs", b % 2)], writes=[BK(bO)])

                        def fn_d(e, ps=ps, b=b):
                            inst = None
                            for mc in range(2):
                                inst = e.matmul(banks[bD][:, b * 32:(b + 1) * 32], ones1[:, :], ps[:, mc * 32:(mc + 1) * 32],
                                                start=(mc == 0), stop=(mc == 1))
                            return inst
                        P.op("pe", fn_d, reads=["ones1", ("PTs", b % 2)], writes=[BK(bD)])
                    rs = rsb[0]
                    act(rs[:, 0:512], banks[bD][:, :], AF.Ln, [BK(bD)], [("rs", 0)])
                    act(rs[:, 0:512], rs[:, 0:512], AF.Exp, [("rs", 0)], [("rs", 0)], scale=-1.0)
                    rs4 = rs[:, 0:512].rearrange("p (b h t) -> p h b t", h=4, t=ST_)
                    for h in range(4):
                        vtt(yattn[:, h, 0:N].rearrange("p (t b) -> p b t", b=SB_),
                            banks[bO][:, h * 128:(h + 1) * 128].rearrange("p (b t) -> p b t", t=ST_),
                            rs4[:, h, :, :], ALU.mult, [BK(bO), ("rs", 0)], [("yattn", h)])
                    unhold(bO); unhold(bD)
                    barrier(KEYS_M)

                yield
            def gen_pool():
                if samp:
                    ro_pool = RowOut(128, 4, lambda grp, w: [(s_pool[l][:, 7 + t, :], t * 16, 16) for t in range(ST_)])
                elif last:
                    ro_pool = RowOut(15, 4, lambda grp, w: [(p_pool[l][:, :], 0, 15)])
                else:
                    ro_pool = None
                for g in range(4):
                    wdw = 2 ** (g + 1)
                    e_ = upx[:, g, :]
                    er = ("upx", g)
                    L = HP + N
                    if g == 0:
                        vtt(sA[:, HP:L], e_[:, HP:L], e_[:, HP - inner:L - inner], ALU.add, [er, "sA"], ["sA"])
                        win = sA
                        winr = "sA"
                    else:
                        lo = HP - (wdw - 2) * inner
                        vtt(sA[:, lo:L], e_[:, lo:L], e_[:, lo - inner:L - inner], ALU.add, [er, "sA"], ["sA"])
                        cur, curr, oth, othr = sA, "sA", sB, "sB"
                        step = 2
                        while step < wdw:
                            lo = lo + step * inner
                            vtt(oth[:, lo:L], cur[:, lo:L], cur[:, lo - step * inner:L - step * inner], ALU.add, [curr, othr], [othr])
                            cur, curr, oth, othr = oth, othr, cur, curr
                            step *= 2
                        win, winr = cur, curr
                    dq = g % 2
                    vstt(dpl[:, dq, 0:N], win[:, HP:L], 1.0 / wdw, e_[:, HP:L], ALU.mult, ALU.subtract,
                         [winr, er, ("dpl", dq)], [("dpl", dq)])
                    if first:
                        k = wdw - 1
                        vtt(mtmp[:, 0:k], win[:, HP:HP + k], invcnt[:, 0:k], ALU.mult, [winr, "cst", "mtmp"], ["mtmp"])
                        vtt(dpl[:, dq, 0:k], mtmp[:, 0:k], e_[:, HP:HP + k], ALU.subtract, ["mtmp", er, ("dpl", dq)], [("dpl", dq)])
                    bi = bank()
                    mm_group(bi, banks[bi][:, 0:N], [(poolw[:, l, g, :], dpl[:, dq, 0:N])], reads=[("poolw", l), ("dpl", dq)])
                    vts(ypool[:, g, 0:N], banks[bi][:, 0:N], PVc("pool_scale", l, g), None, ALU.mult, None,
                        [BK(bi), "pv"], [("ypool", g)])
                    if ro_pool is not None:
                        if samp:
                            ro_pool.add(g, e_[:, HP:HP + 128], [er])
                        else:
                            ro_pool.add(g, e_[:, L - 15:L], [er])
                    if not samp and not last:
                        vcopy(h_pool[:, l, g, :], e_[:, L - 15:L], [er], ["h_pool%d" % l])

                    yield
                yield
            def mk_ro_r():
                if samp:
                    ro_rc = RowOut(48, 8, lambda grp, w: [(s_rconv[l][:, t, grp * 512:grp * 512 + w], t * 16, 16) for t in range(3)])
                    ro_rh = RowOut(16, 8, lambda grp, w: [(s_rh[l][:, grp * 512:grp * 512 + w], 0, 16)])
                elif last:
                    ro_rc = RowOut(3, 8, lambda grp, w: [(p_rconv[l][:, grp * 512:grp * 512 + w], 0, 3)])
                    ro_rh = RowOut(1, 8, lambda grp, w: [(p_rh[l][:, grp * 512:grp * 512 + w], 0, 1)])
                else:
                    ro_rc = ro_rh = None
                return ro_rc, ro_rh
            ro_rc, ro_rh = mk_ro_r()

            def gen_rglru(ns=range(8)):
                cw0 = pcol[("rnn_conv_w", l)]
                for n in ns:
                    q = n % NRB
                    e_ = urx[:, n, :]
                    er = ("urx", n)
                    xc, xb, rr, tt, ii, hh = r_xc[q], r_xb[q], r_r[q], r_t[q], r_i[q], r_h[q]
                    R = lambda s, q=q: (s, q)
                    if pool_ok["v"]:
                        P.op("pool", lambda e, xc=xc, e_=e_, n=n: e.tensor_scalar(
                            out=xc[:, 0:N], in0=e_[:, 0:N], scalar1=pv[:, cw0 + n:cw0 + n + 1], scalar2=PVc("rnn_conv_b", l, n),
                            op0=ALU.mult, op1=ALU.add), reads=[er, "pv", R("xc")], writes=[R("xc")])
                    else:
                        act(xc[:, 0:N], e_[:, 0:N], AF.Identity, [er, "pv", R("xc")], [R("xc")],
                            bias=PVc("rnn_conv_b", l, n), scale=pv[:, cw0 + n:cw0 + n + 1])
                    for j in range(1, 4):
                        vstt(xc[:, 0:N], e_[:, j * inner:j * inner + N], pv[:, cw0 + j * 8 + n:cw0 + j * 8 + n + 1], xc[:, 0:N],
                             ALU.mult, ALU.add, [er, "pv", R("xc")], [R("xc")])
                    pcopy(xb[:, 0:N], xc[:, 0:N], [R("xc"), R("xb")], [R("xb")], alt="act")
                    br_ = bank()
                    mm_group(br_, banks[br_][:, 0:N], [(rwa[:, l, n, :], xb[:, 0:N])], reads=[("rwa", l), R("xb")])
                    bx_ = bank()
                    mm_group(bx_, banks[bx_][:, 0:N], [(rwx[:, l, n, :], xb[:, 0:N])], reads=[("rwx", l), R("xb")])
                    act(rr[:, 0:N], banks[br_][:, 0:N], AF.Sigmoid, [BK(br_), "pv", R("r")], [R("r")], bias=PVc("rnn_ba", l, n))
                    act(ii[:, 0:N], banks[bx_][:, 0:N], AF.Sigmoid, [BK(bx_), "pv", R("i")], [R("i")], bias=PVc("rnn_bx", l, n))
                    act(tt[:, 0:N], rr[:, 0:N], AF.Exp, [R("r"), "clam", R("t")], [R("t")], scale=clam2[:, l, n:n + 1])
                    act(rr[:, 0:N], rr[:, 0:N], AF.Exp, [R("r"), "clam"], [R("r")], scale=clam[:, l, n:n + 1])
                    act(tt[:, 0:N], tt[:, 0:N], AF.Ln, [R("t"), "cst"], [R("t")], bias=c_one, scale=-1.0)
                    act(tt[:, 0:N], tt[:, 0:N], AF.Exp, [R("t")], [R("t")], scale=0.5)
                    vtt(ii[:, 0:N], ii[:, 0:N], xc[:, 0:N], ALU.mult, [R("i"), R("xc")], [R("i")])
                    vtt(ii[:, 0:N], ii[:, 0:N], tt[:, 0:N], ALU.mult, [R("i"), R("t")], [R("i")])
                    if not samp:
                        P.op("dve", lambda e, hh=hh, rr=rr, ii=ii, n=n: e.tensor_tensor_scan(
                            out=hh[:, 0:N], data0=rr[:, 0:N], data1=ii[:, 0:N], initial=h_h[:, l, n:n + 1],
                            op0=ALU.mult, op1=ALU.add), reads=[R("r"), R("i"), "h_h%d" % l, R("h")], writes=[R("h")])
                        vcopy(h_h[:, l, n:n + 1], hh[:, N - 1:N], [R("h")], ["h_h%d" % l])
                    else:
                        for t in range(ST_):
                            prev = hs_h[:, n, :] if t == 0 else hh[:, (t - 1) * 16:t * 16]
                            vtt(hh[:, t * 16:(t + 1) * 16], rr[:, t * 16:(t + 1) * 16], prev, ALU.mult,
                                [R("r"), R("h"), ("hs_h", n)], [R("h")])
                            vtt(hh[:, t * 16:(t + 1) * 16], hh[:, t * 16:(t + 1) * 16], ii[:, t * 16:(t + 1) * 16], ALU.add,
                                [R("i"), R("h")], [R("h")])
                    pcopy(yrnn[:, n, 0:N], hh[:, 0:N], [R("h")], [("yrnn", n)], alt="act")
                    if ro_rc is not None:
                        if samp:
                            ro_rc.add(n, e_[:, HR + 5 * 16:HR + 8 * 16], [er])
                            ro_rh.add(n, hh[:, 7 * 16:8 * 16], [R("h")])
                        else:
                            ro_rc.add(n, e_[:, HR + N - 3:HR + N], [er])
                            ro_rh.add(n, hh[:, N - 1:N], [R("h")])
                    if not samp and not last:
                        vcopy(h_rnn[:, l, n, :], e_[:, HR + N - 3:HR + N], [er], ["h_rnn%d" % l])

                    yield
                yield
            def drain(g_):
                for _ in g_:
                    pass
            if (samp and not INTERLEAVE_SAMPLE) or not INTERLEAVE:
                drain(gen_attn()); drain(gen_pool()); drain(gen_rglru())
            else:
                P.interleave([lambda: drain(gen_rglru(range(0, 8, 2))), lambda: drain(gen_rglru(range(1, 8, 2))),
                              lambda: drain(gen_attn()), lambda: drain(gen_pool())])

            bcol = pcol[("b_gate", l)]
            yp_r = [("ypool", g) for g in range(4)]
            yr_r = [("yrnn", n) for n in range(8)]
            ya_r = [("yattn", h) for h in range(4)]
            gi = 0
            for f in range(8):
                wtg, wrg = w_get("gg%d" % f, l)
                wg3 = wtg[:, 0:3072].rearrange("p (k c) -> p k c", c=128)
                gates_ps = {}
                for br in (0, 2, 1):
                    bg = bank()
                    mm_group(bg, banks[bg][:, 0:N], [(wg3[:, br * 8 + k, :], xn[:, k, 0:N]) for k in range(8)], reads=[wrg] + xnr)
                    gs = gsb[gi % NGS]
                    gr = ("gs", gi % NGS)
                    gi += 1
                    act(gs[:, 0:N], banks[bg][:, 0:N], AF.Sigmoid, [BK(bg), "pv", gr], [gr],
                        bias=pv[:, bcol + br * 8 + f: bcol + br * 8 + f + 1])
                    gates_ps[br] = (gs, gr)
                w_done()
                wt, wr = w_get("gr%d" % f, l)
                w3 = wt[:, 0:2048].rearrange("p (k c) -> p k c", c=128)
                first_term = True
                for (br, k0, nk, ysrc, yres) in ((0, 0, 4, ypool, yp_r), (2, 12, 4, yattn, ya_r), (1, 4, 8, yrnn, yr_r)):
                    gs, gr = gates_ps[br]
                    bp = bank()
                    mm_group(bp, banks[bp][:, 0:N], [(w3[:, k0 + k, :], ysrc[:, k, 0:N]) for k in range(nk)], reads=[wr] + yres)
                    if first_term:
                        vtt(macc[:, 0:N], gs[:, 0:N], banks[bp][:, 0:N], ALU.mult, [gr, BK(bp), "macc"], ["macc"])
                        first_term = False
                    elif br == 2:
                        vtt(mtmp[:, 0:N], gs[:, 0:N], banks[bp][:, 0:N], ALU.mult, [gr, BK(bp), "mtmp"], ["mtmp"])
                        vtt(macc[:, 0:N], macc[:, 0:N], mtmp[:, 0:N], ALU.add, ["macc", "mtmp"], ["macc"])
                    else:
                        vtt(mtmp[:, 0:N], gs[:, 0:N], banks[bp][:, 0:N], ALU.mult, [gr, BK(bp), "mtmp"], ["mtmp"])
                        vtt(merged[:, f, 0:N], macc[:, 0:N], mtmp[:, 0:N], ALU.add, ["macc", "mtmp"], [("merged", f)])
                w_done()

            mr = [("merged", f) for f in range(8)]
            for ob in range(2):
                wt, wr = w_get("wout%d" % ob, l)
                w3 = wt[:, 0:4096].rearrange("p (k f) -> p k f", k=8)
                for fi in range(4):
                    f = ob * 4 + fi
                    bi = bank()
                    mm_group(bi, banks[bi][:, 0:N], [(w3[:, k, fi * 128:(fi + 1) * 128], merged[:, k, 0:N]) for k in range(8)],
                             reads=[wr] + mr)
                    vtt(hT[:, f, 0:N], hT[:, f, 0:N], banks[bi][:, 0:N], ALU.add, [("hT", f), BK(bi)], [("hT", f)])
                w_done()

            barrier(KEYS_A)
            rmsnorm(lambda c: hT[:, c, 0:N], lambda c: ("hT", c), N, "g_ffn", l,
                    lambda c: xn[:, c, 0:N], lambda c: ("xn", c))
            if samp:
                ro_fc = RowOut(32, 24, lambda grp, w: [(s_fconv[l][:, t, grp * 512:grp * 512 + w], t * 16, 16) for t in range(2)])
            elif last:
                ro_fc = RowOut(2, 24, lambda grp, w: [(p_fconv[l][:, grp * 512:grp * 512 + w], 0, 2)])
            else:
                ro_fc = None
            fw0 = pcol[("ffn_conv_w", l)]
            for ub in range(12):
                wt, wr = w_get("wup%d" % ub, l)
                w4 = wt[:, 0:4096].rearrange("p (g k f) -> p g k f", g=2, k=8)
                for jj in range(2):
                    j = ub * 2 + jj
                    q = j % NFB
                    ex, gc, ge = f_ext[q], f_gc[q], f_ge[q]
                    R = lambda s, q=q: (s, q)
                    bg = bank()
                    mm_group(bg, banks[bg][:, 0:N], [(w4[:, 0, k, jj * 128:(jj + 1) * 128], xn[:, k, 0:N]) for k in range(8)],
                             reads=[wr] + xnr)
                    bv = bank()
                    mm_group(bv, banks[bv][:, 0:N], [(w4[:, 1, k, jj * 128:(jj + 1) * 128], xn[:, k, 0:N]) for k in range(8)],
                             reads=[wr] + xnr)
                    if samp:
                        pcopy(ex[:, 0:HF], hs_ffn[:, j, :], [("hs_ffn", j), R("fext")], [R("fext")])
                    else:
                        pcopy(ex[:, 0:HF], h_ffn[:, l, j, :], ["h_ffn%d" % l, R("fext")], [R("fext")])
                    acopy(ex[:, HF:HF + N], banks[bg][:, 0:N], [BK(bg), R("fext")], [R("fext")])
                    act(gc[:, 0:N], banks[bg][:, 0:N], AF.Identity, [BK(bg), "pv", R("fgc")], [R("fgc")],
                        bias=PVc("ffn_conv_b", l, j), scale=pv[:, fw0 + 2 * 24 + j:fw0 + 2 * 24 + j + 1])
                    for tap in range(0, 2):
                        vstt(gc[:, 0:N], ex[:, tap * inner:tap * inner + N], pv[:, fw0 + tap * 24 + j:fw0 + tap * 24 + j + 1],
                             gc[:, 0:N], ALU.mult, ALU.add, [R("fext"), "pv", R("fgc")], [R("fgc")])
                    act(ge[:, 0:N], gc[:, 0:N], AF.Gelu_apprx_tanh, [R("fgc"), R("fge")], [R("fge")])
                    vtt(actb[:, j, 0:N], ge[:, 0:N], banks[bv][:, 0:N], ALU.mult, [R("fge"), BK(bv)], [("actb", j)])
                    if ro_fc is not None:
                        ro_fc.add(j, ex[:, HF + N - 2 * inner:HF + N], [R("fext")])
                    if not samp and not last:
                        pcopy(h_ffn[:, l, j, :], ex[:, HF + N - 2:HF + N], [R("fext")], ["h_ffn%d" % l])
                w_done()
            ar = [("actb", j) for j in range(24)]
            for f in range(8):
                wt, wr = w_get("wdn%d" % f, l)
                w3 = wt[:, 0:3072].rearrange("p (k c) -> p k c", c=128)
                bi = bank()
                mm_group(bi, banks[bi][:, 0:N], [(w3[:, k, :], actb[:, k, 0:N]) for k in range(24)], reads=[wr] + ar)
                vtt(hT[:, f, 0:N], hT[:, f, 0:N], banks[bi][:, 0:N], ALU.add, [("hT", f), BK(bi)], [("hT", f)])
                w_done()

        if stage < 3:
            P.finalize()
            return nc
        import os as _os
        _sel = _os.environ.get('TILESEL')
        _tl = [tiles[int(x)] for x in _sel.split(',')] if _sel else (tiles[:ntiles] + (tiles[-1:] if ntiles < 0 else []))
        for (kind, ti) in _tl:
            samp = kind == "s"
            N = 128 if samp else NT
            if samp:
                for t in range(ST_):
                    P.dma("sp", xr[t * 16:(t + 1) * 16, 0, :], xs[:, t, :], reads=["xr"], writes=[("xrs", t)])
                P.op("dve", lambda e: e.memset(ltmp[:, :], 0.0), reads=[("xrs", t) for t in range(ST_)] + ["xr"], writes=["xr", "ltmp"])
            for a in range(N // 128):
                q = a % 2
                if samp:
                    xres = "xr"
                else:
                    xres = ("xrg", q)
                    r0 = ti * NT + a * 128
                    P.dma("sp", xr[:, q, :], xp[r0:r0 + 128, :], reads=["xr"], writes=[xres])
                for half in range(2):
                    bi = bank()
                    transposes(bi, [(banks[bi][:, k * 128:(k + 1) * 128], xr[:, q, (half * 4 + k) * 128:(half * 4 + k + 1) * 128], ident)
                                    for k in range(4)], reads=[xres, "cst"])
                    anycopy(hT[:, half * 4:(half + 1) * 4, a * 128:(a + 1) * 128],
                            banks[bi][:, :].rearrange("p (k t) -> p k t", k=4), [BK(bi)],
                            [("hT", c) for c in range(half * 4, half * 4 + 4)])
            P.op("dve", lambda e: e.memset(ltmp[:, :], 0.0), reads=[("hT", c) for c in range(8)],
                 writes=["xr", ("xrg", 0), ("xrg", 1), "ltmp"])
            pool_ok["v"] = not (kind == "p" and ti == 0)
            for l in range(2):
                tile_layer(kind, ti, l)
            rmsnorm(lambda c: hT[:, c, 0:N], lambda c: ("hT", c), N, "g_final", 0,
                    lambda c: hT[:, c, 0:N], lambda c: ("hT", c))
            for a in range(N // 128):
                for half in range(2):
                    bi = bank()
                    transposes(bi, [(banks[bi][:, k * 128:(k + 1) * 128], hT[:, half * 4 + k, a * 128:(a + 1) * 128], ident)
                                    for k in range(4)], reads=[("hT", half * 4 + k) for k in range(4)] + ["cst"])
                    anycopy(yrow[:, half * 512:(half + 1) * 512], banks[bi][:, :], [BK(bi)], [("yrow", half)])
                if samp:
                    for t in range(ST_):
                        P.dma("sp", y_s[:, t, :], yrow[t * 16:(t + 1) * 16, :], reads=[("yrow", 0), ("yrow", 1)])
                else:
                    r0 = ti * NT + a * 128
                    P.dma("sp", y_p[r0:r0 + 128, :], yrow[:, :], reads=[("yrow", 0), ("yrow", 1)])
        assert stage < 99 or ntiles < 99 or wst["pos"] == len(seq), (wst["pos"], len(seq))
        print('OPCOUNTS', P.cnt, P.dcnt, flush=True)
        P.finalize()
    return nc


_CACHE = {}


def _consts():
    c = np.zeros((128, 160), np.float32)
    c[:, 0:128] = np.eye(128, dtype=np.float32)
    c[:, 128:143] = (1.0 / np.arange(1, 16, dtype=np.float32))[None, :]
    c[:, 143] = 1.0
    c[:, 144] = EPS
    return c


def kernel(x_prompt, x_sample, mem_prompt, cache_mem_k, cache_mem_v, state_pool, state_rnn_conv,
           state_rnn_h, state_ffn_conv, g_mix, w_in, w_gate, b_gate, pool_w, pool_scale,
           rnn_conv_w, rnn_conv_b, rnn_wa, rnn_ba, rnn_wx, rnn_bx, rnn_lambda, g_mem, w_k, w_v,
           w_br_pool, w_br_rnn, w_br_attn, w_out, g_ffn, w_up, ffn_conv_w, ffn_conv_b, w_down, g_final):
    f32 = lambda a: np.ascontiguousarray(np.asarray(a, dtype=np.float32))
    if "nc" not in _CACHE:
        _CACHE["nc"] = build_program()
    nc = _CACHE["nc"]
    shared = dict(consts=_consts(), g_mix=f32(g_mix), w_in=f32(w_in), w_gate=f32(w_gate), b_gate=f32(b_gate),
                  pool_w=f32(pool_w), pool_scale=f32(pool_scale), rnn_conv_w=f32(rnn_conv_w), rnn_conv_b=f32(rnn_conv_b),
                  rnn_wa=f32(rnn_wa), rnn_ba=f32(rnn_ba), rnn_wx=f32(rnn_wx), rnn_bx=f32(rnn_bx), rnn_lambda=f32(rnn_lambda),
                  g_mem=f32(g_mem), w_k=f32(w_k), w_v=f32(w_v), w_br_pool=f32(w_br_pool), w_br_rnn=f32(w_br_rnn),
                  w_br_attn=f32(w_br_attn), w_out=f32(w_out), g_ffn=f32(g_ffn), w_up=f32(w_up), ffn_conv_w=f32(ffn_conv_w),
                  ffn_conv_b=f32(ffn_conv_b), w_down=f32(w_down), g_final=f32(g_final))
    xpr = f32(x_prompt); xsa = f32(x_sample); mp = f32(mem_prompt)
    ckk = f32(cache_mem_k).reshape(2, 128, NMEM, 512); cvv = f32(cache_mem_v).reshape(2, 128, NMEM, 512)
    sp_ = f32(state_pool); src_ = f32(state_rnn_conv); srh_ = f32(state_rnn_h); sfc_ = f32(state_ffn_conv)
    in_maps = []
    for c in range(NCORES):
        b0, b1 = c * SB_, (c + 1) * SB_
        m = dict(shared)
        m.update(xp=xpr[c], xs=np.ascontiguousarray(xsa[b0:b1]), memp=mp[c],
                 ck=np.ascontiguousarray(ckk[:, b0:b1]), cv=np.ascontiguousarray(cvv[:, b0:b1]),
                 st_pool=np.ascontiguousarray(sp_[:, b0:b1]), st_rconv=np.ascontiguousarray(src_[:, b0:b1]),
                 st_rh=np.ascontiguousarray(srh_[:, b0:b1]), st_fconv=np.ascontiguousarray(sfc_[:, b0:b1]))
        in_maps.append(m)
    res = run_bass_kernel_spmd(nc, in_maps, core_ids=list(range(NCORES)))
    R = res.results
    g = lambda k: [np.asarray(R[c][k], dtype=np.float32) for c in range(NCORES)]
    y_prompt = np.stack(g("y_p"), 0)
    y_sample = np.concatenate(g("y_s"), 0)
    p_pool = np.stack(g("p_pool"), 1)
    p_rconv = np.stack(g("p_rconv"), 1)
    p_rh = np.stack([a.reshape(2, D) for a in g("p_rh")], 1)
    p_fconv = np.stack(g("p_fconv"), 1)
    p_mk = np.stack(g("p_mk"), 1).reshape(2, NCORES, NMEM, 4, 128)
    p_mv = np.stack(g("p_mv"), 1).reshape(2, NCORES, NMEM, 4, 128)
    s_pool = np.concatenate(g("s_pool"), 1)
    s_rconv = np.concatenate(g("s_rconv"), 1)
    s_rh = np.concatenate(g("s_rh"), 1)
    s_fconv = np.concatenate(g("s_fconv"), 1)
    return (y_prompt, y_sample, p_pool, p_rconv, p_rh, p_fconv, p_mk, p_mv, s_pool, s_rconv, s_rh, s_fconv)
```

```python
import contextlib
import numpy as np
import concourse.bass as bass
import concourse.mybir as mybir
from concourse.bass_utils import run_bass_kernel_spmd

F32 = mybir.dt.float32
BF16 = mybir.dt.bfloat16
AF = mybir.ActivationFunctionType
ALU = mybir.AluOpType

NCORES = 8
D = 1024
SEQ = 2048
NT = 512
NPT = SEQ // NT
SB_ = 16
ST_ = 8
NMEM = 256
DFF = 3072
EPS = 1e-6
NSLOT = 4
WSC_KIND = "Internal"
SLOTC = 4096
NDS = 24
NDS_POOL = 6
SAME_ENGINE_SYNC = True
INTERLEAVE = True
INTERLEAVE_SAMPLE = True
import os as _os0
SERIALIZE = _os0.environ.get('SERIALIZE', '0') == '1'


class Prog:
    ENG = ["pe", "act", "dve", "pool", "sp"]

    def __init__(self, nc, es):
        self.nc = nc
        self.sem = {e: es.enter_context(nc.semaphore("sem_" + e)) for e in self.ENG}
        self.cnt = {e: 0 for e in self.ENG}
        self.waited = {e: {} for e in self.ENG}
        self.code = {e: [] for e in self.ENG}
        self.lastw = {}
        self.readers = {}
        self.dsem = [es.enter_context(nc.semaphore("dsem%d" % i)) for i in range(NDS + NDS_POOL)]
        self.dcnt = [0] * (NDS + NDS_POOL)
        self.dnext = 0
        self.dnext_pool = 0
        self.nops = 0

    def _deps(self, engine, reads, writes):
        deps = {}

        def add(tok, raw=False):
            if tok is None:
                return
            key, sem, val, eng = tok
            if eng == engine and (engine == "pe" or not SAME_ENGINE_SYNC) and not key.startswith("d"):
                return
            if eng == engine and not raw and not key.startswith("d"):
                return
            if self.waited[engine].get(key, 0) >= val:
                return
            if key not in deps or deps[key][1] < val:
                deps[key] = (sem, val)

        for r in reads:
            add(self.lastw.get(r), raw=True)
            if isinstance(r, tuple) and r[0] == "bank":
                for tok in self.readers.get(r, {}).values():
                    if tok[3] != engine:
                        add(tok)
        for w in writes:
            add(self.lastw.get(w))
            for tok in self.readers.get(w, {}).values():
                add(tok)
        for key, (sem, val) in deps.items():
            self.waited[engine][key] = val
        return list(deps.values())

    def _commit(self, tok, reads, writes):
        key = tok[0]
        for w in writes:
            self.lastw[w] = tok
            self.readers[w] = {}
        for r in reads:
            if r in writes:
                continue
            d = self.readers.setdefault(r, {})
            if key not in d or d[key][2] < tok[2]:
                d[key] = tok

    def op(self, engine, fn, reads=(), writes=()):
        reads = tuple(reads)
        writes = tuple(writes)
        deps = self._deps(engine, reads, writes)
        if SERIALIZE and getattr(self, "lastop", None) is not None:
            lk, ls, lv, le = self.lastop
            if le != engine and self.waited[engine].get(lk, 0) < lv:
                deps.append((ls, lv))
                self.waited[engine][lk] = lv
        self.cnt[engine] += 1
        val = self.cnt[engine]
        sem = self.sem[engine]
        tok = ("c_" + engine, sem, val, engine)
        self.lastop = tok

        def emit(e, deps=deps, fn=fn, sem=sem):
            for s, v in deps:
                e.wait_ge(s, v)
            inst = fn(e)
            inst.then_inc(sem, 1)

        self.code[engine].append(emit)
        self._commit(tok, reads, writes)
        self.nops += 1
        self._coop_yield()

    def _coop_yield(self):
        import threading
        st = getattr(self, "_coop", None)
        if st is None or getattr(self, "_noyield", 0):
            return
        me = threading.current_thread()
        if me not in st["go"]:
            return
        st["yielded"].set()
        st["go"][me].wait()
        st["go"][me].clear()

    def interleave(self, chains):
        import threading
        st = {"go": {}, "yielded": threading.Event()}
        done = {}
        errs = []
        threads = []
        for fn in chains:
            def target(fn=fn):
                me = threading.current_thread()
                st["go"][me].wait()
                st["go"][me].clear()
                try:
                    fn()
                except BaseException as ex:
                    errs.append(ex)
                done[me] = True
                st["yielded"].set()
            t = threading.Thread(target=target)
            st["go"][t] = threading.Event()
            done[t] = False
            threads.append(t)
        self._coop = st
        for t in threads:
            t.start()
        active = list(threads)
        while active:
            for t in list(active):
                st["yielded"].clear()
                st["go"][t].set()
                st["yielded"].wait()
                if done[t]:
                    active.remove(t)
                if errs:
                    break
            if errs:
                break
        self._coop = None
        if errs:
            for t in threads:
                st["go"][t].set()
            raise errs[0]
        for t in threads:
            t.join()

    def dma(self, engine, out, in_, reads=(), writes=()):
        reads = tuple(reads)
        writes = tuple(writes)
        deps = self._deps(engine, reads, writes)
        if engine == "pool":
            i = NDS + self.dnext_pool
            self.dnext_pool = (self.dnext_pool + 1) % NDS_POOL
        else:
            i = self.dnext
            self.dnext = (self.dnext + 1) % NDS
        prev = self.dcnt[i]
        self.dcnt[i] += 16
        val = self.dcnt[i]
        sem = self.dsem[i]
        key = "d%d" % i
        if prev > 0 and self.waited[engine].get(key, 0) < prev:
            deps.append((sem, prev))
            self.waited[engine][key] = prev
        tok = (key, sem, val, None)

        def emit(e, deps=deps, sem=sem, out=out, in_=in_):
            for s, v in deps:
                e.wait_ge(s, v)
            e.dma_start(out=out, in_=in_).then_inc(sem, 16)

        self.code[engine].append(emit)
        self._commit(tok, reads, writes)

    def finalize(self):
        final = [(self.dsem[i], self.dcnt[i]) for i in range(NDS + NDS_POOL) if self.dcnt[i] > 0]
        allc = [(self.sem[e], self.cnt[e]) for e in self.ENG if self.cnt[e] > 0]
        code = self.code
        with self.nc.Block() as block:
            @block.tensor
            def _(e):
                for f in code["pe"]:
                    f(e)

            @block.scalar
            def _(e):
                for f in code["act"]:
                    f(e)

            @block.vector
            def _(e):
                for f in code["dve"]:
                    f(e)

            @block.gpsimd
            def _(e):
                for f in code["pool"]:
                    f(e)
                for s, v in final:
                    e.wait_ge(s, v)

            @block.sync
            def _(e):
                for f in code["sp"]:
                    f(e)
                for s, v in final:
                    e.wait_ge(s, v)
                for s, v in allc:
                    e.wait_ge(s, v)


def build_program(stage=99, ntiles=99, conv_only=None):
    nc = bass.Bass("TRN2", target_bir_lowering=False)
    es = contextlib.ExitStack()
    with es:
        es.enter_context(nc.allow_low_precision("bf16 matmul operands, fp32 accumulation"))
        es.enter_context(nc.allow_non_contiguous_dma("small strided state rows"))
        P = Prog(nc, es)

        def dram(name, shape, dt=F32, kind="ExternalInput"):
            return nc.dram_tensor(name, list(shape), dt, kind=kind).ap()

        xp = dram("xp", [SEQ, D])
        xs = dram("xs", [SB_, ST_, D])
        memp = dram("memp", [NMEM, D])
        ck = dram("ck", [2, SB_, NMEM, 512])
        cv = dram("cv", [2, SB_, NMEM, 512])
        st_pool = dram("st_pool", [2, SB_, 15, 512])
        st_rconv = dram("st_rconv", [2, SB_, 3, D])
        st_rh = dram("st_rh", [2, SB_, D])
        st_fconv = dram("st_fconv", [2, SB_, 2, DFF])
        consts = dram("consts", [128, 160])
        g_mix = dram("g_mix", [2, D]); w_in = dram("w_in", [2, D, 2048]); w_gate = dram("w_gate", [2, D, 3072])
        b_gate = dram("b_gate", [2, 3072]); pool_w = dram("pool_w", [2, 4, 128, 128]); pool_scale = dram("pool_scale", [2, 512])
        rnn_conv_w = dram("rnn_conv_w", [2, 4, D]); rnn_conv_b = dram("rnn_conv_b", [2, D])
        rnn_wa = dram("rnn_wa", [2, 8, 128, 128]); rnn_ba = dram("rnn_ba", [2, D])
        rnn_wx = dram("rnn_wx", [2, 8, 128, 128]); rnn_bx = dram("rnn_bx", [2, D]); rnn_lambda = dram("rnn_lambda", [2, D])
        g_mem = dram("g_mem", [2, D]); w_k = dram("w_k", [2, D, 512]); w_v = dram("w_v", [2, D, 512])
        w_br_pool = dram("w_br_pool", [2, 512, D]); w_br_rnn = dram("w_br_rnn", [2, D, D]); w_br_attn = dram("w_br_attn", [2, 512, D])
        w_out = dram("w_out", [2, D, D]); g_ffn = dram("g_ffn", [2, D]); w_up = dram("w_up", [2, D, 2 * DFF])
        ffn_conv_w = dram("ffn_conv_w", [2, 3, DFF]); ffn_conv_b = dram("ffn_conv_b", [2, DFF])
        w_down = dram("w_down", [2, DFF, D]); g_final = dram("g_final", [D])

        O = "ExternalOutput"
        y_p = dram("y_p", [SEQ, D], kind=O); y_s = dram("y_s", [SB_, ST_, D], kind=O)
        p_pool = dram("p_pool", [2, 15, 512], kind=O); p_rconv = dram("p_rconv", [2, 3, D], kind=O)
        p_rh = dram("p_rh", [2, 1, D], kind=O); p_fconv = dram("p_fconv", [2, 2, DFF], kind=O)
        p_mk = dram("p_mk", [2, NMEM, 512], kind=O); p_mv = dram("p_mv", [2, NMEM, 512], kind=O)
        s_pool = dram("s_pool", [2, SB_, 15, 512], kind=O); s_rconv = dram("s_rconv", [2, SB_, 3, D], kind=O)
        s_rh = dram("s_rh", [2, SB_, D], kind=O); s_fconv = dram("s_fconv", [2, SB_, 2, DFF], kind=O)

        blocks = []
        for i in (3, 1, 2, 0):
            blocks.append(("win%d" % i, 4096))
        blocks.append(("wk", 4096)); blocks.append(("wv", 4096))
        for f in range(8):
            blocks.append(("gg%d" % f, 3072))
            blocks.append(("gr%d" % f, 2048))
        for i in range(2):
            blocks.append(("wout%d" % i, 4096))
        for b in range(12):
            blocks.append(("wup%d" % b, 4096))
        for f in range(8):
            blocks.append(("wdn%d" % f, 3072))
        boff = {}
        off = 0
        for n, c in blocks:
            boff[n] = (off, c)
            off += c
        LCOLS = off
        wsc = nc.dram_tensor("wsc", [2, 128, LCOLS], BF16, kind=WSC_KIND).ap()

        def sb(name, shape, dt=F32):
            return es.enter_context(nc.sbuf_tensor(name, list(shape), dt))

        cst = sb("cst", [128, 160])
        ident = cst[:, 0:128]
        invcnt = cst[:, 128:143]
        c_one = cst[:, 143:144]
        c_eps = cst[:, 144:145]
        onesD = sb("onesD", [128, 128], BF16)
        ones1 = sb("ones1", [128, 128], BF16)
        NPV = 440
        pv = sb("pv", [128, NPV])
        clam = sb("clam", [128, 2, 8])
        clam2 = sb("clam2", [128, 2, 8])
        ltmp = sb("ltmp", [128, 8])
        poolw = sb("poolw", [128, 2, 4, 128], BF16)
        rwa = sb("rwa", [128, 2, 8, 128], BF16)
        rwx = sb("rwx", [128, 2, 8, 128], BF16)
        kT = sb("kT", [128, 2, 4, NMEM], BF16)
        vtm = sb("vtm", [128, 2, 2, 512], BF16)
        h_pool = sb("h_pool", [128, 2, 4, 15])
        h_rnn = sb("h_rnn", [128, 2, 8, 3])
        h_h = sb("h_h", [128, 2, 8])
        h_ffn = sb("h_ffn", [128, 2, 24, 2])
        def view(reg, off, shape, dt):
            n = 1
            for s_ in shape[1:]:
                n *= s_
            nb = n * (4 if dt == F32 else 2)
            ap = reg[:, off // 2:(off + nb) // 2]
            if dt == F32:
                ap = ap.bitcast(F32)
            if len(shape) == 3:
                ap = ap.rearrange("p (a b) -> p a b", a=shape[1])
            return ap

        hs_ffn = sb("hs_ffn", [128, 24, 32])
        hT = sb("hT", [128, 8, NT])
        xn = sb("xn", [128, 8, NT], BF16)
        xsq = sb("xsq", [128, 2, NT], BF16)
        rstd = sb("rstd", [128, NT])
        lnb = sb("lnb", [128, NT])
        PEXT = max(15 + NT, 23 * 16)
        REXT = max(3 + NT, 11 * 16)
        FEXT = max(2 + NT, 10 * 16)
        NRB = 2
        NFB = 2
        al = lambda x: (x + 63) // 64 * 64
        oA = {}
        o = 0
        for nm, nb in (("upx", 4 * PEXT * 4), ("urx", 8 * REXT * 4), ("qb", 4 * NT * 2), ("sA", PEXT * 4), ("sB", PEXT * 4),
                       ("dpl", 2 * NT * 2)):
            oA[nm] = o
            o = al(o + nb)
        szA = o
        o = 0
        for nm, nb in (("actb", 24 * NT * 2), ("fext0", FEXT * 4), ("fext1", FEXT * 4), ("fgc0", NT * 4), ("fgc1", NT * 4)):
            oA[nm] = o
            o = al(o + nb)
        szA = max(szA, o)
        regA = sb("regA", [128, szA // 2], BF16)
        upx = view(regA, oA["upx"], [128, 4, PEXT], F32)
        urx = view(regA, oA["urx"], [128, 8, REXT], F32)
        qb = view(regA, oA["qb"], [128, 4, NT], BF16)
        sA = view(regA, oA["sA"], [128, PEXT], F32)
        sB = view(regA, oA["sB"], [128, PEXT], F32)
        dpl = view(regA, oA["dpl"], [128, 2, NT], BF16)
        actb = view(regA, oA["actb"], [128, 24, NT], BF16)
        f_ext = [view(regA, oA["fext%d" % i], [128, FEXT], F32) for i in range(NFB)]
        f_gc = [view(regA, oA["fgc%d" % i], [128, NT], F32) for i in range(NFB)]
        ypool = sb("ypool", [128, 4, NT], BF16)
        yrnn = sb("yrnn", [128, 8, NT], BF16)
        yattn = sb("yattn", [128, 4, NT], BF16)
        r_xc = [sb("r_xc%d" % i, [128, NT]) for i in range(NRB)]
        r_xb = [sb("r_xb%d" % i, [128, NT], BF16) for i in range(NRB)]
        r_r = [sb("r_r%d" % i, [128, NT]) for i in range(NRB)]
        r_t = [sb("r_t%d" % i, [128, NT]) for i in range(NRB)]
        r_i = [sb("r_i%d" % i, [128, NT]) for i in range(NRB)]
        r_h = [sb("r_h%d" % i, [128, NT]) for i in range(NRB)]
        f_ge = r_xc
        regP = sb("regP", [128, 8192 // 2], BF16)
        PTb = [view(regP, i * 2048, [128, 2, 512], BF16) for i in range(2)]
        rsb = [view(regP, 4096 + i * 2048, [128, 512], F32) for i in range(2)]
        hs_pool = view(regP, 0, [128, 4, 240], F32)
        hs_rnn = view(regP, 6144, [128, 8, 48], F32)
        hs_h = view(regP, 6144 + 1536, [128, 8, 16], F32)
        NGS = 3
        regM = sb("regM", [128, (8 * NT * 2 + NGS * NT * 4) // 2], BF16)
        merged = view(regM, 0, [128, 8, NT], BF16)
        gsb = [view(regM, 8 * NT * 2 + i * NT * 4, [128, NT], F32) for i in range(NGS)]
        Kst = [view(regM, i * 4096, [128, 2, 512], F32) for i in range(2)]
        Vst = [view(regM, 8192, [128, 2, 512], BF16)]
        kTb = [view(regM, 10240, [128, 4, NMEM], BF16)]
        PTs = [view(regM, 12288 + i * 128, [128, 64], BF16) for i in range(2)]
        macc = sb("macc", [128, NT])
        mtmp = sb("mtmp", [128, NT])
        xr = sb("xr", [128, 2, D])
        yrow = sb("yrow", [128, D])
        orow = [sb("orow%d" % i, [128, 512]) for i in range(2)]
        kvrow = orow
        ring = [sb("ring%d" % i, [128, SLOTC], BF16) for i in range(NSLOT)]
        banks = [es.enter_context(nc.psum_tensor("psb%d" % i, [128, 512], F32)) for i in range(8)]

        bstate = {"next": 0, "held": set()}

        def bank_free(i):
            if i in bstate["held"]:
                return False
            key = ("bank", i)
            if P.lastw.get(key) is None:
                return True
            return len(P.readers.get(key, {})) > 0

        def bank(hold=False):
            tries = 0
            while True:
                for _ in range(8):
                    i = bstate["next"]
                    bstate["next"] = (i + 1) % 8
                    if bank_free(i):
                        if hold:
                            bstate["held"].add(i)
                        return i
                tries += 1
                if getattr(P, "_coop", None) is None or tries > 5000:
                    raise RuntimeError("no psum bank available (tries=%d)" % tries)
                P._coop_yield()

        def unhold(i):
            bstate["held"].discard(i)

        def BK(i):
            return ("bank", i)

        def mm_group(bi, out_ap, pairs, reads, extra_writes=()):
            n = len(pairs)

            def fn(e):
                inst = None
                for k, (l, r) in enumerate(pairs):
                    inst = e.matmul(out_ap, l, r, start=(k == 0), stop=(k == n - 1))
                return inst
            P.op("pe", fn, reads=reads, writes=(BK(bi),) + tuple(extra_writes))

        def transposes(bi, items, reads):
            def fn(e):
                inst = None
                for (o, i_, idn) in items:
                    inst = e.transpose(o, i_, idn)
                return inst
            P.op("pe", fn, reads=reads, writes=(BK(bi),))

        def act(out, in_, func, reads, writes, bias=None, scale=None):
            kw = {}
            if bias is not None:
                kw["bias"] = bias
            if scale is not None:
                kw["scale"] = scale
            P.op("act", lambda e: e.activation(out=out, in_=in_, func=func, **kw), reads=reads, writes=writes)

        def acopy(out, in_, reads, writes):
            P.op("act", lambda e: e.copy(out=out, in_=in_), reads=reads, writes=writes)

        def vcopy(out, in_, reads, writes):
            P.op("dve", lambda e: e.tensor_copy(out=out, in_=in_), reads=reads, writes=writes)

        pool_ok = {"v": False}

        def pcopy(out, in_, reads, writes, alt="dve"):
            if pool_ok["v"]:
                P.op("pool", lambda e: e.tensor_copy(out=out, in_=in_), reads=reads, writes=writes)
            elif alt == "act":
                acopy(out, in_, reads, writes)
            else:
                vcopy(out, in_, reads, writes)

        cp_toggle = {"i": 0}

        def anycopy(out, in_, reads, writes):
            cp_toggle["i"] ^= 1
            if cp_toggle["i"]:
                acopy(out, in_, reads, writes)
            else:
                vcopy(out, in_, reads, writes)

        def vtt(out, in0, in1, op, reads, writes):
            P.op("dve", lambda e: e.tensor_tensor(out=out, in0=in0, in1=in1, op=op), reads=reads, writes=writes)

        def vts(out, in0, s1, s2, op0, op1, reads, writes):
            if op1 is None:
                P.op("dve", lambda e: e.tensor_scalar(out=out, in0=in0, scalar1=s1, scalar2=None, op0=op0),
                     reads=reads, writes=writes)
            else:
                P.op("dve", lambda e: e.tensor_scalar(out=out, in0=in0, scalar1=s1, scalar2=s2, op0=op0, op1=op1),
                     reads=reads, writes=writes)

        def vstt(out, in0, scalar, in1, op0, op1, reads, writes):
            P.op("dve", lambda e: e.scalar_tensor_tensor(out=out, in0=in0, scalar=scalar, in1=in1, op0=op0, op1=op1),
                 reads=reads, writes=writes)

        P.dma("sp", cst[:, :], consts, writes=["cst"])
        P.op("dve", lambda e: e.memset(onesD[:, :], 1.0 / D), writes=["onesD"])
        P.op("dve", lambda e: e.memset(ones1[:, :], 1.0), writes=["ones1"])
        P.op("dve", lambda e: e.memset(h_pool[:, :, :, :], 0.0), writes=["h_pool0", "h_pool1"])
        P.op("dve", lambda e: e.memset(h_rnn[:, :, :, :], 0.0), writes=["h_rnn0", "h_rnn1"])
        P.op("dve", lambda e: e.memset(h_h[:, :, :], 0.0), writes=["h_h0", "h_h1"])
        P.op("dve", lambda e: e.memset(h_ffn[:, :, :, :], 0.0), writes=["h_ffn0", "h_ffn1"])

        plist = []
        for l in range(2):
            plist += [
                (("g_mix", l), g_mix[l].rearrange("(c p) -> c p", p=128), 8),
                (("g_ffn", l), g_ffn[l].rearrange("(c p) -> c p", p=128), 8),
                (("b_gate", l), b_gate[l].rearrange("(c p) -> c p", p=128), 24),
                (("pool_scale", l), pool_scale[l].rearrange("(c p) -> c p", p=128), 4),
                (("rnn_conv_w", l), rnn_conv_w[l].rearrange("j (c p) -> (j c) p", p=128), 32),
                (("rnn_conv_b", l), rnn_conv_b[l].rearrange("(c p) -> c p", p=128), 8),
                (("rnn_ba", l), rnn_ba[l].rearrange("(c p) -> c p", p=128), 8),
                (("rnn_bx", l), rnn_bx[l].rearrange("(c p) -> c p", p=128), 8),
                (("rnn_lambda", l), rnn_lambda[l].rearrange("(c p) -> c p", p=128), 8),
                (("g_mem", l), g_mem[l].rearrange("(c p) -> c p", p=128), 8),
                (("ffn_conv_w", l), ffn_conv_w[l].rearrange("j (c p) -> (j c) p", p=128), 72),
                (("ffn_conv_b", l), ffn_conv_b[l].rearrange("(c p) -> c p", p=128), 24),
            ]
        plist.append((("g_final", 0), g_final.rearrange("(c p) -> c p", p=128), 8))
        pcol = {}
        col = 0
        segs = []
        for key, ap, C in plist:
            pcol[key] = col
            done = 0
            while done < C:
                ti, r0 = divmod(col + done, 128)
                n = min(C - done, 128 - r0)
                segs.append((ti, r0, n, ap[done:done + n, :]))
                done += n
            col += C
        assert col <= NPV
        nstage = (col + 127) // 128
        for (ti, r0, n, src) in segs:
            P.dma("sp", xr[r0:r0 + n, 0, ti * 128:(ti + 1) * 128], src, writes=[("pstage", ti)])
        for ti in range(nstage):
            R = min(128, col - ti * 128)
            bi = bank()
            transposes(bi, [(banks[bi][:, 0:R], xr[0:R, 0, ti * 128:(ti + 1) * 128], ident[0:R, 0:R])],
                       reads=[("pstage", ti), "cst"])
            vcopy(pv[:, ti * 128:ti * 128 + R], banks[bi][:, 0:R], [BK(bi)], ["pv"])
        P.op("dve", lambda e: e.memset(ltmp[:, :], 0.0), reads=[("pstage", t) for t in range(nstage)] + ["pv"],
             writes=["xr", "ltmp"])

        def PVc(name, l, c, n=1):
            o = pcol[(name, l)] + c
            return pv[:, o:o + n]

        for l in range(2):
            act(ltmp[:, :], PVc("rnn_lambda", l, 0, 8), AF.Exp, ["pv", "ltmp"], ["ltmp"], scale=-1.0)
            act(ltmp[:, :], ltmp[:, :], AF.Ln, ["ltmp", "cst"], ["ltmp"], bias=c_one)
            P.op("act", lambda e, l=l: e.activation(out=clam[:, l, :], in_=ltmp[:, :], func=AF.Copy, scale=-8.0),
                 reads=["ltmp"], writes=["clam"])
            P.op("act", lambda e, l=l: e.activation(out=clam2[:, l, :], in_=ltmp[:, :], func=AF.Copy, scale=-16.0),
                 reads=["ltmp"], writes=["clam"])

        for l in range(2):
            P.dma("pool", poolw[:, l, :, :], pool_w[l].rearrange("g c d -> c g d"), writes=[("poolw", l)])
            P.dma("pool", rwa[:, l, :, :], rnn_wa[l].rearrange("g c d -> c g d"), writes=[("rwa", l)])
            P.dma("pool", rwx[:, l, :, :], rnn_wx[l].rearrange("g c d -> c g d"), writes=[("rwx", l)])

        def conv_block(l, name):
            o, c = boff[name]
            dst = wsc[l][:, o:o + c]
            res = ("wsc", l, name)
            parts = []
            if name.startswith("win"):
                i = int(name[3:])
                parts.append((dst.rearrange("p (k f) -> p k f", k=8),
                              w_in[l][:, i * 512:(i + 1) * 512].rearrange("(k p) f -> p k f", p=128)))
            elif name == "wk":
                parts.append((dst.rearrange("p (k f) -> p k f", k=8), w_k[l].rearrange("(k p) f -> p k f", p=128)))
            elif name == "wv":
                parts.append((dst.rearrange("p (k f) -> p k f", k=8), w_v[l].rearrange("(k p) f -> p k f", p=128)))
            elif name.startswith("gg"):
                f = int(name[2:])
                d3 = dst.rearrange("p (k c) -> p k c", c=128)
                for br in range(3):
                    parts.append((d3[:, br * 8:(br + 1) * 8, :],
                                  w_gate[l][:, br * 1024 + f * 128: br * 1024 + (f + 1) * 128].rearrange("(k p) c -> p k c", p=128)))
            elif name.startswith("gr"):
                f = int(name[2:])
                d3 = dst.rearrange("p (k c) -> p k c", c=128)
                parts.append((d3[:, 0:4, :], w_br_pool[l][:, f * 128:(f + 1) * 128].rearrange("(k p) c -> p k c", p=128)))
                parts.append((d3[:, 4:12, :], w_br_rnn[l][:, f * 128:(f + 1) * 128].rearrange("(k p) c -> p k c", p=128)))
                parts.append((d3[:, 12:16, :], w_br_attn[l][:, f * 128:(f + 1) * 128].rearrange("(k p) c -> p k c", p=128)))
            elif name.startswith("wout"):
                i = int(name[4:])
                parts.append((dst.rearrange("p (k f) -> p k f", k=8),
                              w_out[l][:, i * 512:(i + 1) * 512].rearrange("(k p) f -> p k f", p=128)))
            elif name.startswith("wup"):
                b = int(name[3:])
                d4 = dst.rearrange("p (g k f) -> p g k f", g=2, k=8)
                for gv in range(2):
                    parts.append((d4[:, gv, :, :],
                                  w_up[l][:, gv * DFF + b * 256: gv * DFF + (b + 1) * 256].rearrange("(k p) f -> p k f", p=128)))
            elif name.startswith("wdn"):
                f = int(name[3:])
                parts.append((dst.rearrange("p (k c) -> p k c", c=128),
                              w_down[l][:, f * 128:(f + 1) * 128].rearrange("(k p) c -> p k c", p=128)))
            for pi, (d_, s_) in enumerate(parts):
                P.dma("pool", d_, s_, writes=[res + (pi,)])
            return [res + (pi,) for pi in range(len(parts))]

        wsc_parts = {}
        if stage < 1:
            conv_order_skip = True
        else:
            conv_order_skip = False
        conv_order = [(0, "wk"), (0, "wv"), (1, "wk"), (1, "wv")]
        for l in range(2):
            for n, _ in blocks:
                if n not in ("wk", "wv"):
                    conv_order.append((l, n))
        for (l, n) in conv_order:
            wsc_parts[(l, n)] = [] if (conv_order_skip or (conv_only is not None and not n.startswith(conv_only))) else conv_block(l, n)

        tiles = [("p", i) for i in range(NPT)] + [("s", 0)]
        seq = [(0, "wk"), (0, "wv"), (1, "wk"), (1, "wv")]
        for tl in tiles:
            for l in range(2):
                for n, _ in blocks:
                    if n not in ("wk", "wv"):
                        seq.append((l, n))
        wst = {"loaded": 0, "pos": 0}

        def w_issue(k):
            l, n = seq[k]
            o, c = boff[n]
            s = k % NSLOT
            P.dma("sp", ring[s][:, 0:c], wsc[l][:, o:o + c], reads=wsc_parts[(l, n)], writes=[("slot", s)])

        def w_get(name_expected, l_expected):
            k = wst["pos"]
            assert seq[k] == (l_expected, name_expected), (seq[k], l_expected, name_expected)
            while wst["loaded"] <= min(k + NSLOT - 1, len(seq) - 1) and wst["loaded"] < k + NSLOT:
                if wst["loaded"] >= len(seq):
                    break
                w_issue(wst["loaded"])
                wst["loaded"] += 1
            s = k % NSLOT
            return ring[s], ("slot", s)

        def w_done():
            wst["pos"] += 1
            k = wst["pos"]
            nxt = k - 1 + NSLOT
            if nxt < len(seq) and wst["loaded"] == nxt:
                w_issue(nxt)
                wst["loaded"] += 1

        def rmsnorm(src, src_res, N, gname, l, out_fn, out_res):
            bi = bank()
            for c in range(8):
                q = c % 2
                act(xsq[:, q, 0:N], src(c), AF.Square, [src_res(c), ("xsq", q)], [("xsq", q)])
                P.op("pe", lambda e, c=c, q=q, bi=bi: e.matmul(banks[bi][:, 0:N], onesD[:, :], xsq[:, q, 0:N],
                                                              start=(c == 0), stop=(c == 7)),
                     reads=[("xsq", q), "onesD"], writes=[BK(bi)])
            act(lnb[:, 0:N], banks[bi][:, 0:N], AF.Ln, [BK(bi), "cst"], ["lnb"], bias=c_eps)
            act(rstd[:, 0:N], lnb[:, 0:N], AF.Exp, ["lnb"], ["rstd"], scale=-0.5)
            for c in range(8):
                vstt(out_fn(c), src(c), PVc(gname, l, c), rstd[:, 0:N], ALU.mult, ALU.mult,
                     [src_res(c), "pv", "rstd"], [out_res(c)])

        class RowOut:
            def __init__(self, R, nchunks, dst_fn):
                self.R, self.n, self.dst_fn = R, nchunks, dst_fn
                self.bi = None
                self.k = 0
                self.flip = 0

            def add(self, c, src_ap, src_reads):
                P._noyield = getattr(P, "_noyield", 0) + 1
                try:
                    self._add(c, src_ap, src_reads)
                finally:
                    P._noyield -= 1

            def _add(self, c, src_ap, src_reads):
                R = self.R
                if self.bi is None:
                    self.bi = bank(hold=True)
                bi = self.bi
                j = c % 4
                transposes(bi, [(banks[bi][0:R, j * 128:(j + 1) * 128], src_ap, ident)], reads=list(src_reads) + ["cst"])
                self.k += 1
                if j == 3 or c == self.n - 1:
                    grp = c // 4
                    w = (j + 1) * 128
                    ob = orow[self.flip]
                    ores = ("orow", self.flip)
                    self.flip ^= 1
                    anycopy(ob[0:R, 0:w], banks[bi][0:R, 0:w], [BK(bi)], [ores])
                    for (dap, r0, nr) in self.dst_fn(grp, w):
                        P.dma("sp", dap, ob[r0:r0 + nr, 0:w], reads=[ores], writes=[])
                    unhold(bi)
                    self.bi = None

        def load_fm(rows_ap, rows_res, R, nch, dst_fn, dst_res):
            c = 0
            while c < nch:
                g = min(4, nch - c)
                bi = bank()
                transposes(bi, [(banks[bi][:, j * 128:j * 128 + R], rows_ap[0:R, (c + j) * 128:(c + j + 1) * 128],
                                 ident[0:R, 0:R]) for j in range(g)], reads=[rows_res, "cst"])
                cp_toggle["i"] ^= 1
                cpf = acopy if cp_toggle["i"] else vcopy
                for j in range(g):
                    cpf(dst_fn(c + j), banks[bi][:, j * 128:j * 128 + R], [BK(bi)], [dst_res(c + j)])
                c += g

        if stage < 2:
            P.finalize()
            return nc
        P.dma("sp", xr[:, 0:2, :], memp.rearrange("(a p) d -> p a d", p=128), reads=[], writes=["xr"])
        for a in range(2):
            for half in range(2):
                bi = bank()
                transposes(bi, [(banks[bi][:, k * 128:(k + 1) * 128], xr[:, a, (half * 4 + k) * 128:(half * 4 + k + 1) * 128], ident)
                                for k in range(4)], reads=["xr", "cst"])
                anycopy(hT[:, half * 4:(half + 1) * 4, a * 128:(a + 1) * 128],
                        banks[bi][:, :].rearrange("p (k t) -> p k t", k=4), [BK(bi)],
                        [("hT", c) for c in range(half * 4, half * 4 + 4)])
        for l in range(2):
            rmsnorm(lambda c: hT[:, c, 0:NMEM], lambda c: ("hT", c), NMEM, "g_mem", l,
                    lambda c: xn[:, c, 0:NMEM], lambda c: ("xn", c))
            wk_t, wk_r = w_get("wk", l)
            wk3 = wk_t[:, 0:4096].rearrange("p (k f) -> p k f", k=8)
            xnr = [("xn", c) for c in range(8)]
            for h in range(4):
                bi = bank()
                mm_group(bi, banks[bi][:, 0:NMEM], [(wk3[:, k, h * 128:(h + 1) * 128], xn[:, k, 0:NMEM]) for k in range(8)],
                         reads=[wk_r] + xnr)
                anycopy(kT[:, l, h, :], banks[bi][:, 0:NMEM], [BK(bi)], [("kT", l)])
            for mc in range(2):
                bi = bank()
                mm_group(bi, banks[bi][:, :], [(xn[:, k, mc * 128:(mc + 1) * 128], wk3[:, k, :]) for k in range(8)],
                         reads=[wk_r] + xnr)
                acopy(kvrow[mc][:, :], banks[bi][:, :], [BK(bi)], [("orow", mc)])
                P.dma("sp", p_mk[l][mc * 128:(mc + 1) * 128, :], kvrow[mc][:, :], reads=[("orow", mc)])
            w_done()
            wv_t, wv_r = w_get("wv", l)
            wv3 = wv_t[:, 0:4096].rearrange("p (k f) -> p k f", k=8)
            for mc in range(2):
                bi = bank()
                mm_group(bi, banks[bi][:, :], [(xn[:, k, mc * 128:(mc + 1) * 128], wv3[:, k, :]) for k in range(8)],
                         reads=[wv_r] + xnr)
                acopy(kvrow[mc][:, :], banks[bi][:, :], [BK(bi)], [("orow", mc)])
                acopy(vtm[:, l, mc, :], banks[bi][:, :], [BK(bi)], [("vtm", l)])
                P.dma("sp", p_mv[l][mc * 128:(mc + 1) * 128, :], kvrow[mc][:, :], reads=[("orow", mc)])
            w_done()

        KEYS_A = ([("upx", g) for g in range(4)] + [("urx", n) for n in range(8)] + [("qb", h) for h in range(4)]
                  + ["sA", "sB", ("dpl", 0), ("dpl", 1)] + [("xc", q) for q in range(NRB)]
                  + [("actb", j) for j in range(24)] + [(nm, q) for nm in ("fext", "fgc", "fge") for q in range(NFB)])
        KEYS_M = ([("merged", f) for f in range(8)] + [("gs", i) for i in range(NGS)] + [("Kst", 0), ("Kst", 1), ("Vst", 0),
                  ("kTb", 0, 0), ("kTb", 0, 1), ("PTs", 0), ("PTs", 1)])
        KEYS_P = ([("PT", i, mc) for i in range(2) for mc in range(2)] + [("rs", 0), ("rs", 1)]
                  + [("hs_pool", c) for c in range(4)] + [("hs_rnn", c) for c in range(8)] + [("hs_h", c) for c in range(8)])

        def barrier(keys):
            P.op("dve", lambda e: e.memset(ltmp[:, :], 0.0), reads=[], writes=list(keys) + ["ltmp"])

        def tile_layer(kind, ti, l):
            samp = (kind == "s")
            N = 128 if samp else NT
            inner = 16 if samp else 1
            T = N // inner
            import os as _os2
            first = (not samp) and (ti == 0 or _os2.environ.get('FORCE_FIRST') == '1')
            last = (not samp) and ti == NPT - 1
            hTr = [("hT", c) for c in range(8)]
            xnr = [("xn", c) for c in range(8)]
            HP, HR, HF = 15 * inner, 3 * inner, 2 * inner

            barrier(KEYS_A + KEYS_M + KEYS_P)
            if samp:
                for j in range(15):
                    rr = (j % 8) * 16
                    if j % 8 == 0:
                        pass
                    P.dma("sp", xr[rr:rr + 16, 0 if j < 8 else 1, 0:512], st_pool[l][:, j, :], reads=["xr"], writes=[("xrs", j)])
                P.op("dve", lambda e: e.memset(ltmp[:, :], 0.0), reads=[("xrs", j) for j in range(15)] + ["xr"], writes=["xr", "ltmp"])
                load_fm(xr[:, 0, :], "xr", 128, 4, lambda c: hs_pool[:, c, 0:128], lambda c: ("hs_pool", c))
                load_fm(xr[:, 1, :], "xr", 112, 4, lambda c: hs_pool[:, c, 128:240], lambda c: ("hs_pool", c))
                P.dma("sp", s_pool[l][:, 0:7, :], st_pool[l][:, 8:15, :])
                P.op("dve", lambda e: e.memset(ltmp[:, :], 0.0), reads=["xr", ("hs_pool", 0), ("hs_pool", 1), ("hs_pool", 2), ("hs_pool", 3)],
                     writes=["xr", "ltmp"])
                for j in range(3):
                    P.dma("sp", xr[j * 16:(j + 1) * 16, 0, :], st_rconv[l][:, j, :], reads=["xr"], writes=[("xrs", j)])
                P.dma("sp", xr[0:16, 1, :], st_rh[l][:, :], reads=["xr"], writes=[("xrs", 3)])
                P.op("dve", lambda e: e.memset(ltmp[:, :], 0.0), reads=[("xrs", j) for j in range(4)] + ["xr"], writes=["xr", "ltmp"])
                load_fm(xr[:, 0, :], "xr", 48, 8, lambda c: hs_rnn[:, c, :], lambda c: ("hs_rnn", c))
                load_fm(xr[:, 1, :], "xr", 16, 8, lambda c: hs_h[:, c, :], lambda c: ("hs_h", c))
                P.op("dve", lambda e: e.memset(ltmp[:, :], 0.0), reads=["xr"] + [("hs_rnn", c) for c in range(8)] + [("hs_h", c) for c in range(8)],
                     writes=["xr", "ltmp"])
                for part in range(3):
                    for j in range(2):
                        P.dma("sp", xr[j * 16:(j + 1) * 16, 0, :], st_fconv[l][:, j, part * 1024:(part + 1) * 1024],
                              reads=["xr"], writes=[("xrs", j)])
                    P.op("dve", lambda e: e.memset(ltmp[:, :], 0.0), reads=[("xrs", 0), ("xrs", 1), "xr"], writes=["xr", "ltmp"])
                    load_fm(xr[:, 0, :], "xr", 32, 8, lambda c, part=part: hs_ffn[:, part * 8 + c, :],
                            lambda c, part=part: ("hs_ffn", part * 8 + c))
                    P.op("dve", lambda e: e.memset(ltmp[:, :], 0.0), reads=["xr"] + [("hs_ffn", part * 8 + c) for c in range(8)],
                         writes=["xr", "ltmp"])

            rmsnorm(lambda c: hT[:, c, 0:N], lambda c: ("hT", c), N, "g_mix", l,
                    lambda c: xn[:, c, 0:N], lambda c: ("xn", c))

            for zb in (3, 1, 2, 0):
                wt, wr = w_get("win%d" % zb, l)
                w3 = wt[:, 0:4096].rearrange("p (k f) -> p k f", k=8)
                for zi in range(4):
                    zc = zb * 4 + zi
                    bi = bank()
                    mm_group(bi, banks[bi][:, 0:N], [(w3[:, k, zi * 128:(zi + 1) * 128], xn[:, k, 0:N]) for k in range(8)],
                             reads=[wr] + xnr)
                    if zc < 4:
                        anycopy(upx[:, zc, HP:HP + N], banks[bi][:, 0:N], [BK(bi)], [("upx", zc)])
                    elif zc < 12:
                        anycopy(urx[:, zc - 4, HR:HR + N], banks[bi][:, 0:N], [BK(bi)], [("urx", zc - 4)])
                    else:
                        h = zc - 12
                        if samp:
                            o_ = qb[:, h, 0:N].rearrange("p (b t) -> p t b", t=ST_)
                            i_ = banks[bi][:, 0:N].rearrange("p (t b) -> p t b", b=SB_)
                        else:
                            o_ = qb[:, h, 0:N]
                            i_ = banks[bi][:, 0:N]
                        P.op("act", lambda e, o_=o_, i_=i_: e.activation(out=o_, in_=i_, func=AF.Copy, scale=128.0 ** -0.5),
                             reads=[BK(bi)], writes=[("qb", h)])
                w_done()

            for g in range(4):
                if samp:
                    vcopy(upx[:, g, 0:HP], hs_pool[:, g, :], [("hs_pool", g)], [("upx", g)])
                else:
                    vcopy(upx[:, g, 0:HP], h_pool[:, l, g, :], [("h_pool%d" % l)], [("upx", g)])
            for n in range(8):
                if samp:
                    vcopy(urx[:, n, 0:HR], hs_rnn[:, n, :], [("hs_rnn", n)], [("urx", n)])
                else:
                    vcopy(urx[:, n, 0:HR], h_rnn[:, l, n, :], [("h_rnn%d" % l)], [("urx", n)])

            def gen_attn():
                if not samp:
                    for h in range(4):
                        pt = PTb[h % 2]
                        ptr = ("PT", h % 2)
                        for mc in range(2):
                            bi = bank()
                            mm_group(bi, banks[bi][:, 0:N], [(kT[:, l, h, mc * 128:(mc + 1) * 128], qb[:, h, 0:N])],
                                     reads=[("kT", l), ("qb", h)])
                            act(pt[:, mc, 0:N], banks[bi][:, 0:N], AF.Exp, [BK(bi)], [ptr + (mc,)])
                        bo = bank()
                        mm_group(bo, banks[bo][:, 0:N], [(vtm[:, l, mc, h * 128:(h + 1) * 128], pt[:, mc, 0:N]) for mc in range(2)],
                                 reads=[("vtm", l), ptr + (0,), ptr + (1,)])
                        bd = bank()
                        mm_group(bd, banks[bd][:, 0:N], [(ones1[:, :], pt[:, mc, 0:N]) for mc in range(2)],
                                 reads=["ones1", ptr + (0,), ptr + (1,)])
                        rs = rsb[h % 2]
                        rsr = ("rs", h % 2)
                        act(rs[:, 0:N], banks[bd][:, 0:N], AF.Ln, [BK(bd)], [rsr])
                        act(rs[:, 0:N], rs[:, 0:N], AF.Exp, [rsr], [rsr], scale=-1.0)
                        vtt(yattn[:, h, 0:N], banks[bo][:, 0:N], rs[:, 0:N], ALU.mult, [BK(bo), rsr], [("yattn", h)])
                        yield
                else:
                    barrier(KEYS_M)
                    bO = bank(hold=True)
                    bD = bank(hold=True)
                    for b in range(SB_):
                        ks = Kst[b % 2]; vs = Vst[0]; kb = kTb[0]; ps = PTs[b % 2]
                        P.dma("sp", ks[:, :, :], ck[l][b].rearrange("(a p) f -> p a f", p=128), writes=[("Kst", b % 2)])
                        P.dma("pool", vs[:, :, :], cv[l][b].rearrange("(a p) f -> p a f", p=128), writes=[("Vst", 0)])
                        for hp in range(2):
                            bi = bank()
                            transposes(bi, [(banks[bi][:, (hh * 2 + mc) * 128:(hh * 2 + mc + 1) * 128],
                                             ks[:, mc, (hp * 2 + hh) * 128:(hp * 2 + hh + 1) * 128], ident)
                                            for hh in range(2) for mc in range(2)], reads=[("Kst", b % 2), "cst"])
                            anycopy(kb[:, hp * 2:hp * 2 + 2, :], banks[bi][:, :].rearrange("p (h m) -> p h m", h=2),
                                    [BK(bi)], [("kTb", 0, hp)])
                        bs = bank()
                        def fn_s(e, kb=kb, bs=bs, b=b):
                            inst = None
                            for mc in range(2):
                                for h in range(4):
                                    inst = e.matmul(banks[bs][:, (mc * 4 + h) * 8:(mc * 4 + h + 1) * 8],
                                                    kb[:, h, mc * 128:(mc + 1) * 128], qb[:, h, b * 8:(b + 1) * 8],
                                                    start=True, stop=True)
                            return inst
                        P.op("pe", fn_s, reads=[("kTb", 0, 0), ("kTb", 0, 1)] + [("qb", h) for h in range(4)], writes=[BK(bs)])
                        act(ps[:, 0:64], banks[bs][:, 0:64], AF.Exp, [BK(bs)], [("PTs", b % 2)])

                        def fn_o(e, vs=vs, ps=ps, b=b):
                            inst = None
                            for h in range(4):
                                for mc in range(2):
                                    inst = e.matmul(banks[bO][:, h * 128 + b * 8: h * 128 + (b + 1) * 8],
                                                    vs[:, mc, h * 128:(h + 1) * 128], ps[:, (mc * 4 + h) * 8:(mc * 4 + h + 1) * 8],
                                                    start=(mc == 0), stop=(mc == 1))
                            return inst
                        P.op("pe", fn_o, reads=[("Vst", 0), ("PTs", b % 2)], writes=[BK(bO)])

                        def fn_d(e, ps=ps, b=b):
                            inst = None
                            for mc in range(2):
                                inst = e.matmul(banks[bD][:, b * 32:(b + 1) * 32], ones1[:, :], ps[:, mc * 32:(mc + 1) * 32],
                                                start=(mc == 0), stop=(mc == 1))
                            return inst
                        P.op("pe", fn_d, reads=["ones1", ("PTs", b % 2)], writes=[BK(bD)])
                    rs = rsb[0]
                    act(rs[:, 0:512], banks[bD][:, :], AF.Ln, [BK(bD)], [("rs", 0)])
                    act(rs[:, 0:512], rs[:, 0:512], AF.Exp, [("rs", 0)], [("rs", 0)], scale=-1.0)
                    rs4 = rs[:, 0:512].rearrange("p (b h t) -> p h b t", h=4, t=ST_)
                    for h in range(4):
                        vtt(yattn[:, h, 0:N].rearrange("p (t b) -> p b t", b=SB_),
                            banks[bO][:, h * 128:(h + 1) * 128].rearrange("p (b t) -> p b t", t=ST_),
                            rs4[:, h, :, :], ALU.mult, [BK(bO), ("rs", 0)], [("yattn", h)])
                    unhold(bO); unhold(bD)
                    barrier(KEYS_M)

                yield
            def gen_pool():
                if samp:
                    ro_pool = RowOut(128, 4, lambda grp, w: [(s_pool[l][:, 7 + t, :], t * 16, 16) for t in range(ST_)])
                elif last:
                    ro_pool = RowOut(15, 4, lambda grp, w: [(p_pool[l][:, :], 0, 15)])
                else:
                    ro_pool = None
                for g in range(4):
                    wdw = 2 ** (g + 1)
                    e_ = upx[:, g, :]
                    er = ("upx", g)
                    L = HP + N
                    if g == 0:
                        vtt(sA[:, HP:L], e_[:, HP:L], e_[:, HP - inner:L - inner], ALU.add, [er, "sA"], ["sA"])
                        win = sA
                        winr = "sA"
                    else:
                        lo = HP - (wdw - 2) * inner
                        vtt(sA[:, lo:L], e_[:, lo:L], e_[:, lo - inner:L - inner], ALU.add, [er, "sA"], ["sA"])
                        cur, curr, oth, othr = sA, "sA", sB, "sB"
                        step = 2
                        while step < wdw:
                            lo = lo + step * inner
                            vtt(oth[:, lo:L], cur[:, lo:L], cur[:, lo - step * inner:L - step * inner], ALU.add, [curr, othr], [othr])
                            cur, curr, oth, othr = oth, othr, cur, curr
                            step *= 2
                        win, winr = cur, curr
                    dq = g % 2
                    vstt(dpl[:, dq, 0:N], win[:, HP:L], 1.0 / wdw, e_[:, HP:L], ALU.mult, ALU.subtract,
                         [winr, er, ("dpl", dq)], [("dpl", dq)])
                    if first:
                        k = wdw - 1
                        vtt(mtmp[:, 0:k], win[:, HP:HP + k], invcnt[:, 0:k], ALU.mult, [winr, "cst", "mtmp"], ["mtmp"])
                        vtt(dpl[:, dq, 0:k], mtmp[:, 0:k], e_[:, HP:HP + k], ALU.subtract, ["mtmp", er, ("dpl", dq)], [("dpl", dq)])
                    bi = bank()
                    mm_group(bi, banks[bi][:, 0:N], [(poolw[:, l, g, :], dpl[:, dq, 0:N])], reads=[("poolw", l), ("dpl", dq)])
                    vts(ypool[:, g, 0:N], banks[bi][:, 0:N], PVc("pool_scale", l, g), None, ALU.mult, None,
                        [BK(bi), "pv"], [("ypool", g)])
                    if ro_pool is not None:
                        if samp:
                            ro_pool.add(g, e_[:, HP:HP + 128], [er])
                        else:
                            ro_pool.add(g, e_[:, L - 15:L], [er])
                    if not samp and not last:
                        vcopy(h_pool[:, l, g, :], e_[:, L - 15:L], [er], ["h_pool%d" % l])

                    yield
                yield
            def mk_ro_r():
                if samp:
                    ro_rc = RowOut(48, 8, lambda grp, w: [(s_rconv[l][:, t, grp * 512:grp * 512 + w], t * 16, 16) for t in range(3)])
                    ro_rh = RowOut(16, 8, lambda grp, w: [(s_rh[l][:, grp * 512:grp * 512 + w], 0, 16)])
                elif last:
                    ro_rc = RowOut(3, 8, lambda grp, w: [(p_rconv[l][:, grp * 512:grp * 512 + w], 0, 3)])
                    ro_rh = RowOut(1, 8, lambda grp, w: [(p_rh[l][:, grp * 512:grp * 512 + w], 0, 1)])
                else:
                    ro_rc = ro_rh = None
                return ro_rc, ro_rh
            ro_rc, ro_rh = mk_ro_r()

            def gen_rglru(ns=range(8)):
                cw0 = pcol[("rnn_conv_w", l)]
                for n in ns:
                    q = n % NRB
                    e_ = urx[:, n, :]
                    er = ("urx", n)
                    xc, xb, rr, tt, ii, hh = r_xc[q], r_xb[q], r_r[q], r_t[q], r_i[q], r_h[q]
                    R = lambda s, q=q: (s, q)
                    if pool_ok["v"]:
                        P.op("pool", lambda e, xc=xc, e_=e_, n=n: e.tensor_scalar(
                            out=xc[:, 0:N], in0=e_[:, 0:N], scalar1=pv[:, cw0 + n:cw0 + n + 1], scalar2=PVc("rnn_conv_b", l, n),
                            op0=ALU.mult, op1=ALU.add), reads=[er, "pv", R("xc")], writes=[R("xc")])
                    else:
                        act(xc[:, 0:N], e_[:, 0:N], AF.Identity, [er, "pv", R("xc")], [R("xc")],
                            bias=PVc("rnn_conv_b", l, n), scale=pv[:, cw0 + n:cw0 + n + 1])
                    for j in range(1, 4):
                        vstt(xc[:, 0:N], e_[:, j * inner:j * inner + N], pv[:, cw0 + j * 8 + n:cw0 + j * 8 + n + 1], xc[:, 0:N],
                             ALU.mult, ALU.add, [er, "pv", R("xc")], [R("xc")])
                    pcopy(xb[:, 0:N], xc[:, 0:N], [R("xc"), R("xb")], [R("xb")], alt="act")
                    br_ = bank()
                    mm_group(br_, banks[br_][:, 0:N], [(rwa[:, l, n, :], xb[:, 0:N])], reads=[("rwa", l), R("xb")])
                    bx_ = bank()
                    mm_group(bx_, banks[bx_][:, 0:N], [(rwx[:, l, n, :], xb[:, 0:N])], reads=[("rwx", l), R("xb")])
                    act(rr[:, 0:N], banks[br_][:, 0:N], AF.Sigmoid, [BK(br_), "pv", R("r")], [R("r")], bias=PVc("rnn_ba", l, n))
                    act(ii[:, 0:N], banks[bx_][:, 0:N], AF.Sigmoid, [BK(bx_), "pv", R("i")], [R("i")], bias=PVc("rnn_bx", l, n))
                    act(tt[:, 0:N], rr[:, 0:N], AF.Exp, [R("r"), "clam", R("t")], [R("t")], scale=clam2[:, l, n:n + 1])
                    act(rr[:, 0:N], rr[:, 0:N], AF.Exp, [R("r"), "clam"], [R("r")], scale=clam[:, l, n:n + 1])
                    act(tt[:, 0:N], tt[:, 0:N], AF.Ln, [R("t"), "cst"], [R("t")], bias=c_one, scale=-1.0)
                    act(tt[:, 0:N], tt[:, 0:N], AF.Exp, [R("t")], [R("t")], scale=0.5)
                    vtt(ii[:, 0:N], ii[:, 0:N], xc[:, 0:N], ALU.mult, [R("i"), R("xc")], [R("i")])
                    vtt(ii[:, 0:N], ii[:, 0:N], tt[:, 0:N], ALU.mult, [R("i"), R("t")], [R("i")])
                    if not samp:
                        P.op("dve", lambda e, hh=hh, rr=rr, ii=ii, n=n: e.tensor_tensor_scan(
                            out=hh[:, 0:N], data0=rr[:, 0:N], data1=ii[:, 0:N], initial=h_h[:, l, n:n + 1],
                            op0=ALU.mult, op1=ALU.add), reads=[R("r"), R("i"), "h_h%d" % l, R("h")], writes=[R("h")])
                        vcopy(h_h[:, l, n:n + 1], hh[:, N - 1:N], [R("h")], ["h_h%d" % l])
                    else:
                        for t in range(ST_):
                            prev = hs_h[:, n, :] if t == 0 else hh[:, (t - 1) * 16:t * 16]
                            vtt(hh[:, t * 16:(t + 1) * 16], rr[:, t * 16:(t + 1) * 16], prev, ALU.mult,
                                [R("r"), R("h"), ("hs_h", n)], [R("h")])
                            vtt(hh[:, t * 16:(t + 1) * 16], hh[:, t * 16:(t + 1) * 16], ii[:, t * 16:(t + 1) * 16], ALU.add,
                                [R("i"), R("h")], [R("h")])
                    pcopy(yrnn[:, n, 0:N], hh[:, 0:N], [R("h")], [("yrnn", n)], alt="act")
                    if ro_rc is not None:
                        if samp:
                            ro_rc.add(n, e_[:, HR + 5 * 16:HR + 8 * 16], [er])
                            ro_rh.add(n, hh[:, 7 * 16:8 * 16], [R("h")])
                        else:
                            ro_rc.add(n, e_[:, HR + N - 3:HR + N], [er])
                            ro_rh.add(n, hh[:, N - 1:N], [R("h")])
                    if not samp and not last:
                        vcopy(h_rnn[:, l, n, :], e_[:, HR + N - 3:HR + N], [er], ["h_rnn%d" % l])

                    yield
                yield
            def drain(g_):
                for _ in g_:
                    pass
            if (samp and not INTERLEAVE_SAMPLE) or not INTERLEAVE:
                drain(gen_attn()); drain(gen_pool()); drain(gen_rglru())
            else:
                P.interleave([lambda: drain(gen_rglru(range(0, 8, 2))), lambda: drain(gen_rglru(range(1, 8, 2))),
                              lambda: drain(gen_attn()), lambda: drain(gen_pool())])

            bcol = pcol[("b_gate", l)]
            yp_r = [("ypool", g) for g in range(4)]
            yr_r = [("yrnn", n) for n in range(8)]
            ya_r = [("yattn", h) for h in range(4)]
            gi = 0
            for f in range(8):
                wtg, wrg = w_get("gg%d" % f, l)
                wg3 = wtg[:, 0:3072].rearrange("p (k c) -> p k c", c=128)
                gates_ps = {}
                for br in (0, 2, 1):
                    bg = bank()
                    mm_group(bg, banks[bg][:, 0:N], [(wg3[:, br * 8 + k, :], xn[:, k, 0:N]) for k in range(8)], reads=[wrg] + xnr)
                    gs = gsb[gi % NGS]
                    gr = ("gs", gi % NGS)
                    gi += 1
                    act(gs[:, 0:N], banks[bg][:, 0:N], AF.Sigmoid, [BK(bg), "pv", gr], [gr],
                        bias=pv[:, bcol + br * 8 + f: bcol + br * 8 + f + 1])
                    gates_ps[br] = (gs, gr)
                w_done()
                wt, wr = w_get("gr%d" % f, l)
                w3 = wt[:, 0:2048].rearrange("p (k c) -> p k c", c=128)
                first_term = True
                for (br, k0, nk, ysrc, yres) in ((0, 0, 4, ypool, yp_r), (2, 12, 4, yattn, ya_r), (1, 4, 8, yrnn, yr_r)):
                    gs, gr = gates_ps[br]
                    bp = bank()
                    mm_group(bp, banks[bp][:, 0:N], [(w3[:, k0 + k, :], ysrc[:, k, 0:N]) for k in range(nk)], reads=[wr] + yres)
                    if first_term:
                        vtt(macc[:, 0:N], gs[:, 0:N], banks[bp][:, 0:N], ALU.mult, [gr, BK(bp), "macc"], ["macc"])
                        first_term = False
                    elif br == 2:
                        vtt(mtmp[:, 0:N], gs[:, 0:N], banks[bp][:, 0:N], ALU.mult, [gr, BK(bp), "mtmp"], ["mtmp"])
                        vtt(macc[:, 0:N], macc[:, 0:N], mtmp[:, 0:N], ALU.add, ["macc", "mtmp"], ["macc"])
                    else:
                        vtt(mtmp[:, 0:N], gs[:, 0:N], banks[bp][:, 0:N], ALU.mult, [gr, BK(bp), "mtmp"], ["mtmp"])
                        vtt(merged[:, f, 0:N], macc[:, 0:N], mtmp[:, 0:N], ALU.add, ["macc", "mtmp"], [("merged", f)])
                w_done()

            mr = [("merged", f) for f in range(8)]
            for ob in range(2):
                wt, wr = w_get("wout%d" % ob, l)
                w3 = wt[:, 0:4096].rearrange("p (k f) -> p k f", k=8)
                for fi in range(4):
                    f = ob * 4 + fi
                    bi = bank()
                    mm_group(bi, banks[bi][:, 0:N], [(w3[:, k, fi * 128:(fi + 1) * 128], merged[:, k, 0:N]) for k in range(8)],
                             reads=[wr] + mr)
                    vtt(hT[:, f, 0:N], hT[:, f, 0:N], banks[bi][:, 0:N], ALU.add, [("hT", f), BK(bi)], [("hT", f)])
                w_done()

            barrier(KEYS_A)
            rmsnorm(lambda c: hT[:, c, 0:N], lambda c: ("hT", c), N, "g_ffn", l,
                    lambda c: xn[:, c, 0:N], lambda c: ("xn", c))
            if samp:
                ro_fc = RowOut(32, 24, lambda grp, w: [(s_fconv[l][:, t, grp * 512:grp * 512 + w], t * 16, 16) for t in range(2)])
            elif last:
                ro_fc = RowOut(2, 24, lambda grp, w: [(p_fconv[l][:, grp * 512:grp * 512 + w], 0, 2)])
            else:
                ro_fc = None
            fw0 = pcol[("ffn_conv_w", l)]
            for ub in range(12):
                wt, wr = w_get("wup%d" % ub, l)
                w4 = wt[:, 0:4096].rearrange("p (g k f) -> p g k f", g=2, k=8)
                for jj in range(2):
                    j = ub * 2 + jj
                    q = j % NFB
                    ex, gc, ge = f_ext[q], f_gc[q], f_ge[q]
                    R = lambda s, q=q: (s, q)
                    bg = bank()
                    mm_group(bg, banks[bg][:, 0:N], [(w4[:, 0, k, jj * 128:(jj + 1) * 128], xn[:, k, 0:N]) for k in range(8)],
                             reads=[wr] + xnr)
                    bv = bank()
                    mm_group(bv, banks[bv][:, 0:N], [(w4[:, 1, k, jj * 128:(jj + 1) * 128], xn[:, k, 0:N]) for k in range(8)],
                             reads=[wr] + xnr)
                    if samp:
                        pcopy(ex[:, 0:HF], hs_ffn[:, j, :], [("hs_ffn", j), R("fext")], [R("fext")])
                    else:
                        pcopy(ex[:, 0:HF], h_ffn[:, l, j, :], ["h_ffn%d" % l, R("fext")], [R("fext")])
                    acopy(ex[:, HF:HF + N], banks[bg][:, 0:N], [BK(bg), R("fext")], [R("fext")])
                    act(gc[:, 0:N], banks[bg][:, 0:N], AF.Identity, [BK(bg), "pv", R("fgc")], [R("fgc")],
                        bias=PVc("ffn_conv_b", l, j), scale=pv[:, fw0 + 2 * 24 + j:fw0 + 2 * 24 + j + 1])
                    for tap in range(0, 2):
                        vstt(gc[:, 0:N], ex[:, tap * inner:tap * inner + N], pv[:, fw0 + tap * 24 + j:fw0 + tap * 24 + j + 1],
                             gc[:, 0:N], ALU.mult, ALU.add, [R("fext"), "pv", R("fgc")], [R("fgc")])
                    act(ge[:, 0:N], gc[:, 0:N], AF.Gelu_apprx_tanh, [R("fgc"), R("fge")], [R("fge")])
                    vtt(actb[:, j, 0:N], ge[:, 0:N], banks[bv][:, 0:N], ALU.mult, [R("fge"), BK(bv)], [("actb", j)])
                    if ro_fc is not None:
                        ro_fc.add(j, ex[:, HF + N - 2 * inner:HF + N], [R("fext")])
                    if not samp and not last:
                        pcopy(h_ffn[:, l, j, :], ex[:, HF + N - 2:HF + N], [R("fext")], ["h_ffn%d" % l])
                w_done()
            ar = [("actb", j) for j in range(24)]
            for f in range(8):
                wt, wr = w_get("wdn%d" % f, l)
                w3 = wt[:, 0:3072].rearrange("p (k c) -> p k c", c=128)
                bi = bank()
                mm_group(bi, banks[bi][:, 0:N], [(w3[:, k, :], actb[:, k, 0:N]) for k in range(24)], reads=[wr] + ar)
                vtt(hT[:, f, 0:N], hT[:, f, 0:N], banks[bi][:, 0:N], ALU.add, [("hT", f), BK(bi)], [("hT", f)])
                w_done()

        if stage < 3:
            P.finalize()
            return nc
        import os as _os
        _sel = _os.environ.get('TILESEL')
        _tl = [tiles[int(x)] for x in _sel.split(',')] if _sel else (tiles[:ntiles] + (tiles[-1:] if ntiles < 0 else []))
        for (kind, ti) in _tl:
            samp = kind == "s"
            N = 128 if samp else NT
            if samp:
                for t in range(ST_):
                    P.dma("sp", xr[t * 16:(t + 1) * 16, 0, :], xs[:, t, :], reads=["xr"], writes=[("xrs", t)])
                P.op("dve", lambda e: e.memset(ltmp[:, :], 0.0), reads=[("xrs", t) for t in range(ST_)] + ["xr"], writes=["xr", "ltmp"])
            for a in range(N // 128):
                q = a % 2
                if samp:
                    xres = "xr"
                else:
                    xres = ("xrg", q)
                    r0 = ti * NT + a * 128
                    P.dma("sp", xr[:, q, :], xp[r0:r0 + 128, :], reads=["xr"], writes=[xres])
                for half in range(2):
                    bi = bank()
                    transposes(bi, [(banks[bi][:, k * 128:(k + 1) * 128], xr[:, q, (half * 4 + k) * 128:(half * 4 + k + 1) * 128], ident)
                                    for k in range(4)], reads=[xres, "cst"])
                    anycopy(hT[:, half * 4:(half + 1) * 4, a * 128:(a + 1) * 128],
                            banks[bi][:, :].rearrange("p (k t) -> p k t", k=4), [BK(bi)],
                            [("hT", c) for c in range(half * 4, half * 4 + 4)])
            P.op("dve", lambda e: e.memset(ltmp[:, :], 0.0), reads=[("hT", c) for c in range(8)],
                 writes=["xr", ("xrg", 0), ("xrg", 1), "ltmp"])
            pool_ok["v"] = not (kind == "p" and ti == 0)
            for l in range(2):
                tile_layer(kind, ti, l)
            rmsnorm(lambda c: hT[:, c, 0:N], lambda c: ("hT", c), N, "g_final", 0,
                    lambda c: hT[:, c, 0:N], lambda c: ("hT", c))
            for a in range(N // 128):
                for half in range(2):
                    bi = bank()
                    transposes(bi, [(banks[bi][:, k * 128:(k + 1) * 128], hT[:, half * 4 + k, a * 128:(a + 1) * 128], ident)
                                    for k in range(4)], reads=[("hT", half * 4 + k) for k in range(4)] + ["cst"])
                    anycopy(yrow[:, half * 512:(half + 1) * 512], banks[bi][:, :], [BK(bi)], [("yrow", half)])
                if samp:
                    for t in range(ST_):
                        P.dma("sp", y_s[:, t, :], yrow[t * 16:(t + 1) * 16, :], reads=[("yrow", 0), ("yrow", 1)])
                else:
                    r0 = ti * NT + a * 128
                    P.dma("sp", y_p[r0:r0 + 128, :], yrow[:, :], reads=[("yrow", 0), ("yrow", 1)])
        assert stage < 99 or ntiles < 99 or wst["pos"] == len(seq), (wst["pos"], len(seq))
        print('OPCOUNTS', P.cnt, P.dcnt, flush=True)
        P.finalize()
    return nc


_CACHE = {}


def _consts():
    c = np.zeros((128, 160), np.float32)
    c[:, 0:128] = np.eye(128, dtype=np.float32)
    c[:, 128:143] = (1.0 / np.arange(1, 16, dtype=np.float32))[None, :]
    c[:, 143] = 1.0
    c[:, 144] = EPS
    return c


def kernel(x_prompt, x_sample, mem_prompt, cache_mem_k, cache_mem_v, state_pool, state_rnn_conv,
           state_rnn_h, state_ffn_conv, g_mix, w_in, w_gate, b_gate, pool_w, pool_scale,
           rnn_conv_w, rnn_conv_b, rnn_wa, rnn_ba, rnn_wx, rnn_bx, rnn_lambda, g_mem, w_k, w_v,
           w_br_pool, w_br_rnn, w_br_attn, w_out, g_ffn, w_up, ffn_conv_w, ffn_conv_b, w_down, g_final):
    f32 = lambda a: np.ascontiguousarray(np.asarray(a, dtype=np.float32))
    if "nc" not in _CACHE:
        _CACHE["nc"] = build_program()
    nc = _CACHE["nc"]
    shared = dict(consts=_consts(), g_mix=f32(g_mix), w_in=f32(w_in), w_gate=f32(w_gate), b_gate=f32(b_gate),
                  pool_w=f32(pool_w), pool_scale=f32(pool_scale), rnn_conv_w=f32(rnn_conv_w), rnn_conv_b=f32(rnn_conv_b),
                  rnn_wa=f32(rnn_wa), rnn_ba=f32(rnn_ba), rnn_wx=f32(rnn_wx), rnn_bx=f32(rnn_bx), rnn_lambda=f32(rnn_lambda),
                  g_mem=f32(g_mem), w_k=f32(w_k), w_v=f32(w_v), w_br_pool=f32(w_br_pool), w_br_rnn=f32(w_br_rnn),
                  w_br_attn=f32(w_br_attn), w_out=f32(w_out), g_ffn=f32(g_ffn), w_up=f32(w_up), ffn_conv_w=f32(ffn_conv_w),
                  ffn_conv_b=f32(ffn_conv_b), w_down=f32(w_down), g_final=f32(g_final))
    xpr = f32(x_prompt); xsa = f32(x_sample); mp = f32(mem_prompt)
    ckk = f32(cache_mem_k).reshape(2, 128, NMEM, 512); cvv = f32(cache_mem_v).reshape(2, 128, NMEM, 512)
    sp_ = f32(state_pool); src_ = f32(state_rnn_conv); srh_ = f32(state_rnn_h); sfc_ = f32(state_ffn_conv)
    in_maps = []
    for c in range(NCORES):
        b0, b1 = c * SB_, (c + 1) * SB_
        m = dict(shared)
        m.update(xp=xpr[c], xs=np.ascontiguousarray(xsa[b0:b1]), memp=mp[c],
                 ck=np.ascontiguousarray(ckk[:, b0:b1]), cv=np.ascontiguousarray(cvv[:, b0:b1]),
                 st_pool=np.ascontiguousarray(sp_[:, b0:b1]), st_rconv=np.ascontiguousarray(src_[:, b0:b1]),
                 st_rh=np.ascontiguousarray(srh_[:, b0:b1]), st_fconv=np.ascontiguousarray(sfc_[:, b0:b1]))
        in_maps.append(m)
    res = run_bass_kernel_spmd(nc, in_maps, core_ids=list(range(NCORES)))
    R = res.results
    g = lambda k: [np.asarray(R[c][k], dtype=np.float32) for c in range(NCORES)]
    y_prompt = np.stack(g("y_p"), 0)
    y_sample = np.concatenate(g("y_s"), 0)
    p_pool = np.stack(g("p_pool"), 1)
    p_rconv = np.stack(g("p_rconv"), 1)
    p_rh = np.stack([a.reshape(2, D) for a in g("p_rh")], 1)
    p_fconv = np.stack(g("p_fconv"), 1)
    p_mk = np.stack(g("p_mk"), 1).reshape(2, NCORES, NMEM, 4, 128)
    p_mv = np.stack(g("p_mv"), 1).reshape(2, NCORES, NMEM, 4, 128)
    s_pool = np.concatenate(g("s_pool"), 1)
    s_rconv = np.concatenate(g("s_rconv"), 1)
    s_rh = np.concatenate(g("s_rh"), 1)
    s_fconv = np.concatenate(g("s_fconv"), 1)
    return (y_prompt, y_sample, p_pool, p_rconv, p_rh, p_fconv, p_mk, p_mv, s_pool, s_rconv, s_rh, s_fconv)
```
